# Optimizing a Trainium2 kernel written in Bass

```python
import math
import jax, jax.numpy as jnp
from jax import lax
import numpy as np

D_MODEL = 1024
BATCH = 4
SEQ = 4096
DEPTH = 2
DEC_BATCH = 32
DEC_SEQ = 8
PAST_LEN = 8192
PAGE_SIZE = 128

HEAD_DIM = 64
NSA_HEADS = 8
MOBA_HEADS = 8
KV_HEADS = 2
NSA_WIDTH = NSA_HEADS * HEAD_DIM
MOBA_WIDTH = MOBA_HEADS * HEAD_DIM
KV_WIDTH = KV_HEADS * HEAD_DIM
MIX_WIDTH = NSA_WIDTH + MOBA_WIDTH
ROT_DIM = HEAD_DIM // 4
ROPE_THETA = 500000.0
CMP_LEN = 32
CMP_STRIDE = 16
CMP_HIDDEN = 256
SEL_BLOCK = 64
SEL_TOPK = 8
WINDOW = 512
MOBA_BLOCK = 256
MOBA_TOPK = 3
PLE_DIM = 256
N_PAGED_SLOTS = 6
NSA_QBLK = 128
MOBA_QBLK = 16
RMS_EPS = 1e-6
NEG_INF = -1e30
FORCE_SCORE = 1e4
IN_SIZES = (NSA_WIDTH, 6 * KV_WIDTH, 3 * NSA_HEADS, NSA_WIDTH, MOBA_WIDTH, 2 * KV_WIDTH, MOBA_WIDTH)
IN_COLS = 2 * NSA_WIDTH + 8 * KV_WIDTH + 3 * NSA_HEADS + 2 * MOBA_WIDTH

kernel_name = 'hybrid_nsa_moba_decoder_step'


def rms_norm(x, g):
    xf = x.astype(jnp.float32)
    y = xf * lax.rsqrt(jnp.mean(xf * xf, axis=-1, keepdims=True) + RMS_EPS)
    return (y * g.astype(jnp.float32)).astype(x.dtype)


def rope_partial(x, pos):
    half = ROT_DIM // 2
    inv_freq = ROPE_THETA ** (-jnp.arange(half, dtype=jnp.float32) / half)
    ang = pos.astype(jnp.float32)[:, None] * inv_freq[None, :]
    cos = jnp.cos(ang)[None, :, None, :]
    sin = jnp.sin(ang)[None, :, None, :]
    xr = x[..., :ROT_DIM].astype(jnp.float32)
    x1, x2 = xr[..., :half], xr[..., half:]
    rot = jnp.concatenate([x1 * cos - x2 * sin, x2 * cos + x1 * sin], axis=-1)
    return jnp.concatenate([rot.astype(x.dtype), x[..., ROT_DIM:]], axis=-1)


def masked_softmax(s, mask):
    s = jnp.where(mask, s.astype(jnp.float32), NEG_INF)
    return jnp.where(mask, jax.nn.softmax(s, axis=-1), 0.0)


def softmax_last2(s, mask):
    shp = s.shape
    flat = shp[:-2] + (shp[-2] * shp[-1],)
    m = jnp.broadcast_to(mask, shp).reshape(flat)
    return masked_softmax(s.reshape(flat), m).reshape(shp)


def to_groups(a):
    B, T, H, E = a.shape
    return a.reshape(B, T, KV_HEADS, H // KV_HEADS, E).transpose(0, 2, 1, 3, 4)


def from_groups(a):
    B, K, T, G, E = a.shape
    return a.transpose(0, 2, 1, 3, 4).reshape(B, T, K * G, E)


def to_blocks(k, size):
    B, L, K, E = k.shape
    return k.reshape(B, L // size, size, K, E).transpose(0, 3, 1, 2, 4)


def gather_blocks(blocks, idx):
    return jax.vmap(jax.vmap(lambda a, i: a[i]))(blocks, idx)


def compress(k, pos_emb, w1, w2):
    B, L, K, E = k.shape
    chunks = k.reshape(B, L // CMP_STRIDE, CMP_STRIDE, K, E)
    blocks = jnp.concatenate([chunks[:, :-1], chunks[:, 1:]], axis=2) + pos_emb[None, None, :, None, :]
    hid = jax.nn.silu(jnp.einsum('bclkd,ldf->bckf', blocks, w1))
    return jnp.einsum('bckf,fd->bckd', hid, w2)


def selection_map(n_cmp, n_sel):
    c0 = np.arange(n_cmp)[:, None] * CMP_STRIDE
    j0 = np.arange(n_sel)[None, :] * SEL_BLOCK
    ov = np.clip(np.minimum(c0 + CMP_LEN, j0 + SEL_BLOCK) - np.maximum(c0, j0), 0, None)
    return jnp.asarray(ov / CMP_STRIDE, dtype=jnp.float32)


def over_query_blocks(fn, qblk, qpos, *q_arrays):
    T = qpos.shape[0]
    qblk = math.gcd(qblk, T)

    def body(i):
        s = i * qblk
        return fn(lax.dynamic_slice_in_dim(qpos, s, qblk),
                  *[lax.dynamic_slice_in_dim(a, s, qblk, axis=1) for a in q_arrays])

    out = jnp.moveaxis(lax.map(body, jnp.arange(T // qblk)), 0, 1)
    return out.reshape((out.shape[0], T) + out.shape[3:])


def nsa_block(qpos, q, q_rot, gate, kc, vc, ks_blk, vs_blk, win_k, win_v, win_start, sel_map):
    Tq = qpos.shape[0]
    scale = HEAD_DIM ** -0.5
    qg, qr = to_groups(q), to_groups(q_rot)
    c_end = jnp.arange(kc.shape[1]) * CMP_STRIDE + CMP_LEN
    c_mask = (c_end[None, :] <= qpos[:, None] + 1)[None, None, :, None, :]
    p_cmp = masked_softmax(jnp.einsum('bktgd,bckd->bktgc', qg, kc) * scale, c_mask)
    o_cmp = jnp.einsum('bktgc,bckd->bktgd', p_cmp.astype(vc.dtype), vc)
    n_sel = ks_blk.shape[2]
    imp = jnp.einsum('bktc,cj->bktj', p_cmp.sum(axis=3), sel_map)
    blk = jnp.arange(n_sel)[None, :]
    jt = (qpos // SEL_BLOCK)[:, None]
    score = jnp.where((blk == jt) | (blk == 0), FORCE_SCORE, jnp.where(blk < jt, imp, NEG_INF))
    top_s, idx = lax.top_k(score, min(SEL_TOPK, n_sel))
    kpos = idx[..., None] * SEL_BLOCK + jnp.arange(SEL_BLOCK)
    s_mask = (top_s > 0.5 * NEG_INF)[..., None] & (kpos <= qpos[None, None, :, None, None])
    kg, vg = gather_blocks(ks_blk, idx), gather_blocks(vs_blk, idx)
    p_sel = softmax_last2(jnp.einsum('bktgd,bktnsd->bktgns', qr, kg) * scale, s_mask[:, :, :, None])
    o_sel = jnp.einsum('bktgns,bktnsd->bktgd', p_sel.astype(vg.dtype), vg)
    q0 = qpos[0]
    span = WINDOW + Tq
    start = q0 - WINDOW - win_start
    wk = lax.dynamic_slice_in_dim(win_k, start, span, axis=1)
    wv = lax.dynamic_slice_in_dim(win_v, start, span, axis=1)
    wpos = (q0 - WINDOW + jnp.arange(span))[None, :]
    w_mask = (wpos >= 0) & (wpos <= qpos[:, None]) & (wpos >= qpos[:, None] - WINDOW)
    p_win = masked_softmax(jnp.einsum('bktgd,bskd->bktgs', qr, wk) * scale, w_mask[None, None, :, None, :])
    o_win = jnp.einsum('bktgs,bskd->bktgd', p_win.astype(wv.dtype), wv)
    g = to_groups(gate)
    return from_groups(g[..., 0:1] * o_cmp + g[..., 1:2] * o_sel + g[..., 2:3] * o_win)


def moba_block(qpos, q, kb_blk, vb_blk, k_mean):
    scale = HEAD_DIM ** -0.5
    qg = to_groups(q)
    nb = kb_blk.shape[2]
    bt = qpos // MOBA_BLOCK
    past = (jnp.arange(nb)[None, :] < bt[:, None])[None, None, :, None, :]
    gate_s = jnp.where(past, jnp.einsum('bktgd,bknd->bktgn', qg.astype(jnp.float32), k_mean), NEG_INF)
    top_s, top_i = lax.top_k(gate_s, min(MOBA_TOPK, nb))
    own = jnp.broadcast_to(bt[None, None, :, None, None], top_i.shape[:-1] + (1,)).astype(top_i.dtype)
    idx = jnp.concatenate([top_i, own], axis=-1)
    valid = jnp.concatenate([top_s > 0.5 * NEG_INF, jnp.ones(own.shape, dtype=bool)], axis=-1)
    kpos = idx[..., None] * MOBA_BLOCK + jnp.arange(MOBA_BLOCK)
    mask = valid[..., None] & (kpos <= qpos[None, None, :, None, None, None])
    kg, vg = gather_blocks(kb_blk, idx), gather_blocks(vb_blk, idx)
    p = softmax_last2(jnp.einsum('bktgd,bktgnsd->bktgns', qg, kg) * scale, mask)
    return from_groups(jnp.einsum('bktgns,bktgnsd->bktgd', p.astype(vg.dtype), vg))


def mixer_layer(x, ple, pos, past_kv, win_prev, win_start, g_mix, w_in, w_out,
                cmp_pos_k, cmp_w1_k, cmp_w2_k, cmp_pos_v, cmp_w1_v, cmp_w2_v,
                g_ple, w_ple_gate, w_ple_proj):
    B, T, _ = x.shape
    h = rms_norm(x, g_mix)
    q_a, kv_a, gate_a, z_a, q_b, kv_b, z_b = jnp.split(h @ w_in, np.cumsum(IN_SIZES)[:-1].tolist(), axis=-1)
    q_a = q_a.reshape(B, T, NSA_HEADS, HEAD_DIM)
    q_b = rope_partial(q_b.reshape(B, T, MOBA_HEADS, HEAD_DIM), pos)
    kv_a = kv_a.reshape(B, T, 6, KV_HEADS, HEAD_DIM)
    kv_b = kv_b.reshape(B, T, 2, KV_HEADS, HEAD_DIM)
    gate_a = jax.nn.sigmoid(gate_a.astype(jnp.float32)).astype(x.dtype).reshape(B, T, NSA_HEADS, 3)
    new_paged = jnp.stack([kv_a[:, :, 0], kv_a[:, :, 1], rope_partial(kv_a[:, :, 2], pos), kv_a[:, :, 3],
                           rope_partial(kv_b[:, :, 0], pos), kv_b[:, :, 1]], axis=2)
    new_win = jnp.stack([rope_partial(kv_a[:, :, 4], pos), kv_a[:, :, 5]], axis=2)
    full = new_paged if past_kv is None else jnp.concatenate([past_kv, new_paged], axis=1)
    L = full.shape[1]
    Lp = -(-L // MOBA_BLOCK) * MOBA_BLOCK
    full = jnp.pad(full, ((0, 0), (0, Lp - L), (0, 0), (0, 0), (0, 0)))
    win = jnp.concatenate([win_prev, new_win], axis=1)
    win_k, win_v = win[:, :, 0], win[:, :, 1]
    kc = compress(full[:, :, 0], cmp_pos_k, cmp_w1_k, cmp_w2_k)
    vc = compress(full[:, :, 1], cmp_pos_v, cmp_w1_v, cmp_w2_v)
    ks_blk, vs_blk = to_blocks(full[:, :, 2], SEL_BLOCK), to_blocks(full[:, :, 3], SEL_BLOCK)
    kb_blk, vb_blk = to_blocks(full[:, :, 4], MOBA_BLOCK), to_blocks(full[:, :, 5], MOBA_BLOCK)
    k_mean = jnp.mean(kb_blk.astype(jnp.float32), axis=3)
    sel_map = selection_map(kc.shape[1], ks_blk.shape[2])
    o_a = over_query_blocks(
        lambda qp, q, qr, g: nsa_block(qp, q, qr, g, kc, vc, ks_blk, vs_blk, win_k, win_v, win_start, sel_map),
        NSA_QBLK, pos, q_a, rope_partial(q_a, pos), gate_a)
    o_b = over_query_blocks(lambda qp, q: moba_block(qp, q, kb_blk, vb_blk, k_mean), MOBA_QBLK, pos, q_b)
    mixed = jnp.concatenate([o_a.reshape(B, T, NSA_WIDTH) * jax.nn.silu(z_a),
                             o_b.reshape(B, T, MOBA_WIDTH) * jax.nn.silu(z_b)], axis=-1)
    x = x + mixed @ w_out
    x = x + jax.nn.sigmoid(rms_norm(x, g_ple) @ w_ple_gate) * (ple @ w_ple_proj)
    return x, new_paged, new_win


def setup_inputs(seed: int = 0) -> dict:
    key = jax.random.key(seed)
    ks = jax.random.split(key, 20)
    n_pages = PAST_LEN // PAGE_SIZE
    n_used = DEC_BATCH * n_pages
    n_pool = n_used + n_used // 4
    wbuf = min(WINDOW, PAST_LEN)

    def nrm(k, shape, scale=1.0):
        return scale * jax.random.normal(k, shape, jnp.float32)

    page_table = jax.random.permutation(ks[4], n_pool)[:n_used].reshape(DEC_BATCH, n_pages).astype(jnp.int32)
    return {
        'x_prompt': nrm(ks[0], (BATCH, SEQ, D_MODEL)),
        'x_sample': nrm(ks[1], (DEC_BATCH, DEC_SEQ, D_MODEL)),
        'cache_paged_kv': nrm(ks[2], (DEPTH, n_pool, PAGE_SIZE, N_PAGED_SLOTS, KV_HEADS, HEAD_DIM)),
        'cache_win_kv': nrm(ks[3], (DEPTH, DEC_BATCH, wbuf, 2, KV_HEADS, HEAD_DIM)),
        'page_table': page_table,
        'p_prompt': nrm(ks[5], (DEPTH, BATCH, SEQ, PLE_DIM)),
        'p_sample': nrm(ks[6], (DEPTH, DEC_BATCH, DEC_SEQ, PLE_DIM)),
        'g_mix': 1.0 + nrm(ks[7], (DEPTH, D_MODEL), 0.02),
        'w_in': nrm(ks[8], (DEPTH, D_MODEL, IN_COLS), D_MODEL ** -0.5),
        'w_out': nrm(ks[9], (DEPTH, MIX_WIDTH, D_MODEL), MIX_WIDTH ** -0.5),
        'cmp_pos_k': nrm(ks[10], (DEPTH, CMP_LEN, HEAD_DIM), 0.02),
        'cmp_w1_k': nrm(ks[11], (DEPTH, CMP_LEN, HEAD_DIM, CMP_HIDDEN), (CMP_LEN * HEAD_DIM) ** -0.5),
        'cmp_w2_k': nrm(ks[12], (DEPTH, CMP_HIDDEN, HEAD_DIM), CMP_HIDDEN ** -0.5),
        'cmp_pos_v': nrm(ks[13], (DEPTH, CMP_LEN, HEAD_DIM), 0.02),
        'cmp_w1_v': nrm(ks[14], (DEPTH, CMP_LEN, HEAD_DIM, CMP_HIDDEN), (CMP_LEN * HEAD_DIM) ** -0.5),
        'cmp_w2_v': nrm(ks[15], (DEPTH, CMP_HIDDEN, HEAD_DIM), CMP_HIDDEN ** -0.5),
        'g_ple': 1.0 + nrm(ks[16], (DEPTH, D_MODEL), 0.02),
        'w_ple_gate': nrm(ks[17], (DEPTH, D_MODEL, D_MODEL), D_MODEL ** -0.5),
        'w_ple_proj': nrm(ks[18], (DEPTH, PLE_DIM, D_MODEL), PLE_DIM ** -0.5),
        'g_final': 1.0 + nrm(ks[19], (D_MODEL,), 0.02),
    }


def reference(x_prompt, x_sample, cache_paged_kv, cache_win_kv, page_table, p_prompt, p_sample,
              g_mix, w_in, w_out, cmp_pos_k, cmp_w1_k, cmp_w2_k, cmp_pos_v, cmp_w1_v, cmp_w2_v,
              g_ple, w_ple_gate, w_ple_proj, g_final):
    bp, tp = x_prompt.shape[0], x_prompt.shape[1]
    bs, ts = x_sample.shape[0], x_sample.shape[1]
    past_len = page_table.shape[1] * PAGE_SIZE
    wbuf = cache_win_kv.shape[2]
    pos_p = jnp.arange(tp, dtype=jnp.int32)
    pos_s = past_len + jnp.arange(ts, dtype=jnp.int32)
    xp, xs = x_prompt, x_sample
    new_pp, new_pw, new_sp, new_sw = [], [], [], []
    for i in range(DEPTH):
        w = (g_mix[i], w_in[i], w_out[i], cmp_pos_k[i], cmp_w1_k[i], cmp_w2_k[i],
             cmp_pos_v[i], cmp_w1_v[i], cmp_w2_v[i], g_ple[i], w_ple_gate[i], w_ple_proj[i])
        win0 = jnp.zeros((bp, WINDOW, 2, KV_HEADS, HEAD_DIM), xp.dtype)
        xp, pk, pw = mixer_layer(xp, p_prompt[i], pos_p, None, win0, -WINDOW, *w)
        past = cache_paged_kv[i, page_table].reshape(bs, past_len, N_PAGED_SLOTS, KV_HEADS, HEAD_DIM)
        win_prev = jnp.concatenate([jnp.zeros((bs, WINDOW - wbuf, 2, KV_HEADS, HEAD_DIM), xs.dtype),
                                    cache_win_kv[i]], axis=1)
        xs, sk, sw = mixer_layer(xs, p_sample[i], pos_s, past, win_prev, past_len - WINDOW, *w)
        new_pp.append(pk)
        new_pw.append(pw[:, tp - min(WINDOW, tp):])
        new_sp.append(sk)
        new_sw.append(sw)
    y_prompt = rms_norm(xp, g_final)
    y_sample = rms_norm(xs, g_final)
    return (y_prompt, y_sample, jnp.stack(new_pp), jnp.stack(new_pw), jnp.stack(new_sp), jnp.stack(new_sw))
```

```python
import math
from contextlib import ExitStack
import numpy as np
import ml_dtypes
import concourse.bass as bass
import concourse.mybir as mybir
from concourse.bass_utils import run_bass_kernel_spmd

F32, BF16, I32 = mybir.dt.float32, mybir.dt.bfloat16, mybir.dt.int32
AF = mybir.ActivationFunctionType
OP = mybir.AluOpType
AX = mybir.AxisListType

D = 1024
NEG = -30000.0
ENGS = ['pe', 'act', 'dve', 'pool', 'sp']
ETYPE = {'sp': mybir.EngineType.SP, 'act': mybir.EngineType.Activation, 'pool': mybir.EngineType.Pool}
IN_COLS = 3096
C_QA, C_KVA, C_GATE, C_ZA, C_QB, C_KVB, C_ZB = 0, 512, 1280, 1304, 1816, 2328, 2584


class Sched:
    def __init__(self, nc):
        self.nc = nc
        self.ops = []
        self.last_w = {}
        self.readers = {}
        self.by_eng = {e: [] for e in ENGS}
        self.nchan = {'sp': 8, 'pool': 4, 'act': 2}
        self.chan_next = {q: 0 for q in self.nchan}
        self.chan_last = {}
        self.chan_cnt = {}

    def add(self, eng, emit, r=(), w=(), dma=False):
        import os
        if len(self.ops) >= int(os.environ.get('KLIMIT', '100000000')):
            return -1
        idx = len(self.ops)
        deps = set()
        for x in r:
            if x in self.last_w:
                deps.add(self.last_w[x])
        for x in w:
            if x in self.last_w:
                deps.add(self.last_w[x])
            deps.update(self.readers.get(x, ()))
        op = {'idx': idx, 'eng': eng, 'emit': emit, 'deps': deps, 'dma': dma, 'signal': False}
        if dma:
            c = self.chan_next[eng]
            self.chan_next[eng] = (c + 1) % self.nchan[eng]
            key = ('dma', eng, c)
            if key in self.chan_last:
                deps.add(self.chan_last[key])
            self.chan_last[key] = idx
            self.chan_cnt[key] = self.chan_cnt.get(key, 0) + 1
            op['semkey'] = key
            op['val'] = 16 * self.chan_cnt[key]
        for x in w:
            self.last_w[x] = idx
            self.readers[x] = []
        for x in r:
            self.readers.setdefault(x, []).append(idx)
        deps.discard(idx)
        self.ops.append(op)
        self.by_eng[eng].append(op)
        return idx

    def finalize(self, es):
        nc = self.nc
        LIM = 30000
        for op in self.ops:
            for d in op['deps']:
                dop = self.ops[d]
                if dop['dma']:
                    continue
                if dop['eng'] == 'pe' and op['eng'] == 'pe' and not op['dma']:
                    continue
                dop['signal'] = True
        cnt = {e: 0 for e in ENGS}
        semkeys = set()
        for op in self.ops:
            if op['dma']:
                semkeys.add(op['semkey'])
                continue
            if op['signal']:
                e = op['eng']
                cnt[e] += 1
                op['semkey'] = ('eng', e, (cnt[e] - 1) // LIM)
                op['val'] = (cnt[e] - 1) % LIM + 1
                semkeys.add(op['semkey'])
        sems = {}
        for k in sorted(semkeys):
            sems[k] = es.enter_context(nc.semaphore("s_" + "_".join(str(x) for x in k)))
        final = {}
        for op in self.ops:
            if 'semkey' in op:
                final[op['semkey']] = max(final.get(op['semkey'], 0), op['val'])
        ops = self.ops
        by_eng = self.by_eng
        block = es.enter_context(nc.Block())

        def run(eng, e):
            seen = {}
            for op in by_eng[eng]:
                for d in sorted(op['deps']):
                    dop = ops[d]
                    if 'semkey' not in dop:
                        continue
                    if (not dop['dma']) and dop['eng'] == 'pe' and eng == 'pe' and not op['dma']:
                        continue
                    k, v = dop['semkey'], dop['val']
                    if seen.get(k, 0) >= v:
                        continue
                    seen[k] = v
                    e.wait_ge(sems[k], v)
                inst = op['emit'](e)
                if op['dma']:
                    inst.then_inc(sems[op['semkey']], 16)
                elif op['signal']:
                    inst.then_inc(sems[op['semkey']], 1)
            if eng == 'sp':
                for k in sorted(final):
                    if k[0] == 'dma':
                        e.wait_ge(sems[k], final[k])

        @block.tensor
        def _(e):
            run('pe', e)

        @block.scalar
        def _(e):
            run('act', e)

        @block.vector
        def _(e):
            run('dve', e)

        @block.gpsimd
        def _(e):
            run('pool', e)

        @block.sync
        def _(e):
            run('sp', e)


def build(T, NPG, NPOOL, SS=4, do_sample=True):
    NT = T // 128
    NJ = T // 64
    NB = max(T // 256, 1)
    NBP = max(NB, 8)
    NCT = (T // 16 - 1 + 127) // 128
    NCP = NCT * 128
    PAST = NPG * 128
    WT = min(4, NT)
    nc = bass.Bass("TRN2", target_bir_lowering=False)
    es = ExitStack()

    def din(name, shape, dt=F32):
        return nc.dram_tensor(name, list(shape), dt, kind="ExternalInput").ap()

    def dout(name, shape, dt=F32):
        return nc.dram_tensor(name, list(shape), dt, kind="ExternalOutput").ap()

    xp = din("xp", [T, D]); pp = din("pp", [2, T, 256])
    xs = din("xs", [SS * 8, D]); ps_ = din("ps", [2, SS * 8, 256])
    cache = din("cache", [2, NPOOL, 128, 768]); wk = din("wk", [2, SS, 512, 256])
    pt = din("pt", [1, SS * NPG], I32)
    g_mix = din("g_mix", [2, D]); w_in = din("w_in", [2, D, IN_COLS]); w_out = din("w_out", [2, D, D])
    cpos = [din("cmp_pos_k", [2, 32, 64]), din("cmp_pos_v", [2, 32, 64])]
    cw1 = [din("cmp_w1_k", [2, 32, 64, 256]), din("cmp_w1_v", [2, 32, 64, 256])]
    cw2 = [din("cmp_w2_k", [2, 256, 64]), din("cmp_w2_v", [2, 256, 64])]
    g_ple = din("g_ple", [2, D]); w_pg = din("w_ple_gate", [2, D, D]); w_pp = din("w_ple_proj", [2, 256, D])
    g_fin = din("g_final", [D])
    c_rope_p = din("c_rope_p", [T, 64]); c_rope_s = din("c_rope_s", [128, 64])
    c_id16 = din("c_id16", [128, 512], BF16); c_idf = din("c_idf", [128, 128])
    c_tri = din("c_tri", [128, 2, 512], BF16)
    c_selmap = din("c_selmap", [NCP, NJ]); c_iota = din("c_iota", [128, 256])
    NCS = PAST // 16 - 1
    NCTS = (NCS + 127) // 128
    NJS = max(PAST // 64, 8)
    JTS = PAST // 64
    BTS = PAST // 256
    NBS = max(BTS, 8)
    GP = min(16, NPG)
    NG = NPG // GP
    c_selmap_s = din("c_selmap_s", [NCTS * 128, NJS])
    c_smask = din("c_smask", [128, SS, 32], BF16); c_wmask = din("c_wmask", [128, 32], BF16)
    c_id8 = din("c_id8", [128, 32], BF16)
    xsmid = nc.dram_tensor("xsmid", [SS * 8, D], F32, kind="Internal").ap()
    cache3 = cache.rearrange("l n p (c s) -> (l n p c) s", c=3)
    y_p = dout("y_p", [T, D]); y_s = dout("y_s", [SS * 8, D])
    npp = dout("npp", [2, T, 768]); npw = dout("npw", [2, WT * 128, 256])
    nsp = dout("nsp", [2, SS * 8, 768]); nsw = dout("nsw", [2, SS * 8, 256])
    xmid = nc.dram_tensor("xmid", [T, D], F32, kind="Internal").ap()

    def sb(name, shape, dt=F32):
        return es.enter_context(nc.sbuf_tensor(name, list(shape), dt))

    def pst(name, shape, dt=F32):
        return es.enter_context(nc.psum_tensor(name, list(shape), dt))

    NBM = max(NBP, NBS) + 1
    Wi = sb("Wi", [128, 8, IN_COLS], BF16); Wo = sb("Wo", [128, 8, D], BF16)
    Wg = sb("Wg", [128, 8, D], BF16); Wp = sb("Wp", [128, 2, D], BF16)
    W1 = sb("W1", [128, 32, 256], BF16)
    W2k = sb("W2k", [128, 2, 2, 128], BF16)
    W2v = sb("W2v", [128, 2, 64], BF16)
    w2stage = sb("w2stage", [128, 2, 2, 64], BF16)
    posT = sb("posT", [128, 32], BF16)
    b1 = sb("b1", [128, 2, 2])
    gcol = sb("gcol", [128, 3, 8])
    gfin_bc = sb("gfin_bc", [128, D])
    id16 = sb("id16", [128, 512], BF16); idf = sb("idf", [128, 128])
    tri = sb("tri", [128, 2, 512], BF16)
    selmap = sb("selmap", [128, NCT, NJ], BF16)
    iota = sb("iota", [128, 256])
    zer16 = sb("zer16", [128, 512], BF16)
    ksum = sb("ksum", [128, NT + 1])
    kmT = sb("kmT", [128, NBP], BF16)
    x = sb("x", [128, D])
    st4 = sb("st4", [128, 8])
    h16 = sb("h16", [128, D], BF16); hT = sb("hT", [128, 8, 128], BF16)
    junk = h16; mix16 = h16; mixT = hT
    proj = sb("proj", [128, 2072])
    sg = proj[:, 0:1024]
    oa = proj[:, 1024:2048].rearrange("p (h d) -> p h d", d=64)
    sz = sb("sz", [128, D], BF16)
    gate = sb("gate", [128, 24])
    rp = sb("rp", [128, 64])
    rt = sb("rt", [128, 8, 16]); ru = sb("ru", [128, 8, 16])
    okv = sb("okv", [128, 768]); ow = sb("ow", [128, 256])
    oT = okv[0:65, 0:512]
    kb16 = sb("kb16", [128, 5, 128], BF16)
    q16 = sb("q16", [128, 3, 4, 2, 64], BF16)
    QT = sb("QT", [128, 3, 4, 128], BF16)
    pT = [sb("pT%d" % i, [128, 512], BF16) for i in range(3)]
    rr = sb("rr", [128, 4]); cf = sb("cf", [128, 4])
    otmp = sb("otmp", [128, 4, 64])
    imps = sb("imps", [128, 128]); elig = sb("elig", [128, 128]); t8 = sb("t8", [128, 8]); thr = sb("thr", [128, 1])
    Mb = sb("Mb", [128, 128], BF16)
    gsc = sb("gsc", [128, 4, NBM]); Mbm = sb("Mbm", [128, 4, NBM], BF16); t8m = sb("t8m", [128, 4, 8]); thrm = sb("thrm", [128, 4])
    ple = sb("ple", [128, 256]); ple16 = sb("ple16", [128, 256], BF16); pleT = sb("pleT", [128, 2, 128], BF16)
    hk16 = sb("hk16", [128, 2, 2, 128 if do_sample else 8], BF16)
    if do_sample:
        iop = sb("iop", [128, 1], I32); iof = sb("iof", [128, 1])
        ksums = sb("ksums", [128, NPG]); kmTs = sb("kmTs", [128, NBS], BF16)
        gs = sb("gs", [8, SS, 24])
    def _need(shape, dt):
        n = 1
        for v in shape[1:]:
            n *= v
        n *= 2 if dt in (F32, I32) else 1
        return (n + 15) // 16 * 16
    p_items = [('KT', [128, 2, T], BF16), ('KTw', [128, 8 * 128], BF16), ('VS', [128, NT, 2, 2, 65], BF16),
               ('VSw', [128, 8, 2, 65], BF16), ('ckT', [128, 2, 144], BF16), ('kcT', [128, NCP], BF16),
               ('hidv', [128, 2, 2, NCP], BF16), ('vcs', [128, NCT, 2, 65], BF16), ('pcT', [128, max(NCT, 1), 512], BF16),
               ('cb', [128, 2, 512], BF16)]
    s_items = []
    if do_sample:
        s_items = [('pg1_0', [128, 2, 256], F32), ('pg1_1', [128, 2, 256], F32), ('pg2_0', [128, 256], F32), ('pg2_1', [128, 256], F32),
                   ('c16_0', [128, 3, 128], BF16), ('c16_1', [128, 3, 128], BF16), ('ckTs', [128, 2, 16 + GP * 128], BF16),
                   ('kcTs', [128, NCTS * 128], BF16), ('hidvs', [128, 2, 2, NCTS * 128], BF16), ('vcss', [128, NCTS, 2, 65], BF16),
                   ('pcTs', [128, NCTS, 2, 32], BF16), ('ktp0', [128, 128], BF16), ('ktp1', [128, 128], BF16),
                   ('vp0', [128, 2, 65], BF16), ('vp1', [128, 2, 65], BF16), ('kp16_0', [128, 128], BF16), ('kp16_1', [128, 128], BF16),
                   ('wkb', [128, 4, 256], F32), ('ptb', [128, SS * NPG], I32), ('ptf', [128, SS * NPG], F32),
                   ('idxv', [128, 3, SS * NPG], I32), ('selmap_s', [128, NCTS, NJS], BF16), ('smask', [128, SS, 32], BF16),
                   ('wmask', [128, 32], BF16), ('id8', [128, 32], BF16), ('Mbs', [128, 2, 128], BF16), ('Mbms', [128, 2, 4, NBM], BF16),
                   ('nkT', [128, 3, 128], BF16), ('vnew', [128, 3, 2, 65], BF16), ('QTs', [128, 3, SS, 4, 8], BF16),
                   ('oas', [8, 16, 64], F32)]
    asz = max(sum(_need(sh, dt) for _, sh, dt in p_items), sum(_need(sh, dt) for _, sh, dt in s_items))
    arena = sb("arena", [128, asz], BF16)
    AV = {}
    for items in (p_items, s_items):
        off = 0
        for nm, sh, dt in items:
            nb_ = _need(sh, dt)
            n = 1
            for v in sh[1:]:
                n *= v
            v_ = arena[0:sh[0], off:off + (n * 2 if dt in (F32, I32) else n)]
            if dt in (F32, I32):
                v_ = v_.bitcast(dt)
            if len(sh) > 2:
                names = 'abcd'[:len(sh) - 1]
                v_ = v_.rearrange("p (%s) -> p %s" % (' '.join(names), ' '.join(names)), **{names[i]: sh[i + 1] for i in range(len(names))})
            AV[nm] = v_
            off += nb_
    KT, KTw, VS, VSw, ckT, kcT, hidv, vcs, pcT, cb = [AV[k] for k in ['KT', 'KTw', 'VS', 'VSw', 'ckT', 'kcT', 'hidv', 'vcs', 'pcT', 'cb']]
    if do_sample:
        pg1 = [AV['pg1_0'], AV['pg1_1']]; pg2 = [AV['pg2_0'], AV['pg2_1']]; c16 = [AV['c16_0'], AV['c16_1']]
        ckTs, kcTs, hidvs, vcss, pcTs, wkb = [AV[k] for k in ['ckTs', 'kcTs', 'hidvs', 'vcss', 'pcTs', 'wkb']]
        ktp = [AV['ktp0'], AV['ktp1']]; vp = [AV['vp0'], AV['vp1']]; kp16 = [AV['kp16_0'], AV['kp16_1']]
        ptb, ptf, idxv, selmap_s, smask, wmask, id8, Mbs, Mbms, nkT, vnew, QTs, oas = [AV[k] for k in [
            'ptb', 'ptf', 'idxv', 'selmap_s', 'smask', 'wmask', 'id8', 'Mbs', 'Mbms', 'nkT', 'vnew', 'QTs', 'oas']]
    P_RES = ([('KT', t) for t in range(NT)] + [('VS', t) for t in range(NT)] + [('KTw', t) for t in range(8)] + [('VSw', t) for t in range(8)]
             + ['ckT', 'kcT', 'hidv', 'vcs', ('cb', 0), ('cb', 1)] + [('pcT', n) for n in range(max(NCT, 1))])
    S_RES = []
    if do_sample:
        S_RES = (['pg1a_0', 'pg1a_1', 'pg1b_0', 'pg1b_1', 'pg2_0', 'pg2_1', 'c16_0', 'c16_1', 'ckTs', 'kcT_s', 'hidv_s', 'vcs_s',
                  'ktp0', 'ktp1', 'vp0', 'vp1', 'kp16_0', 'kp16_1', 'wkb', 'ptb', 'ptf', 'idxv', 'selmap_s', 'smask', 'wmask',
                  'id8', 'Mbs', 'Mbms', 'nkT', 'vnew', 'QTs', 'oas'] + [('pcTs', n, k) for n in range(NCTS) for k in range(2)])
    fence_t = sb("fence_t", [128, 8], BF16)
    psA = [pst("psA0", [128, 512]), pst("psA1", [128, 512])]
    psS = [pst("psS0", [128, 512]), pst("psS1", [128, 512])]
    psO = [pst("psO0", [128, 512]), pst("psO1", [128, 512])]
    psT = pst("psT", [128, 1024], BF16)
    psM = pst("psM", [128, 512])

    S = Sched(nc)
    add = S.add
    cnt = {'A': 0, 'S': 0, 'O': 0, 'P': 0, 'G': 0}
    regcache = {}

    def nxt(k, n):
        v = cnt[k] % n
        cnt[k] += 1
        return v

    def dma(q, out, in_, r, w, **kw):
        add(q, lambda e: e.dma_start(out=out, in_=in_, **kw), r=r, w=w, dma=True)

    def mm(out, lhsT, rhs, start, stop, r, w, skip=False):
        add('pe', lambda e: e.matmul(out, lhsT, rhs, start=start, stop=stop, skip_group_check=skip), r=r, w=w)

    def tp(out, in_, ident, r, w):
        add('pe', lambda e: e.transpose(out, in_, ident), r=r, w=w)

    def actf(out, in_, func, r, w, **kw):
        add('act', lambda e: e.activation(out=out, in_=in_, func=func, **kw), r=r, w=w)

    def cp(eng, out, in_, r, w):
        if eng == 'act':
            add('act', lambda e: e.copy(out=out, in_=in_), r=r, w=w)
        else:
            add(eng, lambda e: e.tensor_copy(out=out, in_=in_), r=r, w=w)

    def tt(eng, out, in0, in1, op, r, w):
        add(eng, lambda e: e.tensor_tensor(out=out, in0=in0, in1=in1, op=op), r=r, w=w)

    def ts(eng, out, in0, s1, s2, op0, op1, r, w):
        if op1 is None:
            add(eng, lambda e: e.tensor_scalar(out=out, in0=in0, scalar1=s1, scalar2=None, op0=op0), r=r, w=w)
        else:
            add(eng, lambda e: e.tensor_scalar(out=out, in0=in0, scalar1=s1, scalar2=s2, op0=op0, op1=op1), r=r, w=w)

    def stt(out, in0, scalar, in1, op0, op1, r, w):
        add('dve', lambda e: e.scalar_tensor_tensor(out=out, in0=in0, scalar=scalar, in1=in1, op0=op0, op1=op1), r=r, w=w)

    def memset(eng, ap, val, w):
        add(eng, lambda e: e.memset(ap, val), w=w)

    dma('sp', id16[:], c_id16[:, :], [], ['id16'])
    dma('sp', idf[:], c_idf[:, :], [], ['idf'])
    dma('sp', tri[:], c_tri[:, :, :], [], ['tri'])
    dma('sp', iota[:], c_iota[:, :], [], ['iota'])
    dma('sp', gfin_bc[:], g_fin.partition_broadcast(128), [], ['gfin_bc'])
    dma('pool', selmap[:], c_selmap.rearrange("(t p) j -> p t j", p=128), [], ['selmap'])
    memset('pool', zer16[:], 0.0, ['zer16'])
    memset('pool', x[:], 0.0, ['x'])

    def load_weights(L):
        for k in range(8):
            dma('pool', Wi[:, k, :], w_in[L, k * 128:(k + 1) * 128, :], [], ['Wi'], max_dma_last_dim=4096)
        dma('pool', Wo[:], w_out[L].rearrange("(k p) n -> p k n", p=128), [], ['Wo'])
        dma('pool', Wg[:], w_pg[L].rearrange("(k p) n -> p k n", p=128), [], ['Wg'])
        dma('pool', Wp[:], w_pp[L].rearrange("(k p) n -> p k n", p=128), [], ['Wp'])
        for kv in range(2):
            dma('pool', W1[kv * 64:(kv + 1) * 64, :, :], cw1[kv][L].rearrange("l d f -> d l f"), [], ['W1'])
            dma('pool', posT[kv * 64:(kv + 1) * 64, :], cpos[kv][L].rearrange("l d -> d l"), [], ['posT'],
                allow_slow_non_contiguous=True)
        dma('pool', w2stage[:, 0, :, :], cw2[0][L].rearrange("(c p) d -> p c d", p=128), [], ['w2stage'])
        dma('pool', W2v[:], cw2[1][L].rearrange("(c p) d -> p c d", p=128), [], ['W2v'])
        memset('pool', W2k[:], 0.0, ['W2k'])
        for fc in range(2):
            for kvh in range(2):
                cp('pool', W2k[:, fc, kvh, kvh * 64:(kvh + 1) * 64], w2stage[:, 0, fc, :], ['w2stage', 'W2k'], ['W2k'])
        dma('sp', gcol[:, 0, :], g_mix[L].rearrange("(k p) -> p k", p=128), [], ['gcol'], allow_slow_non_contiguous=True)
        dma('sp', gcol[:, 1, :], g_ple[L].rearrange("(k p) -> p k", p=128), [], ['gcol'], allow_slow_non_contiguous=True)
        cnt['A'] += 2
        for kv in range(2):
            a = psA[kv]; an = 'psA%d' % kv
            for fc in range(2):
                for l in range(32):
                    mm(a[:, fc:fc + 1], W1[kv * 64:(kv + 1) * 64, l, fc * 128:(fc + 1) * 128], posT[kv * 64:(kv + 1) * 64, l:l + 1],
                       l == 0, l == 31, ['W1', 'posT'], [an])
        for kv in range(2):
            cp('dve', b1[:, kv, :], psA[kv][:, 0:2], ['psA%d' % kv], ['b1'])

    def rmsnorm_T(src, srcn, gi, dstT, dstn):
        actf(junk[:], src, AF.Square, [srcn], ['h16'])
        add('dve', lambda e: e.reduce_sum(out=st4[:, 0:1], in_=junk[:], axis=AX.X), r=['h16'], w=['st4'])
        actf(st4[:, 1:2], st4[:, 0:1], AF.Sqrt, ['st4'], ['st4'], scale=1.0 / D, bias=1e-6)
        add('dve', lambda e: e.reciprocal(out=st4[:, 2:3], in_=st4[:, 1:2]), r=['st4'], w=['st4'])
        ts('dve', h16[:], src, st4[:, 2:3], None, OP.mult, None, [srcn, 'st4'], ['h16'])
        for k in range(8):
            tp(psT[:, k * 128:(k + 1) * 128], h16[:, k * 128:(k + 1) * 128], id16[:, 0:128], ['h16', 'id16'], ['psT'])
        tt('dve', dstT, psT[:, 0:1024].rearrange("p (k t) -> p k t", k=8),
           gcol[:, gi, :].unsqueeze(2).to_broadcast([128, 8, 128]), OP.mult, ['psT', 'gcol'], [dstn])

    def rope(src, dst, H, cs, r, w):
        C = rp[:, cs * 16:(cs + 1) * 16].unsqueeze(1).to_broadcast([128, H, 16])
        Sn = rp[:, (cs + 1) * 16:(cs + 1) * 16 + 8].unsqueeze(1).to_broadcast([128, H, 8])
        Sp = rp[:, (cs + 1) * 16 + 8:(cs + 2) * 16].unsqueeze(1).to_broadcast([128, H, 8])
        if cs == 2:
            add('act', lambda e: e.mul(out=dst[:, :, 16:64], in_=src[:, :, 16:64], mul=0.125), r=r, w=w)
        else:
            cp('act', dst[:, :, 16:64], src[:, :, 16:64], r, w)
        tt('dve', rt[:, 0:H, :], src[:, :, 0:16], C, OP.mult, r + ['rp'], ['rt'])
        tt('dve', ru[:, 0:H, 0:8], src[:, :, 8:16], Sn, OP.mult, r + ['rp'], ['ru'])
        tt('dve', ru[:, 0:H, 8:16], src[:, :, 0:8], Sp, OP.mult, r + ['rp', 'ru'], ['ru'])
        tt('dve', dst[:, :, 0:16], rt[:, 0:H, :], ru[:, 0:H, :], OP.add, ['rt', 'ru'], w)

    def proj_phase(L, rope_src, npaged_dst, nwin_dst, NR=128):
        dma('sp', rp[:], rope_src, [], ['rp'])
        rmsnorm_T(x[:], 'x', 0, hT[:], 'hT')
        groups = [(0, 512, 0), (512, 512, 512), (1024, 280, 1024), (1816, 512, 1304), (2328, 256, 1816)]
        for (c0, wd, d0) in groups:
            a = psA[nxt('A', 2)]; an = 'psA%d' % ((cnt['A'] - 1) & 1)
            for k in range(8):
                mm(a[:, 0:wd], hT[:, k, :], Wi[:, k, c0:c0 + wd], k == 0, k == 7, ['hT', 'Wi'], [an])
            if c0 == 1024:
                cp('dve', proj[:, 1024:1280], a[:, 0:256], [an], ['proj'])
                actf(gate[:], a[:, 256:280], AF.Sigmoid, [an], ['gate'])
            else:
                cp('dve', proj[:, d0:d0 + wd], a[:, 0:wd], [an], ['proj'])
        for (c0, d0) in [(C_ZA, 0), (C_ZB, 512)]:
            a = psA[nxt('A', 2)]; an = 'psA%d' % ((cnt['A'] - 1) & 1)
            for k in range(8):
                mm(a[:, 0:512], hT[:, k, :], Wi[:, k, c0:c0 + 512], k == 0, k == 7, ['hT', 'Wi'], [an])
            actf(sz[:, d0:d0 + 512], a[:, 0:512], AF.Silu, [an], ['sz'])
        kva = lambda s: proj[:, 512 + s * 128: 512 + (s + 1) * 128]
        kvb = lambda s: proj[:, 1816 + s * 128: 1816 + (s + 1) * 128]
        h2 = lambda ap: ap.rearrange("p (h d) -> p h d", d=64)
        cp('pool', okv[:, 0:256], proj[:, 512:768], ['proj'], ['okv'])
        cp('pool', okv[:, 384:512], kva(3), ['proj'], ['okv'])
        cp('pool', okv[:, 640:768], kvb(1), ['proj'], ['okv'])
        cp('pool', ow[:, 128:256], kva(5), ['proj'], ['ow'])
        rope(h2(kva(2)), h2(okv[:, 256:384]), 2, 0, ['proj'], ['okv'])
        rope(h2(kvb(0)), h2(okv[:, 512:640]), 2, 0, ['proj'], ['okv'])
        rope(h2(kva(4)), h2(ow[:, 0:128]), 2, 0, ['proj'], ['ow'])
        dma('sp', npaged_dst, okv[0:NR, :], ['okv'], [])
        if nwin_dst is not None:
            dma('sp', nwin_dst, ow[0:NR, :], ['ow'], [])
        qa = proj[:, 0:512].rearrange("p (k g d) -> p k g d", k=2, g=4)
        qb = proj[:, 1304:1816].rearrange("p (k g d) -> p k g d", k=2, g=4)
        for kvh in range(2):
            add('act', lambda e, kvh=kvh: e.mul(out=q16[:, 0, :, kvh, :], in_=qa[:, kvh, :, :], mul=0.125), r=['proj'], w=['q16'])
            rope(qa[:, kvh, :, :], q16[:, 1, :, kvh, :], 4, 2, ['proj'], ['q16'])
            rope(qb[:, kvh, :, :], q16[:, 2, :, kvh, :], 4, 2, ['proj'], ['q16'])
        cp('pool', kb16[:, 0, :], okv[:, 256:384], ['okv'], ['kb16'])
        cp('pool', kb16[:, 1, :], okv[:, 512:640], ['okv'], ['kb16'])
        cp('pool', kb16[:, 2, :], ow[:, 0:128], ['ow'], ['kb16'])
        cp('pool', kb16[:, 3:5, :].rearrange("p h (kv d) -> p h kv d", kv=2),
           okv[:, 0:256].rearrange("p (kv h d) -> p h kv d", kv=2, h=2), ['okv'], ['kb16'])
        q16f = q16[:].rearrange("p w g k d -> p (w g) (k d)")
        QTf = QT[:].rearrange("p w g t -> p (w g) t")
        for n in range(12):
            sl = n % 8
            tp(psT[:, sl * 128:(sl + 1) * 128], q16f[:, n, :], id16[:, 0:128], ['q16', 'id16'], ['psT'])
            if n == 7:
                cp('act', QTf[:, 0:8, :], psT[:, 0:1024].rearrange("p (n t) -> p n t", t=128), ['psT'], ['QT'])
            if n == 11:
                cp('act', QTf[:, 8:12, :], psT[:, 0:512].rearrange("p (n t) -> p n t", t=128), ['psT'], ['QT'])

    def kv_append(i, keyoff):
        for b in range(5):
            tp(psT[:, b * 128:(b + 1) * 128], kb16[:, b, :], id16[:, 0:128], ['kb16', 'id16'], ['psT'])
        cp('act', KT[:, :, i * 128:(i + 1) * 128], psT[:, 0:256].rearrange("p (b t) -> p b t", b=2), ['psT'], [('KT', i)])
        cp('act', KTw[:, (i % 8) * 128:(i % 8 + 1) * 128], psT[:, 256:384], ['psT'], [('KTw', i % 8)])
        cp('act', ckT[:, :, 0:16], ckT[:, :, 128:144], ['ckT'], ['ckT'])
        cp('act', ckT[:, :, 16:144], psT[:, 384:640].rearrange("p (b t) -> p b t", b=2), ['psT', 'ckT'], ['ckT'])
        add('dve', lambda e: e.reduce_sum(out=ksum[:, i:i + 1], in_=KT[:, 1, i * 128:(i + 1) * 128], axis=AX.X),
            r=[('KT', i)], w=['ksum'])
        cp('pool', VS[:, i, 0, :, 0:64], okv[:, 384:512].rearrange("p (k d) -> p k d", k=2), ['okv'], [('VS', i)])
        cp('pool', VS[:, i, 1, :, 0:64], okv[:, 640:768].rearrange("p (k d) -> p k d", k=2), ['okv', ('VS', i)], [('VS', i)])
        cp('pool', VSw[:, i % 8, :, 0:64], ow[:, 128:256].rearrange("p (k d) -> p k d", k=2), ['ow'], [('VSw', i % 8)])
        if i % 2 == 1:
            n = i // 2
            tt('dve', ksum[:, NT:NT + 1], ksum[:, i - 1:i], ksum[:, i:i + 1], OP.add, ['ksum'], ['ksum'])
            ts('dve', kmT[:, n:n + 1], ksum[:, NT:NT + 1], 1.0 / 256, None, OP.mult, None, ['ksum'], ['kmT'])

    def compress(c0, b0, nb, src, srcn, hidv_t, kcT_t, res_sfx=''):
        cnt['A'] += 2
        for kv in range(2):
            a = psA[kv]; an = 'psA%d' % kv
            for kvh in range(2):
                for fc in range(2):
                    o = (kvh * 2 + fc) * nb
                    for l in range(32):
                        mm(a[:, o:o + nb], W1[kv * 64:(kv + 1) * 64, l, fc * 128:(fc + 1) * 128],
                           src[kv * 64:(kv + 1) * 64, kvh, 16 * b0 + l:16 * b0 + l + 16 * (nb - 1) + 1:16],
                           l == 0, l == 31, ['W1', srcn], [an])
        for kv in range(2):
            av = psA[kv][:, 0:4 * nb].rearrange("p (kvh fc b) -> p kvh fc b", kvh=2, fc=2)
            an = 'psA%d' % kv
            for fc in range(2):
                if kv == 0:
                    actf(hk16[:, fc, :, 0:nb], av[:, :, fc, :], AF.Silu, [an, 'b1'], ['hk16'], bias=b1[:, 0, fc:fc + 1])
                else:
                    actf(hidv_t[:, fc, :, c0:c0 + nb], av[:, :, fc, :], AF.Silu, [an, 'b1'], ['hidv' + res_sfx], bias=b1[:, 1, fc:fc + 1])
        a2 = psA[nxt('A', 2)]; an2 = 'psA%d' % ((cnt['A'] - 1) & 1)
        n = 0
        for kvh in range(2):
            for fc in range(2):
                mm(a2[:, 0:nb], W2k[:, fc, kvh, :], hk16[:, fc, kvh, 0:nb], n == 0, n == 3, ['W2k', 'hk16'], [an2])
                n += 1
        cp('dve', kcT_t[:, c0:c0 + nb], a2[:, 0:nb], [an2], ['kcT' + res_sfx])

    def vc_refresh(ct, rows, hidv_t, vcs_t, res_sfx=''):
        a = psA[nxt('A', 2)]; an = 'psA%d' % ((cnt['A'] - 1) & 1)
        for kvh in range(2):
            for fc in range(2):
                mm(a[0:rows, kvh * 64:(kvh + 1) * 64], hidv_t[:, fc, kvh, ct * 128:ct * 128 + rows], W2v[:, fc, :],
                   fc == 0, fc == 1, ['hidv' + res_sfx, 'W2v'], [an])
        cp('dve', vcs_t[0:rows, ct, :, 0:64], a[0:rows, 0:128].rearrange("p (k d) -> p k d", k=2), [an], ['vcs' + res_sfx])

    def attn_core(NQ, kvh, qap, tiles, extra_r=()):
        W = 4 * NQ
        o = psO[nxt('O', 2)]; on = 'psO%d' % ((cnt['O'] - 1) & 1)
        nt = len(tiles)
        for n, (ktap, nk, vap, extras, rd) in enumerate(tiles):
            s = psS[nxt('S', 2)]; sn = 'psS%d' % ((cnt['S'] - 1) & 1)
            mm(s[0:nk, 0:W], ktap, qap, True, True, list(rd) + ['QT'], [sn])
            for j, (el, er, p0, p1, c0, c1, rd2) in enumerate(extras):
                mm(s[p0:p1, c0:c1], el, er, False, False, list(rd2), [sn], skip=True)
            pi = nxt('P', 3)
            p = pT[pi]
            actf(p[0:nk, 0:W], s[0:nk, 0:W], AF.Exp, [sn], ['pT%d' % pi])
            mm(o[0:65, 0:W], vap, p[0:nk, 0:W], n == 0, n == nt - 1, list(rd) + ['pT%d' % pi], [on])
        cp('dve', oT[:, 0:W], o[0:65, 0:W], [on], ['okv'])
        for g in range(4):
            tp(psM[0:NQ, g * 65:(g + 1) * 65], oT[0:65, g * NQ:(g + 1) * NQ], idf[0:65, 0:65], ['okv', 'idf'], ['psM'])
        pm = psM[0:NQ, 0:260].rearrange("p (g e) -> p g e", g=4)
        ts('dve', rr[0:NQ, :], pm[:, :, 64], 1e-30, None, OP.max, None, ['psM'], ['rr'])
        add('dve', lambda e: e.reciprocal(out=rr[0:NQ, :], in_=rr[0:NQ, :]), r=['rr'], w=['rr'])
        return pm

    def accum_out(NQ, pm, kvh, gate_idx, first, gsrc=None, odst=None, gres='gate', ores='proj'):
        if gsrc is None:
            gsrc = gate[0:NQ, :]
        if odst is None:
            odst = oa
        if gate_idx is None:
            cfa = rr
            rd = ['rr']
        else:
            gv = gsrc.rearrange("p (h b) -> p h b", b=3)[:, kvh * 4:(kvh + 1) * 4, gate_idx]
            tt('dve', cf[0:NQ, :], rr[0:NQ, :], gv, OP.mult, ['rr', gres], ['cf'])
            cfa = cf
            rd = ['cf']
        dst = odst[0:NQ, kvh * 4:(kvh + 1) * 4, :]
        bc = cfa[0:NQ, :].unsqueeze(2).to_broadcast([NQ, 4, 64])
        if first:
            tt('dve', dst, pm[:, :, 0:64], bc, OP.mult, ['psM'] + rd, [ores])
        else:
            tt('dve', otmp[0:NQ], pm[:, :, 0:64], bc, OP.mult, ['psM'] + rd, ['otmp'])
            tt('dve', dst, dst, otmp[0:NQ], OP.add, [ores, 'otmp'], [ores])

    def sel_mask(NQ, pm_cmp_imp_ps, nj, halves, an):
        ip = pm_cmp_imp_ps
        ts('dve', imps[0:NQ, 0:nj], ip[:, 0, :], rr[0:NQ, 0:1], None, OP.mult, None, [an, 'rr'], ['imps'])
        for g in range(1, 4):
            stt(imps[0:NQ, 0:nj], ip[:, g, :], rr[0:NQ, g:g + 1], imps[0:NQ, 0:nj], OP.mult, OP.add, [an, 'rr', 'imps'], ['imps'])
        for (p0, p1, jt) in halves:
            ts('dve', elig[p0:p1, 0:nj], iota[p0:p1, 0:nj], float(jt), None, OP.is_lt, None, ['iota'], ['elig'])
        memset('dve', elig[0:NQ, 0:1], 0.0, ['elig'])
        stt(imps[0:NQ, 0:nj], imps[0:NQ, 0:nj], 1.0, elig[0:NQ, 0:nj], OP.add, OP.mult, ['imps', 'elig'], ['imps'])
        ts('dve', imps[0:NQ, 0:nj], imps[0:NQ, 0:nj], -1.0, None, OP.add, None, ['imps'], ['imps'])
        add('dve', lambda e: e.max(out=t8[0:NQ, :], in_=imps[0:NQ, 0:nj]), r=['imps'], w=['t8'])
        ts('dve', thr[0:NQ, :], t8[0:NQ, 5:6], 0.0, None, OP.max, None, ['t8'], ['thr'])
        ts('dve', elig[0:NQ, 0:nj], imps[0:NQ, 0:nj], thr[0:NQ, 0:1], None, OP.is_ge, None, ['imps', 'thr'], ['elig'])
        memset('dve', elig[0:NQ, 0:1], 1.0, ['elig'])
        for (p0, p1, jt) in halves:
            if jt < nj:
                memset('dve', elig[p0:p1, jt:jt + 1], 1.0, ['elig'])
        ts('dve', Mb[0:NQ, 0:nj], elig[0:NQ, 0:nj], -1.0, -NEG, OP.add, OP.mult, ['elig'], ['Mb'])

    def moba_mask(NQ, gps, nbk, bt, an):
        memset('dve', gsc[0:NQ, :, 0:nbk], -1e30, ['gsc'])
        if bt > 0:
            cp('dve', gsc[0:NQ, :, 0:bt], gps[:, :, 0:bt], [an], ['gsc'])
        for g in range(4):
            add('dve', lambda e, g=g: e.max(out=t8m[0:NQ, g, :], in_=gsc[0:NQ, g, 0:nbk]), r=['gsc'], w=['t8m'])
        ts('dve', thrm[0:NQ, :], t8m[0:NQ, :, 2], -1e29, None, OP.max, None, ['t8m'], ['thrm'])
        tt('dve', gsc[0:NQ, :, 0:nbk], gsc[0:NQ, :, 0:nbk], thrm[0:NQ, :].unsqueeze(2).to_broadcast([NQ, 4, nbk]), OP.is_ge, ['gsc', 'thrm'], ['gsc'])
        ts('dve', Mbm[0:NQ, :, 0:nbk], gsc[0:NQ, :, 0:nbk], -1.0, -NEG, OP.add, OP.mult, ['gsc'], ['Mbm'])
        memset('dve', Mbm[0:NQ, :, bt:bt + 1], 0.0, ['Mbm'])

    def out_phase(L, NQ, ple_src, final_dst, mid_dst, midres=None):
        tt('dve', mix16[:], oa[:].rearrange("p h d -> p (h d)"), sz[:], OP.mult, ['proj', 'sz'], ['h16'])
        dma('sp', ple[0:NQ, :], ple_src, [], ['ple'])
        for k in range(8):
            tp(psT[:, k * 128:(k + 1) * 128], mix16[:, k * 128:(k + 1) * 128], id16[:, 0:128], ['h16', 'id16'], ['psT'])
        cp('act', mixT[:], psT[:, 0:1024].rearrange("p (k t) -> p k t", k=8), ['psT'], ['hT'])
        for c in range(2):
            a = psA[nxt('A', 2)]; an = 'psA%d' % ((cnt['A'] - 1) & 1)
            for k in range(8):
                mm(a[:, 0:512], mixT[:, k, :], Wo[:, k, c * 512:(c + 1) * 512], k == 0, k == 7, ['hT', 'Wo'], [an])
            tt('dve', x[:, c * 512:(c + 1) * 512], x[:, c * 512:(c + 1) * 512], a[:, 0:512], OP.add, ['x', an], ['x'])
        rmsnorm_T(x[:], 'x', 1, hT[:], 'hT')
        cp('pool', ple16[:], ple[:], ['ple'], ['ple16'])
        for k in range(2):
            tp(psT[:, k * 128:(k + 1) * 128], ple16[:, k * 128:(k + 1) * 128], id16[:, 0:128], ['ple16', 'id16'], ['psT'])
        cp('act', pleT[:], psT[:, 0:256].rearrange("p (k t) -> p k t", k=2), ['psT'], ['pleT'])
        for c in range(2):
            a = psA[nxt('A', 2)]; an = 'psA%d' % ((cnt['A'] - 1) & 1)
            for k in range(8):
                mm(a[:, 0:512], hT[:, k, :], Wg[:, k, c * 512:(c + 1) * 512], k == 0, k == 7, ['hT', 'Wg'], [an])
            actf(sg[:, c * 512:(c + 1) * 512], a[:, 0:512], AF.Sigmoid, [an], ['proj'])
            a = psA[nxt('A', 2)]; an = 'psA%d' % ((cnt['A'] - 1) & 1)
            for k in range(2):
                mm(a[:, 0:512], pleT[:, k, :], Wp[:, k, c * 512:(c + 1) * 512], k == 0, k == 1, ['pleT', 'Wp'], [an])
            tt('dve', sg[:, c * 512:(c + 1) * 512], sg[:, c * 512:(c + 1) * 512], a[:, 0:512], OP.mult, ['proj', an], ['proj'])
        tt('pool', x[:], x[:], sg[:], OP.add, ['x', 'proj'], ['x'])
        if mid_dst is not None:
            dma('sp', mid_dst, x[0:NQ, :], ['x'], [midres])
        else:
            actf(junk[:], x[:], AF.Square, ['x'], ['h16'])
            add('dve', lambda e: e.reduce_sum(out=st4[:, 0:1], in_=junk[:], axis=AX.X), r=['h16'], w=['st4'])
            actf(st4[:, 1:2], st4[:, 0:1], AF.Sqrt, ['st4'], ['st4'], scale=1.0 / D, bias=1e-6)
            add('dve', lambda e: e.reciprocal(out=st4[:, 2:3], in_=st4[:, 1:2]), r=['st4'], w=['st4'])
            stt(sg[:], x[:], st4[:, 2:3], gfin_bc[:], OP.mult, OP.mult, ['x', 'st4', 'gfin_bc'], ['proj'])
            dma('sp', final_dst, sg[0:NQ, :], ['proj'], [])

    def prompt_tile(L, i):
        import os
        dbg = os.environ.get('KDBG')
        if dbg: print('tile', L, i, 'start', len(S.ops))
        src = xp if L == 0 else xmid
        dma('sp', x[:], src[i * 128:(i + 1) * 128, :], [('xmid', i)] if L == 1 else [], ['x'])
        wdst = None
        if i >= NT - WT:
            j = i - (NT - WT)
            wdst = npw[L, j * 128:(j + 1) * 128, :]
        proj_phase(L, c_rope_p[i * 128:(i + 1) * 128, :], npp[L, i * 128:(i + 1) * 128, :], wdst)
        if dbg: print(' after proj', len(S.ops))
        kv_append(i, 0)
        if dbg: print(' after kv_append', len(S.ops))
        b0 = 1 if i == 0 else 0
        compress(8 * i - 1 + b0, b0, 8 - b0, ckT, 'ckT', hidv, kcT)
        cts = sorted(set([max(8 * i - 1, 0) // 128, (8 * i + 6) // 128]))
        for ct in cts:
            vc_refresh(ct, 128, hidv, vcs)
        nct = (8 * i + 6) // 128 + 1
        if dbg: print(' after compress', len(S.ops))
        for ct in range(nct):
            base = 128 * i - 2048 * ct - 31

            def emit_sel(e, ct=ct, base=base):
                if 'fill' not in regcache:
                    regcache['fill'] = e.to_reg(NEG)
                return e.affine_select(out=cb[:, ct, :], in_=zer16[:], pattern=[[0, 4], [1, 128]], compare_op=OP.is_ge,
                                       fill=regcache['fill'], base=base, channel_multiplier=-16)
            add('pool', emit_sel, r=['zer16'], w=[('cb', ct)])
        for kvh in range(2):
            ks = slice(kvh * 64, (kvh + 1) * 64)
            tiles = []
            for ct in range(nct):
                tiles.append((kcT[ks, ct * 128:(ct + 1) * 128], 128, vcs[:, ct, kvh, :],
                              [(id16[:, 0:128], cb[:, ct, :], 0, 128, 0, 512, ['id16', ('cb', ct)])], ['kcT', 'vcs']))
            qap = QT[ks, 0, :, :].rearrange("p g t -> p (g t)")
            o = psO[nxt('O', 2)]; on = 'psO%d' % ((cnt['O'] - 1) & 1)
            for n, (ktap, nk, vap, extras, rd) in enumerate(tiles):
                s = psS[nxt('S', 2)]; sn = 'psS%d' % ((cnt['S'] - 1) & 1)
                mm(s[:, 0:512], ktap, qap, True, True, rd + ['QT'], [sn])
                (el, er, p0, p1, c0, c1, rd2) = extras[0]
                mm(s[:, 0:512], el, er, False, False, rd2, [sn], skip=True)
                actf(pcT[:, n, :], s[:, 0:512], AF.Exp, [sn], [('pcT', n)])
                mm(o[0:65, 0:512], vap, pcT[:, n, :], n == 0, n == nct - 1, rd + [('pcT', n)], [on])
            cp('dve', oT[:, :], o[0:65, 0:512], [on], ['okv'])
            for g in range(4):
                tp(psM[:, g * 65:(g + 1) * 65], oT[0:65, g * 128:(g + 1) * 128], idf[0:65, 0:65], ['okv', 'idf'], ['psM'])
            pm = psM[:, 0:260].rearrange("p (g e) -> p g e", g=4)
            ts('dve', rr[:, :], pm[:, :, 64], 1e-30, None, OP.max, None, ['psM'], ['rr'])
            add('dve', lambda e: e.reciprocal(out=rr[:, :], in_=rr[:, :]), r=['rr'], w=['rr'])
            accum_out(128, pm, kvh, 0, True)
            a = psA[nxt('A', 2)]; an = 'psA%d' % ((cnt['A'] - 1) & 1)
            for g in range(4):
                for n in range(nct):
                    mm(a[:, g * NJ:(g + 1) * NJ], pcT[:, n, g * 128:(g + 1) * 128], selmap[:, n, :], n == 0, n == nct - 1,
                       [('pcT', n), 'selmap'], [an])
            sel_mask(128, a[:, 0:4 * NJ].rearrange("p (g j) -> p g j", g=4), NJ, [(0, 64, 2 * i), (64, 128, 2 * i + 1)], an)
            if dbg: print('  after cmp+mask', kvh, len(S.ops))
            qrot = QT[ks, 1, :, :].rearrange("p g t -> p (g t)")
            tiles = []
            for kt in range(i + 1):
                ex = [(Mb[:, 2 * kt + hh:2 * kt + hh + 1].to_broadcast([128, 64]), id16[:, :], hh * 64, hh * 64 + 64, 0, 512, ['Mb', 'id16'])
                      for hh in range(2)]
                if kt == i:
                    ex.append((id16[:, 0:128], tri[:, 0, :], 0, 128, 0, 512, ['id16', 'tri']))
                tiles.append((KT[ks, 0, kt * 128:(kt + 1) * 128], 128, VS[:, kt, 0, kvh, :], ex, [('KT', kt), ('VS', kt)]))
            pm = attn_core(128, kvh, qrot, tiles)
            accum_out(128, pm, kvh, 1, False)
            if dbg: print('  after sel', kvh, len(S.ops))
            tiles = []
            for kt in range(max(0, i - 4), i + 1):
                ex = []
                if kt == i:
                    ex.append((id16[:, 0:128], tri[:, 0, :], 0, 128, 0, 512, ['id16', 'tri']))
                if kt == i - 4:
                    ex.append((id16[:, 0:128], tri[:, 1, :], 0, 128, 0, 512, ['id16', 'tri']))
                tiles.append((KTw[ks, (kt % 8) * 128:(kt % 8 + 1) * 128], 128, VSw[:, kt % 8, kvh, :], ex,
                              [('KTw', kt % 8), ('VSw', kt % 8)]))
            pm = attn_core(128, kvh, qrot, tiles)
            accum_out(128, pm, kvh, 2, False)
            if dbg: print('  after win', kvh, len(S.ops))
            bt = i // 2
            a = psA[nxt('A', 2)]; an = 'psA%d' % ((cnt['A'] - 1) & 1)
            for g in range(4):
                mm(a[:, g * NBP:(g + 1) * NBP], QT[ks, 2, g, :], kmT[ks, 0:NBP], True, True, ['QT', 'kmT'], [an])
            moba_mask(128, a[:, 0:4 * NBP].rearrange("p (g n) -> p g n", g=4), NBP, bt, an)
            qm = QT[ks, 2, :, :].rearrange("p g t -> p (g t)")
            tiles = []
            for kt in range(i + 1):
                ex = []
                for g in range(4):
                    ex.append((Mbm[:, g, kt // 2:kt // 2 + 1].to_broadcast([128, 128]), id16[:, 0:128], 0, 128, g * 128, (g + 1) * 128, ['Mbm', 'id16']))
                if kt == i:
                    ex.append((id16[:, 0:128], tri[:, 0, :], 0, 128, 0, 512, ['id16', 'tri']))
                tiles.append((KT[ks, 1, kt * 128:(kt + 1) * 128], 128, VS[:, kt, 1, kvh, :], ex, [('KT', kt), ('VS', kt)]))
            pm = attn_core(128, kvh, qm, tiles)
            dst = oa[:, 8 + kvh * 4:8 + (kvh + 1) * 4, :]
            tt('dve', dst, pm[:, :, 0:64], rr[:, :].unsqueeze(2).to_broadcast([128, 4, 64]), OP.mult, ['psM', 'rr'], ['proj'])
        if dbg: print(' after moba', len(S.ops))
        if L == 0:
            out_phase(L, 128, pp[L, i * 128:(i + 1) * 128, :], None, xmid[i * 128:(i + 1) * 128, :], ('xmid', i))
        else:
            out_phase(L, 128, pp[L, i * 128:(i + 1) * 128, :], y_p[i * 128:(i + 1) * 128, :], None)


    def gather(out_ap, chunk, pgidx, wname):
        add('pool', lambda e: e.indirect_dma_start(out=out_ap, out_offset=None, in_=cache3[:, :],
                                                   in_offset=bass.IndirectOffsetOnAxis(ap=idxv[:, chunk, pgidx:pgidx + 1], axis=0)),
            r=['idxv'], w=[wname], dma=True)

    def sample_setup():
        dma('pool', ptb[:], pt.partition_broadcast(128), [], ['ptb'])
        add('pool', lambda e: e.iota(iop[:], pattern=[[0, 1]], base=0, channel_multiplier=3), w=['iop'])
        cp('dve', ptf[:], ptb[:], ['ptb'], ['ptf'])
        cp('dve', iof[:], iop[:], ['iop'], ['iof'])
        ts('dve', ptf[:], ptf[:], 384.0, iof[:, 0:1], OP.mult, OP.add, ['ptf', 'iof'], ['ptf'])
        dma('pool', selmap_s[:], c_selmap_s.rearrange("(t p) j -> p t j", p=128), [], ['selmap_s'])
        dma('sp', smask[:], c_smask[:, :, :], [], ['smask'])
        dma('sp', wmask[:], c_wmask[:, :], [], ['wmask'])
        dma('sp', id8[:], c_id8[:, :], [], ['id8'])
        memset('pool', vnew[:], 1.0, ['vnew'])
        memset('pool', Mbs[:], 0.0, ['Mbs'])
        memset('pool', Mbms[:], 0.0, ['Mbms'])

    def sample_attn(s_, wq, tiles, keep=None):
        nt = len(tiles)
        for n, prep in enumerate(tiles):
            ktf, nk, vf, exf, rd = prep()
            for kvh in range(2):
                ks = slice(kvh * 64, (kvh + 1) * 64)
                sps = psS[kvh]; sn = 'psS%d' % kvh
                qap = QTs[ks, wq, s_, :, :].rearrange("p g q -> p (g q)")
                mm(sps[0:nk, 0:32], ktf(kvh), qap, True, True, list(rd) + ['QTs'], [sn])
                for (el, er, p0, p1, c0, c1, rd2) in exf(kvh):
                    mm(sps[p0:p1, c0:c1], el, er, False, False, list(rd2), [sn], skip=True)
                if keep is not None:
                    p = keep[:, n, kvh, :]; pn = ('pcTs', n, kvh)
                else:
                    pi = nxt('P', 3); p = pT[pi]; pn = 'pT%d' % pi
                actf(p[0:nk, 0:32], sps[0:nk, 0:32], AF.Exp, [sn], [pn])
                mm(psO[kvh][0:65, 0:32], vf(kvh), p[0:nk, 0:32], n == 0, n == nt - 1, list(rd) + [pn], ['psO%d' % kvh])

    def sample_fin(kvh):
        on = 'psO%d' % kvh
        cp('dve', oT[:, 0:32], psO[kvh][0:65, 0:32], [on], ['okv'])
        for g in range(4):
            tp(psM[0:8, g * 65:(g + 1) * 65], oT[0:65, g * 8:(g + 1) * 8], idf[0:65, 0:65], ['okv', 'idf'], ['psM'])
        pm = psM[0:8, 0:260].rearrange("p (g e) -> p g e", g=4)
        ts('dve', rr[0:8, :], pm[:, :, 64], 1e-30, None, OP.max, None, ['psM'], ['rr'])
        add('dve', lambda e: e.reciprocal(out=rr[0:8, :], in_=rr[0:8, :]), r=['rr'], w=['rr'])
        return pm

    def sample_seq(L, s_):
        import os
        dbg = os.environ.get('KDBG')
        if dbg: print('sample_seq', L, s_, len(S.ops))
        pbase = s_ * NPG
        memset('pool', kmTs[:], 0.0, ['kmTs'])
        memset('pool', hidvs[:], 0.0, ['hidv_s'])
        memset('pool', kcTs[:], 0.0, ['kcT_s'])
        for gi in range(NG):
            if gi > 0:
                cp('act', ckTs[:, :, 0:16], ckTs[:, :, GP * 128:GP * 128 + 16], ['ckTs'], ['ckTs'])
            else:
                memset('pool', ckTs[:, :, 0:16], 0.0, ['ckTs'])
            for j in range(GP):
                pgi = gi * GP + j
                n = nxt('G', 2)
                gather(pg1[n][:, 0, :], 0, pbase + pgi, 'pg1a_%d' % n)
                gather(pg1[n][:, 1, :], 2, pbase + pgi, 'pg1b_%d' % n)
                cp('pool', c16[n][:, 0:2, :].rearrange("p h (kv d) -> p h kv d", kv=2),
                   pg1[n][:, 0, :].rearrange("p (kv h d) -> p h kv d", kv=2, h=2), ['pg1a_%d' % n], ['c16_%d' % n])
                cp('pool', c16[n][:, 2, :], pg1[n][:, 1, 0:128], ['pg1b_%d' % n, 'c16_%d' % n], ['c16_%d' % n])
                for b in range(3):
                    tp(psT[:, b * 128:(b + 1) * 128], c16[n][:, b, :], id16[:, 0:128], ['c16_%d' % n, 'id16'], ['psT'])
                cp('act', ckTs[:, :, 16 + j * 128:16 + (j + 1) * 128], psT[:, 0:256].rearrange("p (b t) -> p b t", b=2),
                   ['psT', 'ckTs'], ['ckTs'])
                cp('act', kp16[n][:], psT[:, 256:384], ['psT'], ['kp16_%d' % n])
                add('dve', lambda e, pgi=pgi, n=n: e.reduce_sum(out=ksums[:, pgi:pgi + 1], in_=kp16[n][:], axis=AX.X),
                    r=['kp16_%d' % n], w=['ksums'])
            b0 = 1 if gi == 0 else 0
            compress(GP * 8 * gi - 1 + b0, b0, GP * 8 - b0, ckTs, 'ckTs', hidvs, kcTs, '_s')
        if dbg: print(' after pass1', len(S.ops))
        rows_of = lambda ct: min(128, NCS - ct * 128)
        for ct in range(NCTS):
            vc_refresh(ct, rows_of(ct), hidvs, vcss, '_s')
        if BTS > 0:
            kv2 = ksums[:, 0:2 * BTS].rearrange("p (n two) -> p n two", two=2)
            tt('dve', imps[:, 0:BTS], kv2[:, :, 0], kv2[:, :, 1], OP.add, ['ksums'], ['imps'])
            ts('dve', kmTs[:, 0:BTS], imps[:, 0:BTS], 1.0 / 256, None, OP.mult, None, ['imps'], ['kmTs'])
        if dbg: print(' after vc/kmean', len(S.ops))
        gsr = gs[:, s_, :]
        tiles = []
        for ct in range(NCTS):
            def prep(ct=ct):
                rws = rows_of(ct)
                return (lambda kvh: kcTs[kvh * 64:(kvh + 1) * 64, ct * 128:ct * 128 + rws], rws,
                        lambda kvh: vcss[0:rws, ct, kvh, :], lambda kvh: [], ['kcT_s', 'vcs_s'])
            tiles.append(prep)
        sample_attn(s_, 0, tiles, keep=pcTs)
        for kvh in range(2):
            pm = sample_fin(kvh)
            accum_out(8, pm, kvh, 0, True, gsrc=gsr, odst=oas, gres='gs', ores='oas')
            a = psA[nxt('A', 2)]; an = 'psA%d' % ((cnt['A'] - 1) & 1)
            for g in range(4):
                for n in range(NCTS):
                    rws = rows_of(n)
                    mm(a[0:8, g * NJS:(g + 1) * NJS], pcTs[0:rws, n, kvh, g * 8:(g + 1) * 8], selmap_s[0:rws, n, :],
                       n == 0, n == NCTS - 1, [('pcTs', n, kvh), 'selmap_s'], [an])
            sel_mask(8, a[0:8, 0:4 * NJS].rearrange("p (g j) -> p g j", g=4), NJS, [(0, 8, JTS)], an)
            cp('dve', Mbs[0:8, kvh, 0:NJS], Mb[0:8, 0:NJS], ['Mb'], ['Mbs'])
            a = psA[nxt('A', 2)]; an = 'psA%d' % ((cnt['A'] - 1) & 1)
            for g in range(4):
                mm(a[0:8, g * NBS:(g + 1) * NBS], QTs[kvh * 64:(kvh + 1) * 64, 2, s_, g, :], kmTs[kvh * 64:(kvh + 1) * 64, 0:NBS],
                   True, True, ['QTs', 'kmTs'], [an])
            moba_mask(8, a[0:8, 0:4 * NBS].rearrange("p (g n) -> p g n", g=4), NBS, BTS, an)
            cp('dve', Mbms[0:8, kvh, :, 0:BTS + 1], Mbm[0:8, :, 0:BTS + 1], ['Mbm'], ['Mbms'])

        if dbg: print(' after cmp+masks', len(S.ops))
        def new_tile(bi):
            def prep():
                return (lambda kvh: nkT[kvh * 64:(kvh + 1) * 64, bi, 0:32], 32, lambda kvh: vnew[0:32, bi, kvh, :],
                        lambda kvh: [(id16[:, 0:32], smask[:, s_, :], 0, 32, 0, 32, ['id16', 'smask'])], ['nkT', 'vnew'])
            return prep

        def kv_tile(load, bi, exf):
            def prep():
                n = nxt('G', 2)
                kin, vin, rd0 = load(n)
                cp('pool', kp16[n][:], kin, rd0, ['kp16_%d' % n])
                cp('pool', vp[n][:, :, 0:64], vin.rearrange("p (k d) -> p k d", k=2), rd0, ['vp%d' % n])
                tp(psT[:, 0:128], kp16[n][:], id16[:, 0:128], ['kp16_%d' % n, 'id16'], ['psT'])
                cp('act', ktp[n][:], psT[:, 0:128], ['psT'], ['ktp%d' % n])
                return (lambda kvh: ktp[n][kvh * 64:(kvh + 1) * 64, :], 128, lambda kvh: vp[n][:, kvh, :], exf,
                        ['ktp%d' % n, 'vp%d' % n])
            return prep

        def page_load(pgi, chunk):
            def load(n):
                gather(pg2[n][:, :], chunk, pbase + pgi, 'pg2_%d' % n)
                return pg2[n][:, 0:128], pg2[n][:, 128:256], ['pg2_%d' % n]
            return load

        tiles = []
        for pgi in range(NPG):
            exf = lambda kvh, pgi=pgi: [(Mbs[:, kvh, 2 * pgi + hh:2 * pgi + hh + 1].to_broadcast([128, 64]), id8[:, :],
                                         hh * 64, hh * 64 + 64, 0, 32, ['Mbs', 'id8']) for hh in range(2)]
            tiles.append(kv_tile(page_load(pgi, 1), 0, exf))
        tiles.append(new_tile(0))
        sample_attn(s_, 1, tiles)
        for kvh in range(2):
            pm = sample_fin(kvh)
            accum_out(8, pm, kvh, 1, False, gsrc=gsr, odst=oas, gres='gs', ores='oas')
        if dbg: print(' after sel', len(S.ops))
        dma('sp', wkb[:], wk[L, s_].rearrange("(t p) c -> p t c", p=128), [], ['wkb'])
        tiles = []
        for wt in range(4):
            def wload(n, wt=wt):
                return wkb[:, wt, 0:128], wkb[:, wt, 128:256], ['wkb']
            exf = (lambda kvh: [(id16[:, 0:128], wmask[:, :], 0, 128, 0, 32, ['id16', 'wmask'])]) if wt == 0 else (lambda kvh: [])
            tiles.append(kv_tile(wload, 2, exf))
        tiles.append(new_tile(2))
        sample_attn(s_, 1, tiles)
        for kvh in range(2):
            pm = sample_fin(kvh)
            accum_out(8, pm, kvh, 2, False, gsrc=gsr, odst=oas, gres='gs', ores='oas')
        if dbg: print(' after win', len(S.ops))
        tiles = []
        for pgi in range(NPG):
            exf = lambda kvh, pgi=pgi: [(Mbms[:, kvh, g, pgi // 2:pgi // 2 + 1].to_broadcast([128, 128]), id8[:, 0:8],
                                         0, 128, g * 8, g * 8 + 8, ['Mbms', 'id8']) for g in range(4)]
            tiles.append(kv_tile(page_load(pgi, 2), 1, exf))
        tiles.append(new_tile(1))
        sample_attn(s_, 2, tiles)
        for kvh in range(2):
            pm = sample_fin(kvh)
            dst = oas[0:8, 8 + kvh * 4:8 + (kvh + 1) * 4, :]
            tt('dve', dst, pm[:, :, 0:64], rr[0:8, :].unsqueeze(2).to_broadcast([8, 4, 64]), OP.mult, ['psM', 'rr'], ['oas'])
        dma('sp', oa[8 * s_:8 * s_ + 8, :, :], oas[:, :, :], ['oas'], ['proj'])

    def sample_phase(L):
        add('pool', lambda e: e.memset(fence_t[:], 0.0), r=P_RES, w=S_RES + ['fence_t'])
        sample_setup()
        for n in range(2):
            memset('pool', vp[n][:], 1.0, ['vp%d' % n])
        memset('pool', vcss[:], 1.0, ['vcs_s'])
        for c in range(3):
            ts('dve', idxv[:, c, :], ptf[:], float(L * NPOOL * 384 + c), None, OP.add, None, ['ptf'], ['idxv'])
        memset('pool', x[:], 0.0, ['x'])
        if L == 0:
            dma('sp', x[0:SS * 8, :], xs[:, :], [], ['x'])
        else:
            dma('sp', x[0:SS * 8, :], xsmid[:, :], ['xsmid'], ['x'])
        import os
        if os.environ.get('KDBG'): print('sample_phase', L, len(S.ops))
        proj_phase(L, c_rope_s[:, :], nsp[L, :, :], nsw[L, :, :], NR=SS * 8)
        if os.environ.get('KDBG'): print(' after sample proj', len(S.ops))
        for b in range(3):
            tp(psT[:, b * 128:(b + 1) * 128], kb16[:, b, :], id16[:, 0:128], ['kb16', 'id16'], ['psT'])
        cp('act', nkT[:], psT[:, 0:384].rearrange("p (b t) -> p b t", b=3), ['psT'], ['nkT'])
        cp('pool', vnew[:, 0, :, 0:64], okv[:, 384:512].rearrange("p (k d) -> p k d", k=2), ['okv'], ['vnew'])
        cp('pool', vnew[:, 1, :, 0:64], okv[:, 640:768].rearrange("p (k d) -> p k d", k=2), ['okv', 'vnew'], ['vnew'])
        cp('pool', vnew[:, 2, :, 0:64], ow[:, 128:256].rearrange("p (k d) -> p k d", k=2), ['ow', 'vnew'], ['vnew'])
        for wq in range(3):
            cp('dve', QTs[:, wq, :, :, :], QT[:, wq, :, 0:SS * 8].rearrange("p g (s q) -> p s g q", s=SS), ['QT'], ['QTs'])
        for s_ in range(SS):
            dma('sp', gs[:, s_, :], gate[8 * s_:8 * s_ + 8, :], ['gate'], ['gs'])
        for s_ in range(SS):
            sample_seq(L, s_)
        if L == 0:
            out_phase(L, SS * 8, ps_[L, :, :], None, xsmid[:, :], 'xsmid')
        else:
            out_phase(L, SS * 8, ps_[L, :, :], y_s[:, :], None)

    for L in range(2):
        load_weights(L)
        add('pool', lambda e: e.memset(fence_t[:], 0.0), r=S_RES, w=P_RES + ['fence_t'])
        memset('pool', VS[:], 1.0, [('VS', t) for t in range(NT)])
        memset('pool', VSw[:], 1.0, [('VSw', t) for t in range(8)])
        memset('pool', vcs[:], 1.0, ['vcs'])
        memset('pool', hidv[:], 0.0, ['hidv'])
        memset('pool', kcT[:], 0.0, ['kcT'])
        memset('pool', ckT[:], 0.0, ['ckT'])
        memset('pool', kmT[:], 0.0, ['kmT'])
        for i in range(NT):
            prompt_tile(L, i)
        if do_sample:
            sample_phase(L)
    S.finalize(es)
    es.close()
    return nc


def _consts(T, PAST):
    half = 8
    inv = (np.float32(500000.0) ** (-np.arange(half, dtype=np.float32) / np.float32(half))).astype(np.float32)

    def table(pos):
        ang = pos.astype(np.float32)[:, None] * inv[None, :]
        c, s = np.cos(ang).astype(np.float32), np.sin(ang).astype(np.float32)
        Ck = np.concatenate([c, c], 1); Sk = np.concatenate([-s, s], 1)
        return np.concatenate([Ck, Sk, Ck * np.float32(0.125), Sk * np.float32(0.125)], 1).astype(np.float32)

    rope_p = table(np.arange(T))
    rs = table(PAST + np.arange(8))
    rope_s = np.zeros((128, 64), np.float32)
    rope_s[:32] = np.tile(rs, (4, 1))
    bf = ml_dtypes.bfloat16
    id16 = np.tile(np.eye(128, dtype=np.float32), (1, 4)).astype(bf)
    idf = np.eye(128, dtype=np.float32)
    k = np.arange(128)[:, None]; q = np.arange(128)[None, :]
    t0 = np.where(k <= q, 0.0, NEG).astype(np.float32)
    t1 = np.where(k >= q, 0.0, NEG).astype(np.float32)
    tri = np.stack([np.tile(t0, (1, 4)), np.tile(t1, (1, 4))], 1).astype(bf)
    ncmp = T // 16 - 1
    nsel = T // 64
    c0 = np.arange(ncmp)[:, None] * 16; j0 = np.arange(nsel)[None, :] * 64
    ov = np.clip(np.minimum(c0 + 32, j0 + 64) - np.maximum(c0, j0), 0, None) / 16.0
    NCP = (ncmp + 127) // 128 * 128
    selmap = np.zeros((NCP, nsel), np.float32); selmap[:ncmp] = ov
    iota = np.tile(np.arange(256, dtype=np.float32)[None, :], (128, 1))
    ncs = PAST // 16 - 1
    ncts = (ncs + 127) // 128
    njs = max(PAST // 64, 8)
    c0 = np.arange(ncs)[:, None] * 16; j0 = np.arange(njs)[None, :] * 64
    ovs = np.clip(np.minimum(c0 + 32, j0 + 64) - np.maximum(c0, j0), 0, None) / 16.0
    selmap_s = np.zeros((ncts * 128, njs), np.float32); selmap_s[:ncs] = ovs
    kk = np.arange(32)[:, None, None]; ss = np.arange(4)[None, :, None]; qq = (np.arange(32) % 8)[None, None, :]
    smask = np.zeros((128, 4, 32), np.float32)
    smask[:32] = np.where((kk // 8 == ss) & (kk % 8 <= qq), 0.0, NEG)
    smask = smask.astype(bf)
    rr_ = np.arange(128)[:, None]; q2 = (np.arange(32) % 8)[None, :]
    wmask = np.where(rr_ < q2, NEG, 0.0).astype(np.float32).astype(bf)
    id8 = np.zeros((128, 32), np.float32)
    id8[:8] = np.tile(np.eye(8, dtype=np.float32), (1, 4))
    id8 = id8.astype(bf)
    return dict(c_rope_p=rope_p, c_rope_s=rope_s, c_id16=id16, c_idf=idf, c_tri=tri, c_selmap=selmap, c_iota=iota,
                c_selmap_s=selmap_s, c_smask=smask, c_wmask=wmask, c_id8=id8)


_CACHE = {}


def run(inputs, T, NPG, NPOOL, n_cores=8, do_sample=True):
    key = (T, NPG, NPOOL, do_sample)
    if key not in _CACHE:
        _CACHE[key] = build(T, NPG, NPOOL, do_sample=do_sample)
    nc = _CACHE[key]
    PAST = NPG * 128
    cst = _consts(T, PAST)
    f = lambda a: np.ascontiguousarray(np.asarray(a))
    B = inputs['x_prompt'].shape[0]
    SS = 4
    WT = min(4, T // 128)
    in_maps = []
    for c in range(n_cores):
        b = c % B
        s0 = c * SS
        m = dict(cst)
        m['xp'] = f(inputs['x_prompt'][b]); m['pp'] = f(inputs['p_prompt'][:, b])
        m['xs'] = f(inputs['x_sample'][s0:s0 + SS].reshape(SS * 8, D))
        m['ps'] = f(inputs['p_sample'][:, s0:s0 + SS].reshape(2, SS * 8, 256))
        m['cache'] = f(inputs['cache_paged_kv'].reshape(2, NPOOL, 128, 768))
        m['wk'] = f(inputs['cache_win_kv'][:, s0:s0 + SS].reshape(2, SS, 512, 256))
        m['pt'] = f(inputs['page_table'][s0:s0 + SS].reshape(1, SS * NPG).astype(np.int32))
        for k in ['g_mix', 'w_in', 'w_out', 'cmp_pos_k', 'cmp_pos_v', 'cmp_w1_k', 'cmp_w1_v', 'cmp_w2_k', 'cmp_w2_v',
                  'g_ple', 'w_ple_gate', 'w_ple_proj', 'g_final']:
            m[k] = f(inputs[k])
        in_maps.append(m)
    res = run_bass_kernel_spmd(nc, in_maps, core_ids=list(range(n_cores)))
    R = res.results
    y_p = np.stack([R[b]['y_p'] for b in range(B)])
    npp = np.stack([R[b]['npp'] for b in range(B)], 1).reshape(2, B, T, 6, 2, 64)
    npw = np.stack([R[b]['npw'] for b in range(B)], 1).reshape(2, B, WT * 128, 2, 2, 64)
    y_s = np.concatenate([R[c]['y_s'] for c in range(n_cores)]).reshape(n_cores * SS, 8, D)
    nsp = np.concatenate([R[c]['nsp'].reshape(2, SS, 8, 768) for c in range(n_cores)], 1).reshape(2, n_cores * SS, 8, 6, 2, 64)
    nsw = np.concatenate([R[c]['nsw'].reshape(2, SS, 8, 256) for c in range(n_cores)], 1).reshape(2, n_cores * SS, 8, 2, 2, 64)
    return (y_p.astype(np.float32), y_s.astype(np.float32), npp.astype(np.float32), npw.astype(np.float32),
            nsp.astype(np.float32), nsw.astype(np.float32))


def kernel(**inputs):
    inputs = {k: np.asarray(v) for k, v in inputs.items()}
    T = inputs['x_prompt'].shape[1]
    NPG = inputs['page_table'].shape[1]
    NPOOL = inputs['cache_paged_kv'].shape[1]
    return run(inputs, T, NPG, NPOOL)
```

```python
import math
from contextlib import ExitStack
import numpy as np
import ml_dtypes
import concourse.bass as bass
import concourse.mybir as mybir
from concourse.bass_utils import run_bass_kernel_spmd

F32, BF16, I32 = mybir.dt.float32, mybir.dt.bfloat16, mybir.dt.int32
AF = mybir.ActivationFunctionType
OP = mybir.AluOpType
AX = mybir.AxisListType

D = 1024
NEG = -30000.0
ENGS = ['pe', 'act', 'dve', 'pool', 'sp']
ETYPE = {'sp': mybir.EngineType.SP, 'act': mybir.EngineType.Activation, 'pool': mybir.EngineType.Pool}
IN_COLS = 3096
C_QA, C_KVA, C_GATE, C_ZA, C_QB, C_KVB, C_ZB = 0, 512, 1280, 1304, 1816, 2328, 2584


class Sched:
    def __init__(self, nc):
        self.nc = nc
        self.ops = []
        self.last_w = {}
        self.readers = {}
        self.by_eng = {e: [] for e in ENGS}
        self.nchan = {'sp': 8, 'pool': 4, 'act': 2}
        self.chan_next = {q: 0 for q in self.nchan}
        self.chan_last = {}
        self.chan_cnt = {}

    def add(self, eng, emit, r=(), w=(), dma=False):
        import os
        if len(self.ops) >= int(os.environ.get('KLIMIT', '100000000')):
            return -1
        idx = len(self.ops)
        deps = set()
        for x in r:
            if x in self.last_w:
                deps.add(self.last_w[x])
        for x in w:
            if x in self.last_w:
                deps.add(self.last_w[x])
            deps.update(self.readers.get(x, ()))
        op = {'idx': idx, 'eng': eng, 'emit': emit, 'deps': deps, 'dma': dma, 'signal': False}
        if dma:
            c = self.chan_next[eng]
            self.chan_next[eng] = (c + 1) % self.nchan[eng]
            key = ('dma', eng, c)
            if key in self.chan_last:
                deps.add(self.chan_last[key])
            self.chan_last[key] = idx
            self.chan_cnt[key] = self.chan_cnt.get(key, 0) + 1
            op['semkey'] = key
            op['val'] = 16 * self.chan_cnt[key]
        for x in w:
            self.last_w[x] = idx
            self.readers[x] = []
        for x in r:
            self.readers.setdefault(x, []).append(idx)
        deps.discard(idx)
        self.ops.append(op)
        self.by_eng[eng].append(op)
        return idx

    def finalize(self, es):
        nc = self.nc
        LIM = 30000
        for op in self.ops:
            for d in op['deps']:
                dop = self.ops[d]
                if dop['dma']:
                    continue
                if dop['eng'] == 'pe' and op['eng'] == 'pe' and not op['dma']:
                    continue
                dop['signal'] = True
        cnt = {e: 0 for e in ENGS}
        semkeys = set()
        for op in self.ops:
            if op['dma']:
                semkeys.add(op['semkey'])
                continue
            if op['signal']:
                e = op['eng']
                cnt[e] += 1
                op['semkey'] = ('eng', e, (cnt[e] - 1) // LIM)
                op['val'] = (cnt[e] - 1) % LIM + 1
                semkeys.add(op['semkey'])
        sems = {}
        for k in sorted(semkeys):
            sems[k] = es.enter_context(nc.semaphore("s_" + "_".join(str(x) for x in k)))
        final = {}
        for op in self.ops:
            if 'semkey' in op:
                final[op['semkey']] = max(final.get(op['semkey'], 0), op['val'])
        ops = self.ops
        by_eng = self.by_eng
        block = es.enter_context(nc.Block())

        def run(eng, e):
            seen = {}
            for op in by_eng[eng]:
                for d in sorted(op['deps']):
                    dop = ops[d]
                    if 'semkey' not in dop:
                        continue
                    if (not dop['dma']) and dop['eng'] == 'pe' and eng == 'pe' and not op['dma']:
                        continue
                    k, v = dop['semkey'], dop['val']
                    if seen.get(k, 0) >= v:
                        continue
                    seen[k] = v
                    e.wait_ge(sems[k], v)
                inst = op['emit'](e)
                if op['dma']:
                    inst.then_inc(sems[op['semkey']], 16)
                elif op['signal']:
                    inst.then_inc(sems[op['semkey']], 1)
            if eng == 'sp':
                for k in sorted(final):
                    if k[0] == 'dma':
                        e.wait_ge(sems[k], final[k])

        @block.tensor
        def _(e):
            run('pe', e)

        @block.scalar
        def _(e):
            run('act', e)

        @block.vector
        def _(e):
            run('dve', e)

        @block.gpsimd
        def _(e):
            run('pool', e)

        @block.sync
        def _(e):
            run('sp', e)


def build(T, NPG, NPOOL, SS=4, do_sample=True):
    NT = T // 128
    NJ = T // 64
    NB = max(T // 256, 1)
    NBP = max(NB, 8)
    NCT = (T // 16 - 1 + 127) // 128
    NCP = NCT * 128
    PAST = NPG * 128
    WT = min(4, NT)
    nc = bass.Bass("TRN2", target_bir_lowering=False)
    es = ExitStack()

    def din(name, shape, dt=F32):
        return nc.dram_tensor(name, list(shape), dt, kind="ExternalInput").ap()

    def dout(name, shape, dt=F32):
        return nc.dram_tensor(name, list(shape), dt, kind="ExternalOutput").ap()

    xp = din("xp", [T, D]); pp = din("pp", [2, T, 256])
    xs = din("xs", [SS * 8, D]); ps_ = din("ps", [2, SS * 8, 256])
    cache = din("cache", [2, NPOOL, 128, 768]); wk = din("wk", [2, SS, 512, 256])
    pt = din("pt", [1, SS * NPG], I32)
    g_mix = din("g_mix", [2, D]); w_in = din("w_in", [2, D, IN_COLS]); w_out = din("w_out", [2, D, D])
    cpos = [din("cmp_pos_k", [2, 32, 64]), din("cmp_pos_v", [2, 32, 64])]
    cw1 = [din("cmp_w1_k", [2, 32, 64, 256]), din("cmp_w1_v", [2, 32, 64, 256])]
    cw2 = [din("cmp_w2_k", [2, 256, 64]), din("cmp_w2_v", [2, 256, 64])]
    g_ple = din("g_ple", [2, D]); w_pg = din("w_ple_gate", [2, D, D]); w_pp = din("w_ple_proj", [2, 256, D])
    g_fin = din("g_final", [D])
    c_rope_p = din("c_rope_p", [T, 64]); c_rope_s = din("c_rope_s", [128, 64])
    c_id16 = din("c_id16", [128, 512], BF16); c_idf = din("c_idf", [128, 128])
    c_tri = din("c_tri", [128, 2, 512], BF16)
    c_selmap = din("c_selmap", [NCP, NJ]); c_iota = din("c_iota", [128, 256])
    NCS = PAST // 16 - 1
    NCTS = (NCS + 127) // 128
    NJS = max(PAST // 64, 8)
    JTS = PAST // 64
    BTS = PAST // 256
    NBS = max(BTS, 8)
    GP = min(16, NPG)
    NG = NPG // GP
    c_selmap_s = din("c_selmap_s", [NCTS * 128, NJS])
    c_smask = din("c_smask", [128, SS, 32], BF16); c_wmask = din("c_wmask", [128, 32], BF16)
    c_id8 = din("c_id8", [128, 32], BF16)
    xsmid = nc.dram_tensor("xsmid", [SS * 8, D], F32, kind="Internal").ap()
    cache3 = cache.rearrange("l n p (c s) -> (l n p c) s", c=3)
    y_p = dout("y_p", [T, D]); y_s = dout("y_s", [SS * 8, D])
    npp = dout("npp", [2, T, 768]); npw = dout("npw", [2, WT * 128, 256])
    nsp = dout("nsp", [2, SS * 8, 768]); nsw = dout("nsw", [2, SS * 8, 256])
    xmid = nc.dram_tensor("xmid", [T, D], F32, kind="Internal").ap()

    def sb(name, shape, dt=F32):
        return es.enter_context(nc.sbuf_tensor(name, list(shape), dt))

    def pst(name, shape, dt=F32):
        return es.enter_context(nc.psum_tensor(name, list(shape), dt))

    NBM = max(NBP, NBS) + 1
    Wi = sb("Wi", [128, 8, IN_COLS], BF16); Wo = sb("Wo", [128, 8, D], BF16)
    Wg = sb("Wg", [128, 8, D], BF16); Wp = sb("Wp", [128, 2, D], BF16)
    W1 = sb("W1", [128, 32, 256], BF16)
    W2k = sb("W2k", [128, 2, 2, 128], BF16)
    W2v = sb("W2v", [128, 2, 64], BF16)
    w2stage = sb("w2stage", [128, 2, 2, 64], BF16)
    posT = sb("posT", [128, 32], BF16)
    b1 = sb("b1", [128, 2, 2])
    gcol = sb("gcol", [128, 3, 8])
    gfin_bc = sb("gfin_bc", [128, D])
    id16 = sb("id16", [128, 512], BF16); idf = sb("idf", [128, 128])
    tri = sb("tri", [128, 2, 512], BF16)
    selmap = sb("selmap", [128, NCT, NJ], BF16)
    iota = sb("iota", [128, 256])
    zer16 = sb("zer16", [128, 512], BF16)
    ksum = sb("ksum", [128, NT + 1])
    kmT = sb("kmT", [128, NBP], BF16)
    x = sb("x", [128, D])
    st4 = sb("st4", [128, 8])
    h16 = sb("h16", [128, D], BF16); hT = sb("hT", [128, 8, 128], BF16)
    junk = h16; mix16 = h16; mixT = hT
    proj = sb("proj", [128, 2072])
    sg = proj[:, 0:1024]
    oa = proj[:, 1024:2048].rearrange("p (h d) -> p h d", d=64)
    sz = sb("sz", [128, D], BF16)
    gate = sb("gate", [128, 24])
    rp = sb("rp", [128, 64])
    rt = sb("rt", [128, 8, 16]); ru = sb("ru", [128, 8, 16])
    okv = sb("okv", [128, 768]); ow = sb("ow", [128, 256])
    oT = okv[0:65, 0:512]
    kb16 = sb("kb16", [128, 5, 128], BF16)
    q16 = sb("q16", [128, 3, 4, 2, 64], BF16)
    QT = sb("QT", [128, 3, 4, 128], BF16)
    pT = [sb("pT%d" % i, [128, 512], BF16) for i in range(3)]
    rr = sb("rr", [128, 4]); cf = sb("cf", [128, 4])
    otmp = sb("otmp", [128, 4, 64])
    imps = sb("imps", [128, 128]); elig = sb("elig", [128, 128]); t8 = sb("t8", [128, 8]); thr = sb("thr", [128, 1])
    Mb = sb("Mb", [128, 128], BF16)
    gsc = sb("gsc", [128, 4, NBM]); Mbm = sb("Mbm", [128, 4, NBM], BF16); t8m = sb("t8m", [128, 4, 8]); thrm = sb("thrm", [128, 4])
    ple = sb("ple", [128, 256]); ple16 = sb("ple16", [128, 256], BF16); pleT = sb("pleT", [128, 2, 128], BF16)
    hk16 = sb("hk16", [128, 2, 2, 128 if do_sample else 8], BF16)
    if do_sample:
        iop = sb("iop", [128, 1], I32); iof = sb("iof", [128, 1])
        ksums = sb("ksums", [128, NPG]); kmTs = sb("kmTs", [128, NBS], BF16)
        gs = sb("gs", [8, SS, 24])
    def _need(shape, dt):
        n = 1
        for v in shape[1:]:
            n *= v
        n *= 2 if dt in (F32, I32) else 1
        return (n + 15) // 16 * 16
    p_items = [('KT', [128, 2, T], BF16), ('KTw', [128, 8 * 128], BF16), ('VS', [128, NT, 2, 2, 65], BF16),
               ('VSw', [128, 8, 2, 65], BF16), ('ckT', [128, 2, 144], BF16), ('kcT', [128, NCP], BF16),
               ('hidv', [128, 2, 2, NCP], BF16), ('vcs', [128, NCT, 2, 65], BF16), ('pcT', [128, max(NCT, 1), 512], BF16),
               ('cb', [128, 2, 512], BF16)]
    s_items = []
    if do_sample:
        s_items = [('pg1_0', [128, 2, 256], F32), ('pg1_1', [128, 2, 256], F32), ('pg2_0', [128, 256], F32), ('pg2_1', [128, 256], F32),
                   ('c16_0', [128, 3, 128], BF16), ('c16_1', [128, 3, 128], BF16), ('ckTs', [128, 2, 16 + GP * 128], BF16),
                   ('kcTs', [128, NCTS * 128], BF16), ('hidvs', [128, 2, 2, NCTS * 128], BF16), ('vcss', [128, NCTS, 2, 65], BF16),
                   ('pcTs', [128, NCTS, 2, 32], BF16), ('ktp0', [128, 128], BF16), ('ktp1', [128, 128], BF16),
                   ('vp0', [128, 2, 65], BF16), ('vp1', [128, 2, 65], BF16), ('kp16_0', [128, 128], BF16), ('kp16_1', [128, 128], BF16),
                   ('wkb', [128, 4, 256], F32), ('ptb', [128, SS * NPG], I32), ('ptf', [128, SS * NPG], F32),
                   ('idxv', [128, 3, SS * NPG], I32), ('selmap_s', [128, NCTS, NJS], BF16), ('smask', [128, SS, 32], BF16),
                   ('wmask', [128, 32], BF16), ('id8', [128, 32], BF16), ('Mbs', [128, 2, 128], BF16), ('Mbms', [128, 2, 4, NBM], BF16),
                   ('nkT', [128, 3, 128], BF16), ('vnew', [128, 3, 2, 65], BF16), ('QTs', [128, 3, SS, 4, 8], BF16),
                   ('oas', [8, 16, 64], F32)]
    asz = max(sum(_need(sh, dt) for _, sh, dt in p_items), sum(_need(sh, dt) for _, sh, dt in s_items))
    arena = sb("arena", [128, asz], BF16)
    AV = {}
    for items in (p_items, s_items):
        off = 0
        for nm, sh, dt in items:
            nb_ = _need(sh, dt)
            n = 1
            for v in sh[1:]:
                n *= v
            v_ = arena[0:sh[0], off:off + (n * 2 if dt in (F32, I32) else n)]
            if dt in (F32, I32):
                v_ = v_.bitcast(dt)
            if len(sh) > 2:
                names = 'abcd'[:len(sh) - 1]
                v_ = v_.rearrange("p (%s) -> p %s" % (' '.join(names), ' '.join(names)), **{names[i]: sh[i + 1] for i in range(len(names))})
            AV[nm] = v_
            off += nb_
    KT, KTw, VS, VSw, ckT, kcT, hidv, vcs, pcT, cb = [AV[k] for k in ['KT', 'KTw', 'VS', 'VSw', 'ckT', 'kcT', 'hidv', 'vcs', 'pcT', 'cb']]
    if do_sample:
        pg1 = [AV['pg1_0'], AV['pg1_1']]; pg2 = [AV['pg2_0'], AV['pg2_1']]; c16 = [AV['c16_0'], AV['c16_1']]
        ckTs, kcTs, hidvs, vcss, pcTs, wkb = [AV[k] for k in ['ckTs', 'kcTs', 'hidvs', 'vcss', 'pcTs', 'wkb']]
        ktp = [AV['ktp0'], AV['ktp1']]; vp = [AV['vp0'], AV['vp1']]; kp16 = [AV['kp16_0'], AV['kp16_1']]
        ptb, ptf, idxv, selmap_s, smask, wmask, id8, Mbs, Mbms, nkT, vnew, QTs, oas = [AV[k] for k in [
            'ptb', 'ptf', 'idxv', 'selmap_s', 'smask', 'wmask', 'id8', 'Mbs', 'Mbms', 'nkT', 'vnew', 'QTs', 'oas']]
    P_RES = ([('KT', t) for t in range(NT)] + [('VS', t) for t in range(NT)] + [('KTw', t) for t in range(8)] + [('VSw', t) for t in range(8)]
             + ['ckT', 'kcT', 'hidv', 'vcs', ('cb', 0), ('cb', 1)] + [('pcT', n) for n in range(max(NCT, 1))])
    S_RES = []
    if do_sample:
        S_RES = (['pg1a_0', 'pg1a_1', 'pg1b_0', 'pg1b_1', 'pg2_0', 'pg2_1', 'c16_0', 'c16_1', 'ckTs', 'kcT_s', 'hidv_s', 'vcs_s',
                  'ktp0', 'ktp1', 'vp0', 'vp1', 'kp16_0', 'kp16_1', 'wkb', 'ptb', 'ptf', 'idxv', 'selmap_s', 'smask', 'wmask',
                  'id8', 'Mbs', 'Mbms', 'nkT', 'vnew', 'QTs', 'oas'] + [('pcTs', n, k) for n in range(NCTS) for k in range(2)])
    fence_t = sb("fence_t", [128, 8], BF16)
    psA = [pst("psA0", [128, 512]), pst("psA1", [128, 512])]
    psS = [pst("psS0", [128, 512]), pst("psS1", [128, 512])]
    psO = [pst("psO0", [128, 512]), pst("psO1", [128, 512])]
    psT = pst("psT", [128, 1024], BF16)
    psM = pst("psM", [128, 512])

    S = Sched(nc)
    add = S.add
    cnt = {'A': 0, 'S': 0, 'O': 0, 'P': 0, 'G': 0}
    regcache = {}

    def nxt(k, n):
        v = cnt[k] % n
        cnt[k] += 1
        return v

    def dma(q, out, in_, r, w, **kw):
        add(q, lambda e: e.dma_start(out=out, in_=in_, **kw), r=r, w=w, dma=True)

    def mm(out, lhsT, rhs, start, stop, r, w, skip=False):
        add('pe', lambda e: e.matmul(out, lhsT, rhs, start=start, stop=stop, skip_group_check=skip), r=r, w=w)

    def tp(out, in_, ident, r, w):
        add('pe', lambda e: e.transpose(out, in_, ident), r=r, w=w)

    def actf(out, in_, func, r, w, **kw):
        add('act', lambda e: e.activation(out=out, in_=in_, func=func, **kw), r=r, w=w)

    def cp(eng, out, in_, r, w):
        if eng == 'act':
            add('act', lambda e: e.copy(out=out, in_=in_), r=r, w=w)
        else:
            add(eng, lambda e: e.tensor_copy(out=out, in_=in_), r=r, w=w)

    def tt(eng, out, in0, in1, op, r, w):
        add(eng, lambda e: e.tensor_tensor(out=out, in0=in0, in1=in1, op=op), r=r, w=w)

    def ts(eng, out, in0, s1, s2, op0, op1, r, w):
        if op1 is None:
            add(eng, lambda e: e.tensor_scalar(out=out, in0=in0, scalar1=s1, scalar2=None, op0=op0), r=r, w=w)
        else:
            add(eng, lambda e: e.tensor_scalar(out=out, in0=in0, scalar1=s1, scalar2=s2, op0=op0, op1=op1), r=r, w=w)

    def stt(out, in0, scalar, in1, op0, op1, r, w):
        add('dve', lambda e: e.scalar_tensor_tensor(out=out, in0=in0, scalar=scalar, in1=in1, op0=op0, op1=op1), r=r, w=w)

    def memset(eng, ap, val, w):
        add(eng, lambda e: e.memset(ap, val), w=w)

    dma('sp', id16[:], c_id16[:, :], [], ['id16'])
    dma('sp', idf[:], c_idf[:, :], [], ['idf'])
    dma('sp', tri[:], c_tri[:, :, :], [], ['tri'])
    dma('sp', iota[:], c_iota[:, :], [], ['iota'])
    dma('sp', gfin_bc[:], g_fin.partition_broadcast(128), [], ['gfin_bc'])
    dma('pool', selmap[:], c_selmap.rearrange("(t p) j -> p t j", p=128), [], ['selmap'])
    memset('pool', zer16[:], 0.0, ['zer16'])
    memset('pool', x[:], 0.0, ['x'])

    def load_weights(L):
        for k in range(8):
            dma('pool', Wi[:, k, :], w_in[L, k * 128:(k + 1) * 128, :], [], ['Wi'], max_dma_last_dim=4096)
        dma('pool', Wo[:], w_out[L].rearrange("(k p) n -> p k n", p=128), [], ['Wo'])
        dma('pool', Wg[:], w_pg[L].rearrange("(k p) n -> p k n", p=128), [], ['Wg'])
        dma('pool', Wp[:], w_pp[L].rearrange("(k p) n -> p k n", p=128), [], ['Wp'])
        for kv in range(2):
            dma('pool', W1[kv * 64:(kv + 1) * 64, :, :], cw1[kv][L].rearrange("l d f -> d l f"), [], ['W1'])
            dma('pool', posT[kv * 64:(kv + 1) * 64, :], cpos[kv][L].rearrange("l d -> d l"), [], ['posT'],
                allow_slow_non_contiguous=True)
        dma('pool', w2stage[:, 0, :, :], cw2[0][L].rearrange("(c p) d -> p c d", p=128), [], ['w2stage'])
        dma('pool', W2v[:], cw2[1][L].rearrange("(c p) d -> p c d", p=128), [], ['W2v'])
        memset('pool', W2k[:], 0.0, ['W2k'])
        for fc in range(2):
            for kvh in range(2):
                cp('pool', W2k[:, fc, kvh, kvh * 64:(kvh + 1) * 64], w2stage[:, 0, fc, :], ['w2stage', 'W2k'], ['W2k'])
        dma('sp', gcol[:, 0, :], g_mix[L].rearrange("(k p) -> p k", p=128), [], ['gcol'], allow_slow_non_contiguous=True)
        dma('sp', gcol[:, 1, :], g_ple[L].rearrange("(k p) -> p k", p=128), [], ['gcol'], allow_slow_non_contiguous=True)
        cnt['A'] += 2
        for kv in range(2):
            a = psA[kv]; an = 'psA%d' % kv
            for fc in range(2):
                for l in range(32):
                    mm(a[:, fc:fc + 1], W1[kv * 64:(kv + 1) * 64, l, fc * 128:(fc + 1) * 128], posT[kv * 64:(kv + 1) * 64, l:l + 1],
                       l == 0, l == 31, ['W1', 'posT'], [an])
        for kv in range(2):
            cp('dve', b1[:, kv, :], psA[kv][:, 0:2], ['psA%d' % kv], ['b1'])

    def rmsnorm_T(src, srcn, gi, dstT, dstn):
        actf(junk[:], src, AF.Square, [srcn], ['h16'])
        add('dve', lambda e: e.reduce_sum(out=st4[:, 0:1], in_=junk[:], axis=AX.X), r=['h16'], w=['st4'])
        actf(st4[:, 1:2], st4[:, 0:1], AF.Sqrt, ['st4'], ['st4'], scale=1.0 / D, bias=1e-6)
        add('dve', lambda e: e.reciprocal(out=st4[:, 2:3], in_=st4[:, 1:2]), r=['st4'], w=['st4'])
        ts('dve', h16[:], src, st4[:, 2:3], None, OP.mult, None, [srcn, 'st4'], ['h16'])
        for k in range(8):
            tp(psT[:, k * 128:(k + 1) * 128], h16[:, k * 128:(k + 1) * 128], id16[:, 0:128], ['h16', 'id16'], ['psT'])
        tt('dve', dstT, psT[:, 0:1024].rearrange("p (k t) -> p k t", k=8),
           gcol[:, gi, :].unsqueeze(2).to_broadcast([128, 8, 128]), OP.mult, ['psT', 'gcol'], [dstn])

    def rope(src, dst, H, cs, r, w):
        C = rp[:, cs * 16:(cs + 1) * 16].unsqueeze(1).to_broadcast([128, H, 16])
        Sn = rp[:, (cs + 1) * 16:(cs + 1) * 16 + 8].unsqueeze(1).to_broadcast([128, H, 8])
        Sp = rp[:, (cs + 1) * 16 + 8:(cs + 2) * 16].unsqueeze(1).to_broadcast([128, H, 8])
        if cs == 2:
            add('act', lambda e: e.mul(out=dst[:, :, 16:64], in_=src[:, :, 16:64], mul=0.125), r=r, w=w)
        else:
            cp('act', dst[:, :, 16:64], src[:, :, 16:64], r, w)
        tt('dve', rt[:, 0:H, :], src[:, :, 0:16], C, OP.mult, r + ['rp'], ['rt'])
        tt('dve', ru[:, 0:H, 0:8], src[:, :, 8:16], Sn, OP.mult, r + ['rp'], ['ru'])
        tt('dve', ru[:, 0:H, 8:16], src[:, :, 0:8], Sp, OP.mult, r + ['rp', 'ru'], ['ru'])
        tt('dve', dst[:, :, 0:16], rt[:, 0:H, :], ru[:, 0:H, :], OP.add, ['rt', 'ru'], w)

    def proj_phase(L, rope_src, npaged_dst, nwin_dst, NR=128):
        dma('sp', rp[:], rope_src, [], ['rp'])
        rmsnorm_T(x[:], 'x', 0, hT[:], 'hT')
        groups = [(0, 512, 0), (512, 512, 512), (1024, 280, 1024), (1816, 512, 1304), (2328, 256, 1816)]
        for (c0, wd, d0) in groups:
            a = psA[nxt('A', 2)]; an = 'psA%d' % ((cnt['A'] - 1) & 1)
            for k in range(8):
                mm(a[:, 0:wd], hT[:, k, :], Wi[:, k, c0:c0 + wd], k == 0, k == 7, ['hT', 'Wi'], [an])
            if c0 == 1024:
                cp('dve', proj[:, 1024:1280], a[:, 0:256], [an], ['proj'])
                actf(gate[:], a[:, 256:280], AF.Sigmoid, [an], ['gate'])
            else:
                cp('dve', proj[:, d0:d0 + wd], a[:, 0:wd], [an], ['proj'])
        for (c0, d0) in [(C_ZA, 0), (C_ZB, 512)]:
            a = psA[nxt('A', 2)]; an = 'psA%d' % ((cnt['A'] - 1) & 1)
            for k in range(8):
                mm(a[:, 0:512], hT[:, k, :], Wi[:, k, c0:c0 + 512], k == 0, k == 7, ['hT', 'Wi'], [an])
            actf(sz[:, d0:d0 + 512], a[:, 0:512], AF.Silu, [an], ['sz'])
        kva = lambda s: proj[:, 512 + s * 128: 512 + (s + 1) * 128]
        kvb = lambda s: proj[:, 1816 + s * 128: 1816 + (s + 1) * 128]
        h2 = lambda ap: ap.rearrange("p (h d) -> p h d", d=64)
        cp('pool', okv[:, 0:256], proj[:, 512:768], ['proj'], ['okv'])
        cp('pool', okv[:, 384:512], kva(3), ['proj'], ['okv'])
        cp('pool', okv[:, 640:768], kvb(1), ['proj'], ['okv'])
        cp('pool', ow[:, 128:256], kva(5), ['proj'], ['ow'])
        rope(h2(kva(2)), h2(okv[:, 256:384]), 2, 0, ['proj'], ['okv'])
        rope(h2(kvb(0)), h2(okv[:, 512:640]), 2, 0, ['proj'], ['okv'])
        rope(h2(kva(4)), h2(ow[:, 0:128]), 2, 0, ['proj'], ['ow'])
        dma('sp', npaged_dst, okv[0:NR, :], ['okv'], [])
        if nwin_dst is not None:
            dma('sp', nwin_dst, ow[0:NR, :], ['ow'], [])
        qa = proj[:, 0:512].rearrange("p (k g d) -> p k g d", k=2, g=4)
        qb = proj[:, 1304:1816].rearrange("p (k g d) -> p k g d", k=2, g=4)
        for kvh in range(2):
            add('act', lambda e, kvh=kvh: e.mul(out=q16[:, 0, :, kvh, :], in_=qa[:, kvh, :, :], mul=0.125), r=['proj'], w=['q16'])
            rope(qa[:, kvh, :, :], q16[:, 1, :, kvh, :], 4, 2, ['proj'], ['q16'])
            rope(qb[:, kvh, :, :], q16[:, 2, :, kvh, :], 4, 2, ['proj'], ['q16'])
        cp('pool', kb16[:, 0, :], okv[:, 256:384], ['okv'], ['kb16'])
        cp('pool', kb16[:, 1, :], okv[:, 512:640], ['okv'], ['kb16'])
        cp('pool', kb16[:, 2, :], ow[:, 0:128], ['ow'], ['kb16'])
        cp('pool', kb16[:, 3:5, :].rearrange("p h (kv d) -> p h kv d", kv=2),
           okv[:, 0:256].rearrange("p (kv h d) -> p h kv d", kv=2, h=2), ['okv'], ['kb16'])
        q16f = q16[:].rearrange("p w g k d -> p (w g) (k d)")
        QTf = QT[:].rearrange("p w g t -> p (w g) t")
        for n in range(12):
            sl = n % 8
            tp(psT[:, sl * 128:(sl + 1) * 128], q16f[:, n, :], id16[:, 0:128], ['q16', 'id16'], ['psT'])
            if n == 7:
                cp('act', QTf[:, 0:8, :], psT[:, 0:1024].rearrange("p (n t) -> p n t", t=128), ['psT'], ['QT'])
            if n == 11:
                cp('act', QTf[:, 8:12, :], psT[:, 0:512].rearrange("p (n t) -> p n t", t=128), ['psT'], ['QT'])

    def kv_append(i, keyoff):
        for b in range(5):
            tp(psT[:, b * 128:(b + 1) * 128], kb16[:, b, :], id16[:, 0:128], ['kb16', 'id16'], ['psT'])
        cp('act', KT[:, :, i * 128:(i + 1) * 128], psT[:, 0:256].rearrange("p (b t) -> p b t", b=2), ['psT'], [('KT', i)])
        cp('act', KTw[:, (i % 8) * 128:(i % 8 + 1) * 128], psT[:, 256:384], ['psT'], [('KTw', i % 8)])
        cp('act', ckT[:, :, 0:16], ckT[:, :, 128:144], ['ckT'], ['ckT'])
        cp('act', ckT[:, :, 16:144], psT[:, 384:640].rearrange("p (b t) -> p b t", b=2), ['psT', 'ckT'], ['ckT'])
        add('dve', lambda e: e.reduce_sum(out=ksum[:, i:i + 1], in_=KT[:, 1, i * 128:(i + 1) * 128], axis=AX.X),
            r=[('KT', i)], w=['ksum'])
        cp('pool', VS[:, i, 0, :, 0:64], okv[:, 384:512].rearrange("p (k d) -> p k d", k=2), ['okv'], [('VS', i)])
        cp('pool', VS[:, i, 1, :, 0:64], okv[:, 640:768].rearrange("p (k d) -> p k d", k=2), ['okv', ('VS', i)], [('VS', i)])
        cp('pool', VSw[:, i % 8, :, 0:64], ow[:, 128:256].rearrange("p (k d) -> p k d", k=2), ['ow'], [('VSw', i % 8)])
        if i % 2 == 1:
            n = i // 2
            tt('dve', ksum[:, NT:NT + 1], ksum[:, i - 1:i], ksum[:, i:i + 1], OP.add, ['ksum'], ['ksum'])
            ts('dve', kmT[:, n:n + 1], ksum[:, NT:NT + 1], 1.0 / 256, None, OP.mult, None, ['ksum'], ['kmT'])

    def compress(c0, b0, nb, src, srcn, hidv_t, kcT_t, res_sfx=''):
        cnt['A'] += 2
        for kv in range(2):
            a = psA[kv]; an = 'psA%d' % kv
            for kvh in range(2):
                for fc in range(2):
                    o = (kvh * 2 + fc) * nb
                    for l in range(32):
                        mm(a[:, o:o + nb], W1[kv * 64:(kv + 1) * 64, l, fc * 128:(fc + 1) * 128],
                           src[kv * 64:(kv + 1) * 64, kvh, 16 * b0 + l:16 * b0 + l + 16 * (nb - 1) + 1:16],
                           l == 0, l == 31, ['W1', srcn], [an])
        for kv in range(2):
            av = psA[kv][:, 0:4 * nb].rearrange("p (kvh fc b) -> p kvh fc b", kvh=2, fc=2)
            an = 'psA%d' % kv
            for fc in range(2):
                if kv == 0:
                    actf(hk16[:, fc, :, 0:nb], av[:, :, fc, :], AF.Silu, [an, 'b1'], ['hk16'], bias=b1[:, 0, fc:fc + 1])
                else:
                    actf(hidv_t[:, fc, :, c0:c0 + nb], av[:, :, fc, :], AF.Silu, [an, 'b1'], ['hidv' + res_sfx], bias=b1[:, 1, fc:fc + 1])
        a2 = psA[nxt('A', 2)]; an2 = 'psA%d' % ((cnt['A'] - 1) & 1)
        n = 0
        for kvh in range(2):
            for fc in range(2):
                mm(a2[:, 0:nb], W2k[:, fc, kvh, :], hk16[:, fc, kvh, 0:nb], n == 0, n == 3, ['W2k', 'hk16'], [an2])
                n += 1
        cp('dve', kcT_t[:, c0:c0 + nb], a2[:, 0:nb], [an2], ['kcT' + res_sfx])

    def vc_refresh(ct, rows, hidv_t, vcs_t, res_sfx=''):
        a = psA[nxt('A', 2)]; an = 'psA%d' % ((cnt['A'] - 1) & 1)
        for kvh in range(2):
            for fc in range(2):
                mm(a[0:rows, kvh * 64:(kvh + 1) * 64], hidv_t[:, fc, kvh, ct * 128:ct * 128 + rows], W2v[:, fc, :],
                   fc == 0, fc == 1, ['hidv' + res_sfx, 'W2v'], [an])
        cp('dve', vcs_t[0:rows, ct, :, 0:64], a[0:rows, 0:128].rearrange("p (k d) -> p k d", k=2), [an], ['vcs' + res_sfx])

    def attn_core(NQ, kvh, qap, tiles, extra_r=()):
        W = 4 * NQ
        o = psO[nxt('O', 2)]; on = 'psO%d' % ((cnt['O'] - 1) & 1)
        nt = len(tiles)
        pend = None
        for n, (ktap, nk, vap, extras, rd) in enumerate(tiles):
            s = psS[nxt('S', 2)]; sn = 'psS%d' % ((cnt['S'] - 1) & 1)
            mm(s[0:nk, 0:W], ktap, qap, True, True, list(rd) + ['QT'], [sn])
            for j, (el, er, p0, p1, c0, c1, rd2) in enumerate(extras):
                mm(s[p0:p1, c0:c1], el, er, False, False, list(rd2), [sn], skip=True)
            pi = nxt('P', 3)
            p = pT[pi]
            actf(p[0:nk, 0:W], s[0:nk, 0:W], AF.Exp, [sn], ['pT%d' % pi])
            if pend is not None:
                (pn_, pvap, pp, pnk, prd, ppi) = pend
                mm(o[0:65, 0:W], pvap, pp[0:pnk, 0:W], pn_ == 0, False, list(prd) + ['pT%d' % ppi], [on])
            pend = (n, vap, p, nk, rd, pi)
        (pn_, pvap, pp, pnk, prd, ppi) = pend
        mm(o[0:65, 0:W], pvap, pp[0:pnk, 0:W], pn_ == 0, True, list(prd) + ['pT%d' % ppi], [on])
        cp('dve', oT[:, 0:W], o[0:65, 0:W], [on], ['okv'])
        for g in range(4):
            tp(psM[0:NQ, g * 65:(g + 1) * 65], oT[0:65, g * NQ:(g + 1) * NQ], idf[0:65, 0:65], ['okv', 'idf'], ['psM'])
        pm = psM[0:NQ, 0:260].rearrange("p (g e) -> p g e", g=4)
        ts('dve', rr[0:NQ, :], pm[:, :, 64], 1e-30, None, OP.max, None, ['psM'], ['rr'])
        add('dve', lambda e: e.reciprocal(out=rr[0:NQ, :], in_=rr[0:NQ, :]), r=['rr'], w=['rr'])
        return pm

    def accum_out(NQ, pm, kvh, gate_idx, first, gsrc=None, odst=None, gres='gate', ores='proj'):
        if gsrc is None:
            gsrc = gate[0:NQ, :]
        if odst is None:
            odst = oa
        if gate_idx is None:
            cfa = rr
            rd = ['rr']
        else:
            gv = gsrc.rearrange("p (h b) -> p h b", b=3)[:, kvh * 4:(kvh + 1) * 4, gate_idx]
            tt('dve', cf[0:NQ, :], rr[0:NQ, :], gv, OP.mult, ['rr', gres], ['cf'])
            cfa = cf
            rd = ['cf']
        dst = odst[0:NQ, kvh * 4:(kvh + 1) * 4, :]
        bc = cfa[0:NQ, :].unsqueeze(2).to_broadcast([NQ, 4, 64])
        if first:
            tt('dve', dst, pm[:, :, 0:64], bc, OP.mult, ['psM'] + rd, [ores])
        else:
            tt('dve', otmp[0:NQ], pm[:, :, 0:64], bc, OP.mult, ['psM'] + rd, ['otmp'])
            tt('dve', dst, dst, otmp[0:NQ], OP.add, [ores, 'otmp'], [ores])

    def sel_mask(NQ, pm_cmp_imp_ps, nj, halves, an):
        ip = pm_cmp_imp_ps
        ts('dve', imps[0:NQ, 0:nj], ip[:, 0, :], rr[0:NQ, 0:1], None, OP.mult, None, [an, 'rr'], ['imps'])
        for g in range(1, 4):
            stt(imps[0:NQ, 0:nj], ip[:, g, :], rr[0:NQ, g:g + 1], imps[0:NQ, 0:nj], OP.mult, OP.add, [an, 'rr', 'imps'], ['imps'])
        for (p0, p1, jt) in halves:
            ts('dve', elig[p0:p1, 0:nj], iota[p0:p1, 0:nj], float(jt), None, OP.is_lt, None, ['iota'], ['elig'])
        memset('dve', elig[0:NQ, 0:1], 0.0, ['elig'])
        stt(imps[0:NQ, 0:nj], imps[0:NQ, 0:nj], 1.0, elig[0:NQ, 0:nj], OP.add, OP.mult, ['imps', 'elig'], ['imps'])
        ts('dve', imps[0:NQ, 0:nj], imps[0:NQ, 0:nj], -1.0, None, OP.add, None, ['imps'], ['imps'])
        add('dve', lambda e: e.max(out=t8[0:NQ, :], in_=imps[0:NQ, 0:nj]), r=['imps'], w=['t8'])
        ts('dve', thr[0:NQ, :], t8[0:NQ, 5:6], 0.0, None, OP.max, None, ['t8'], ['thr'])
        ts('dve', elig[0:NQ, 0:nj], imps[0:NQ, 0:nj], thr[0:NQ, 0:1], None, OP.is_ge, None, ['imps', 'thr'], ['elig'])
        memset('dve', elig[0:NQ, 0:1], 1.0, ['elig'])
        for (p0, p1, jt) in halves:
            if jt < nj:
                memset('dve', elig[p0:p1, jt:jt + 1], 1.0, ['elig'])
        ts('dve', Mb[0:NQ, 0:nj], elig[0:NQ, 0:nj], -1.0, -NEG, OP.add, OP.mult, ['elig'], ['Mb'])

    def moba_mask(NQ, gps, nbk, bt, an):
        memset('dve', gsc[0:NQ, :, 0:nbk], -1e30, ['gsc'])
        if bt > 0:
            cp('dve', gsc[0:NQ, :, 0:bt], gps[:, :, 0:bt], [an], ['gsc'])
        for g in range(4):
            add('dve', lambda e, g=g: e.max(out=t8m[0:NQ, g, :], in_=gsc[0:NQ, g, 0:nbk]), r=['gsc'], w=['t8m'])
        ts('dve', thrm[0:NQ, :], t8m[0:NQ, :, 2], -1e29, None, OP.max, None, ['t8m'], ['thrm'])
        tt('dve', gsc[0:NQ, :, 0:nbk], gsc[0:NQ, :, 0:nbk], thrm[0:NQ, :].unsqueeze(2).to_broadcast([NQ, 4, nbk]), OP.is_ge, ['gsc', 'thrm'], ['gsc'])
        ts('dve', Mbm[0:NQ, :, 0:nbk], gsc[0:NQ, :, 0:nbk], -1.0, -NEG, OP.add, OP.mult, ['gsc'], ['Mbm'])
        memset('dve', Mbm[0:NQ, :, bt:bt + 1], 0.0, ['Mbm'])

    def out_phase(L, NQ, ple_src, final_dst, mid_dst, midres=None):
        tt('dve', mix16[:], oa[:].rearrange("p h d -> p (h d)"), sz[:], OP.mult, ['proj', 'sz'], ['h16'])
        dma('sp', ple[0:NQ, :], ple_src, [], ['ple'])
        for k in range(8):
            tp(psT[:, k * 128:(k + 1) * 128], mix16[:, k * 128:(k + 1) * 128], id16[:, 0:128], ['h16', 'id16'], ['psT'])
        cp('act', mixT[:], psT[:, 0:1024].rearrange("p (k t) -> p k t", k=8), ['psT'], ['hT'])
        for c in range(2):
            a = psA[nxt('A', 2)]; an = 'psA%d' % ((cnt['A'] - 1) & 1)
            for k in range(8):
                mm(a[:, 0:512], mixT[:, k, :], Wo[:, k, c * 512:(c + 1) * 512], k == 0, k == 7, ['hT', 'Wo'], [an])
            tt('dve', x[:, c * 512:(c + 1) * 512], x[:, c * 512:(c + 1) * 512], a[:, 0:512], OP.add, ['x', an], ['x'])
        rmsnorm_T(x[:], 'x', 1, hT[:], 'hT')
        cp('pool', ple16[:], ple[:], ['ple'], ['ple16'])
        for k in range(2):
            tp(psT[:, k * 128:(k + 1) * 128], ple16[:, k * 128:(k + 1) * 128], id16[:, 0:128], ['ple16', 'id16'], ['psT'])
        cp('act', pleT[:], psT[:, 0:256].rearrange("p (k t) -> p k t", k=2), ['psT'], ['pleT'])
        for c in range(2):
            a = psA[nxt('A', 2)]; an = 'psA%d' % ((cnt['A'] - 1) & 1)
            for k in range(8):
                mm(a[:, 0:512], hT[:, k, :], Wg[:, k, c * 512:(c + 1) * 512], k == 0, k == 7, ['hT', 'Wg'], [an])
            actf(sg[:, c * 512:(c + 1) * 512], a[:, 0:512], AF.Sigmoid, [an], ['proj'])
            a = psA[nxt('A', 2)]; an = 'psA%d' % ((cnt['A'] - 1) & 1)
            for k in range(2):
                mm(a[:, 0:512], pleT[:, k, :], Wp[:, k, c * 512:(c + 1) * 512], k == 0, k == 1, ['pleT', 'Wp'], [an])
            tt('dve', sg[:, c * 512:(c + 1) * 512], sg[:, c * 512:(c + 1) * 512], a[:, 0:512], OP.mult, ['proj', an], ['proj'])
        tt('pool', x[:], x[:], sg[:], OP.add, ['x', 'proj'], ['x'])
        if mid_dst is not None:
            dma('sp', mid_dst, x[0:NQ, :], ['x'], [midres])
        else:
            actf(junk[:], x[:], AF.Square, ['x'], ['h16'])
            add('dve', lambda e: e.reduce_sum(out=st4[:, 0:1], in_=junk[:], axis=AX.X), r=['h16'], w=['st4'])
            actf(st4[:, 1:2], st4[:, 0:1], AF.Sqrt, ['st4'], ['st4'], scale=1.0 / D, bias=1e-6)
            add('dve', lambda e: e.reciprocal(out=st4[:, 2:3], in_=st4[:, 1:2]), r=['st4'], w=['st4'])
            stt(sg[:], x[:], st4[:, 2:3], gfin_bc[:], OP.mult, OP.mult, ['x', 'st4', 'gfin_bc'], ['proj'])
            dma('sp', final_dst, sg[0:NQ, :], ['proj'], [])

    def prompt_tile(L, i):
        import os
        dbg = os.environ.get('KDBG')
        if dbg: print('tile', L, i, 'start', len(S.ops))
        src = xp if L == 0 else xmid
        dma('sp', x[:], src[i * 128:(i + 1) * 128, :], [('xmid', i)] if L == 1 else [], ['x'])
        wdst = None
        if i >= NT - WT:
            j = i - (NT - WT)
            wdst = npw[L, j * 128:(j + 1) * 128, :]
        proj_phase(L, c_rope_p[i * 128:(i + 1) * 128, :], npp[L, i * 128:(i + 1) * 128, :], wdst)
        if dbg: print(' after proj', len(S.ops))
        kv_append(i, 0)
        if dbg: print(' after kv_append', len(S.ops))
        b0 = 1 if i == 0 else 0
        compress(8 * i - 1 + b0, b0, 8 - b0, ckT, 'ckT', hidv, kcT)
        cts = sorted(set([max(8 * i - 1, 0) // 128, (8 * i + 6) // 128]))
        for ct in cts:
            vc_refresh(ct, 128, hidv, vcs)
        nct = (8 * i + 6) // 128 + 1
        if dbg: print(' after compress', len(S.ops))
        for ct in range(nct):
            base = 128 * i - 2048 * ct - 31

            def emit_sel(e, ct=ct, base=base):
                if 'fill' not in regcache:
                    regcache['fill'] = e.to_reg(NEG)
                return e.affine_select(out=cb[:, ct, :], in_=zer16[:], pattern=[[0, 4], [1, 128]], compare_op=OP.is_ge,
                                       fill=regcache['fill'], base=base, channel_multiplier=-16)
            add('pool', emit_sel, r=['zer16'], w=[('cb', ct)])
        for kvh in range(2):
            ks = slice(kvh * 64, (kvh + 1) * 64)
            tiles = []
            for ct in range(nct):
                tiles.append((kcT[ks, ct * 128:(ct + 1) * 128], 128, vcs[:, ct, kvh, :],
                              [(id16[:, 0:128], cb[:, ct, :], 0, 128, 0, 512, ['id16', ('cb', ct)])], ['kcT', 'vcs']))
            qap = QT[ks, 0, :, :].rearrange("p g t -> p (g t)")
            o = psO[nxt('O', 2)]; on = 'psO%d' % ((cnt['O'] - 1) & 1)
            for n, (ktap, nk, vap, extras, rd) in enumerate(tiles):
                s = psS[nxt('S', 2)]; sn = 'psS%d' % ((cnt['S'] - 1) & 1)
                mm(s[:, 0:512], ktap, qap, True, True, rd + ['QT'], [sn])
                (el, er, p0, p1, c0, c1, rd2) = extras[0]
                mm(s[:, 0:512], el, er, False, False, rd2, [sn], skip=True)
                actf(pcT[:, n, :], s[:, 0:512], AF.Exp, [sn], [('pcT', n)])
                mm(o[0:65, 0:512], vap, pcT[:, n, :], n == 0, n == nct - 1, rd + [('pcT', n)], [on])
            cp('dve', oT[:, :], o[0:65, 0:512], [on], ['okv'])
            for g in range(4):
                tp(psM[:, g * 65:(g + 1) * 65], oT[0:65, g * 128:(g + 1) * 128], idf[0:65, 0:65], ['okv', 'idf'], ['psM'])
            pm = psM[:, 0:260].rearrange("p (g e) -> p g e", g=4)
            ts('dve', rr[:, :], pm[:, :, 64], 1e-30, None, OP.max, None, ['psM'], ['rr'])
            add('dve', lambda e: e.reciprocal(out=rr[:, :], in_=rr[:, :]), r=['rr'], w=['rr'])
            accum_out(128, pm, kvh, 0, True)
            a = psA[nxt('A', 2)]; an = 'psA%d' % ((cnt['A'] - 1) & 1)
            for g in range(4):
                for n in range(nct):
                    mm(a[:, g * NJ:(g + 1) * NJ], pcT[:, n, g * 128:(g + 1) * 128], selmap[:, n, :], n == 0, n == nct - 1,
                       [('pcT', n), 'selmap'], [an])
            sel_mask(128, a[:, 0:4 * NJ].rearrange("p (g j) -> p g j", g=4), NJ, [(0, 64, 2 * i), (64, 128, 2 * i + 1)], an)
            if dbg: print('  after cmp+mask', kvh, len(S.ops))
            qrot = QT[ks, 1, :, :].rearrange("p g t -> p (g t)")
            tiles = []
            for kt in range(i + 1):
                ex = [(Mb[:, 2 * kt + hh:2 * kt + hh + 1].to_broadcast([128, 64]), id16[:, :], hh * 64, hh * 64 + 64, 0, 512, ['Mb', 'id16'])
                      for hh in range(2)]
                if kt == i:
                    ex.append((id16[:, 0:128], tri[:, 0, :], 0, 128, 0, 512, ['id16', 'tri']))
                tiles.append((KT[ks, 0, kt * 128:(kt + 1) * 128], 128, VS[:, kt, 0, kvh, :], ex, [('KT', kt), ('VS', kt)]))
            pm = attn_core(128, kvh, qrot, tiles)
            accum_out(128, pm, kvh, 1, False)
            if dbg: print('  after sel', kvh, len(S.ops))
            tiles = []
            for kt in range(max(0, i - 4), i + 1):
                ex = []
                if kt == i:
                    ex.append((id16[:, 0:128], tri[:, 0, :], 0, 128, 0, 512, ['id16', 'tri']))
                if kt == i - 4:
                    ex.append((id16[:, 0:128], tri[:, 1, :], 0, 128, 0, 512, ['id16', 'tri']))
                tiles.append((KTw[ks, (kt % 8) * 128:(kt % 8 + 1) * 128], 128, VSw[:, kt % 8, kvh, :], ex,
                              [('KTw', kt % 8), ('VSw', kt % 8)]))
            pm = attn_core(128, kvh, qrot, tiles)
            accum_out(128, pm, kvh, 2, False)
            if dbg: print('  after win', kvh, len(S.ops))
            bt = i // 2
            a = psA[nxt('A', 2)]; an = 'psA%d' % ((cnt['A'] - 1) & 1)
            for g in range(4):
                mm(a[:, g * NBP:(g + 1) * NBP], QT[ks, 2, g, :], kmT[ks, 0:NBP], True, True, ['QT', 'kmT'], [an])
            moba_mask(128, a[:, 0:4 * NBP].rearrange("p (g n) -> p g n", g=4), NBP, bt, an)
            qm = QT[ks, 2, :, :].rearrange("p g t -> p (g t)")
            tiles = []
            for kt in range(i + 1):
                ex = []
                for g in range(4):
                    ex.append((Mbm[:, g, kt // 2:kt // 2 + 1].to_broadcast([128, 128]), id16[:, 0:128], 0, 128, g * 128, (g + 1) * 128, ['Mbm', 'id16']))
                if kt == i:
                    ex.append((id16[:, 0:128], tri[:, 0, :], 0, 128, 0, 512, ['id16', 'tri']))
                tiles.append((KT[ks, 1, kt * 128:(kt + 1) * 128], 128, VS[:, kt, 1, kvh, :], ex, [('KT', kt), ('VS', kt)]))
            pm = attn_core(128, kvh, qm, tiles)
            dst = oa[:, 8 + kvh * 4:8 + (kvh + 1) * 4, :]
            tt('dve', dst, pm[:, :, 0:64], rr[:, :].unsqueeze(2).to_broadcast([128, 4, 64]), OP.mult, ['psM', 'rr'], ['proj'])
        if dbg: print(' after moba', len(S.ops))
        if L == 0:
            out_phase(L, 128, pp[L, i * 128:(i + 1) * 128, :], None, xmid[i * 128:(i + 1) * 128, :], ('xmid', i))
        else:
            out_phase(L, 128, pp[L, i * 128:(i + 1) * 128, :], y_p[i * 128:(i + 1) * 128, :], None)


    def gather(out_ap, chunk, pgidx, wname):
        add('pool', lambda e: e.indirect_dma_start(out=out_ap, out_offset=None, in_=cache3[:, :],
                                                   in_offset=bass.IndirectOffsetOnAxis(ap=idxv[:, chunk, pgidx:pgidx + 1], axis=0)),
            r=['idxv'], w=[wname], dma=True)

    def sample_setup():
        dma('pool', ptb[:], pt.partition_broadcast(128), [], ['ptb'])
        add('pool', lambda e: e.iota(iop[:], pattern=[[0, 1]], base=0, channel_multiplier=3), w=['iop'])
        cp('dve', ptf[:], ptb[:], ['ptb'], ['ptf'])
        cp('dve', iof[:], iop[:], ['iop'], ['iof'])
        ts('dve', ptf[:], ptf[:], 384.0, iof[:, 0:1], OP.mult, OP.add, ['ptf', 'iof'], ['ptf'])
        dma('pool', selmap_s[:], c_selmap_s.rearrange("(t p) j -> p t j", p=128), [], ['selmap_s'])
        dma('sp', smask[:], c_smask[:, :, :], [], ['smask'])
        dma('sp', wmask[:], c_wmask[:, :], [], ['wmask'])
        dma('sp', id8[:], c_id8[:, :], [], ['id8'])
        memset('pool', vnew[:], 1.0, ['vnew'])
        memset('pool', Mbs[:], 0.0, ['Mbs'])
        memset('pool', Mbms[:], 0.0, ['Mbms'])

    def sample_attn(s_, wq, tiles, keep=None):
        nt = len(tiles)
        for n, prep in enumerate(tiles):
            ktf, nk, vf, exf, rd = prep()
            for kvh in range(2):
                ks = slice(kvh * 64, (kvh + 1) * 64)
                sps = psS[kvh]; sn = 'psS%d' % kvh
                qap = QTs[ks, wq, s_, :, :].rearrange("p g q -> p (g q)")
                mm(sps[0:nk, 0:32], ktf(kvh), qap, True, True, list(rd) + ['QTs'], [sn])
                for (el, er, p0, p1, c0, c1, rd2) in exf(kvh):
                    mm(sps[p0:p1, c0:c1], el, er, False, False, list(rd2), [sn], skip=True)
                if keep is not None:
                    p = keep[:, n, kvh, :]; pn = ('pcTs', n, kvh)
                else:
                    pi = nxt('P', 3); p = pT[pi]; pn = 'pT%d' % pi
                actf(p[0:nk, 0:32], sps[0:nk, 0:32], AF.Exp, [sn], [pn])
                mm(psO[kvh][0:65, 0:32], vf(kvh), p[0:nk, 0:32], n == 0, n == nt - 1, list(rd) + [pn], ['psO%d' % kvh])

    def sample_fin(kvh):
        on = 'psO%d' % kvh
        cp('dve', oT[:, 0:32], psO[kvh][0:65, 0:32], [on], ['okv'])
        for g in range(4):
            tp(psM[0:8, g * 65:(g + 1) * 65], oT[0:65, g * 8:(g + 1) * 8], idf[0:65, 0:65], ['okv', 'idf'], ['psM'])
        pm = psM[0:8, 0:260].rearrange("p (g e) -> p g e", g=4)
        ts('dve', rr[0:8, :], pm[:, :, 64], 1e-30, None, OP.max, None, ['psM'], ['rr'])
        add('dve', lambda e: e.reciprocal(out=rr[0:8, :], in_=rr[0:8, :]), r=['rr'], w=['rr'])
        return pm

    def sample_seq(L, s_):
        import os
        dbg = os.environ.get('KDBG')
        if dbg: print('sample_seq', L, s_, len(S.ops))
        pbase = s_ * NPG
        memset('pool', kmTs[:], 0.0, ['kmTs'])
        memset('pool', hidvs[:], 0.0, ['hidv_s'])
        memset('pool', kcTs[:], 0.0, ['kcT_s'])
        for gi in range(NG):
            if gi > 0:
                cp('act', ckTs[:, :, 0:16], ckTs[:, :, GP * 128:GP * 128 + 16], ['ckTs'], ['ckTs'])
            else:
                memset('pool', ckTs[:, :, 0:16], 0.0, ['ckTs'])
            for j in range(GP):
                pgi = gi * GP + j
                n = nxt('G', 2)
                gather(pg1[n][:, 0, :], 0, pbase + pgi, 'pg1a_%d' % n)
                gather(pg1[n][:, 1, :], 2, pbase + pgi, 'pg1b_%d' % n)
                cp('pool', c16[n][:, 0:2, :].rearrange("p h (kv d) -> p h kv d", kv=2),
                   pg1[n][:, 0, :].rearrange("p (kv h d) -> p h kv d", kv=2, h=2), ['pg1a_%d' % n], ['c16_%d' % n])
                cp('pool', c16[n][:, 2, :], pg1[n][:, 1, 0:128], ['pg1b_%d' % n, 'c16_%d' % n], ['c16_%d' % n])
                for b in range(3):
                    tp(psT[:, b * 128:(b + 1) * 128], c16[n][:, b, :], id16[:, 0:128], ['c16_%d' % n, 'id16'], ['psT'])
                cp('act', ckTs[:, :, 16 + j * 128:16 + (j + 1) * 128], psT[:, 0:256].rearrange("p (b t) -> p b t", b=2),
                   ['psT', 'ckTs'], ['ckTs'])
                cp('act', kp16[n][:], psT[:, 256:384], ['psT'], ['kp16_%d' % n])
                add('dve', lambda e, pgi=pgi, n=n: e.reduce_sum(out=ksums[:, pgi:pgi + 1], in_=kp16[n][:], axis=AX.X),
                    r=['kp16_%d' % n], w=['ksums'])
            b0 = 1 if gi == 0 else 0
            compress(GP * 8 * gi - 1 + b0, b0, GP * 8 - b0, ckTs, 'ckTs', hidvs, kcTs, '_s')
        if dbg: print(' after pass1', len(S.ops))
        rows_of = lambda ct: min(128, NCS - ct * 128)
        for ct in range(NCTS):
            vc_refresh(ct, rows_of(ct), hidvs, vcss, '_s')
        if BTS > 0:
            kv2 = ksums[:, 0:2 * BTS].rearrange("p (n two) -> p n two", two=2)
            tt('dve', imps[:, 0:BTS], kv2[:, :, 0], kv2[:, :, 1], OP.add, ['ksums'], ['imps'])
            ts('dve', kmTs[:, 0:BTS], imps[:, 0:BTS], 1.0 / 256, None, OP.mult, None, ['imps'], ['kmTs'])
        if dbg: print(' after vc/kmean', len(S.ops))
        gsr = gs[:, s_, :]
        tiles = []
        for ct in range(NCTS):
            def prep(ct=ct):
                rws = rows_of(ct)
                return (lambda kvh: kcTs[kvh * 64:(kvh + 1) * 64, ct * 128:ct * 128 + rws], rws,
                        lambda kvh: vcss[0:rws, ct, kvh, :], lambda kvh: [], ['kcT_s', 'vcs_s'])
            tiles.append(prep)
        sample_attn(s_, 0, tiles, keep=pcTs)
        for kvh in range(2):
            pm = sample_fin(kvh)
            accum_out(8, pm, kvh, 0, True, gsrc=gsr, odst=oas, gres='gs', ores='oas')
            a = psA[nxt('A', 2)]; an = 'psA%d' % ((cnt['A'] - 1) & 1)
            for g in range(4):
                for n in range(NCTS):
                    rws = rows_of(n)
                    mm(a[0:8, g * NJS:(g + 1) * NJS], pcTs[0:rws, n, kvh, g * 8:(g + 1) * 8], selmap_s[0:rws, n, :],
                       n == 0, n == NCTS - 1, [('pcTs', n, kvh), 'selmap_s'], [an])
            sel_mask(8, a[0:8, 0:4 * NJS].rearrange("p (g j) -> p g j", g=4), NJS, [(0, 8, JTS)], an)
            cp('dve', Mbs[0:8, kvh, 0:NJS], Mb[0:8, 0:NJS], ['Mb'], ['Mbs'])
            a = psA[nxt('A', 2)]; an = 'psA%d' % ((cnt['A'] - 1) & 1)
            for g in range(4):
                mm(a[0:8, g * NBS:(g + 1) * NBS], QTs[kvh * 64:(kvh + 1) * 64, 2, s_, g, :], kmTs[kvh * 64:(kvh + 1) * 64, 0:NBS],
                   True, True, ['QTs', 'kmTs'], [an])
            moba_mask(8, a[0:8, 0:4 * NBS].rearrange("p (g n) -> p g n", g=4), NBS, BTS, an)
            cp('dve', Mbms[0:8, kvh, :, 0:BTS + 1], Mbm[0:8, :, 0:BTS + 1], ['Mbm'], ['Mbms'])

        if dbg: print(' after cmp+masks', len(S.ops))
        def new_tile(bi):
            def prep():
                return (lambda kvh: nkT[kvh * 64:(kvh + 1) * 64, bi, 0:32], 32, lambda kvh: vnew[0:32, bi, kvh, :],
                        lambda kvh: [(id16[:, 0:32], smask[:, s_, :], 0, 32, 0, 32, ['id16', 'smask'])], ['nkT', 'vnew'])
            return prep

        def kv_tile(load, bi, exf):
            def prep():
                n = nxt('G', 2)
                kin, vin, rd0 = load(n)
                cp('pool', kp16[n][:], kin, rd0, ['kp16_%d' % n])
                cp('pool', vp[n][:, :, 0:64], vin.rearrange("p (k d) -> p k d", k=2), rd0, ['vp%d' % n])
                tp(psT[:, 0:128], kp16[n][:], id16[:, 0:128], ['kp16_%d' % n, 'id16'], ['psT'])
                cp('act', ktp[n][:], psT[:, 0:128], ['psT'], ['ktp%d' % n])
                return (lambda kvh: ktp[n][kvh * 64:(kvh + 1) * 64, :], 128, lambda kvh: vp[n][:, kvh, :], exf,
                        ['ktp%d' % n, 'vp%d' % n])
            return prep

        def page_load(pgi, chunk):
            def load(n):
                gather(pg2[n][:, :], chunk, pbase + pgi, 'pg2_%d' % n)
                return pg2[n][:, 0:128], pg2[n][:, 128:256], ['pg2_%d' % n]
            return load

        tiles = []
        for pgi in range(NPG):
            exf = lambda kvh, pgi=pgi: [(Mbs[:, kvh, 2 * pgi + hh:2 * pgi + hh + 1].to_broadcast([128, 64]), id8[:, :],
                                         hh * 64, hh * 64 + 64, 0, 32, ['Mbs', 'id8']) for hh in range(2)]
            tiles.append(kv_tile(page_load(pgi, 1), 0, exf))
        tiles.append(new_tile(0))
        sample_attn(s_, 1, tiles)
        for kvh in range(2):
            pm = sample_fin(kvh)
            accum_out(8, pm, kvh, 1, False, gsrc=gsr, odst=oas, gres='gs', ores='oas')
        if dbg: print(' after sel', len(S.ops))
        dma('sp', wkb[:], wk[L, s_].rearrange("(t p) c -> p t c", p=128), [], ['wkb'])
        tiles = []
        for wt in range(4):
            def wload(n, wt=wt):
                return wkb[:, wt, 0:128], wkb[:, wt, 128:256], ['wkb']
            exf = (lambda kvh: [(id16[:, 0:128], wmask[:, :], 0, 128, 0, 32, ['id16', 'wmask'])]) if wt == 0 else (lambda kvh: [])
            tiles.append(kv_tile(wload, 2, exf))
        tiles.append(new_tile(2))
        sample_attn(s_, 1, tiles)
        for kvh in range(2):
            pm = sample_fin(kvh)
            accum_out(8, pm, kvh, 2, False, gsrc=gsr, odst=oas, gres='gs', ores='oas')
        if dbg: print(' after win', len(S.ops))
        tiles = []
        for pgi in range(NPG):
            exf = lambda kvh, pgi=pgi: [(Mbms[:, kvh, g, pgi // 2:pgi // 2 + 1].to_broadcast([128, 128]), id8[:, 0:8],
                                         0, 128, g * 8, g * 8 + 8, ['Mbms', 'id8']) for g in range(4)]
            tiles.append(kv_tile(page_load(pgi, 2), 1, exf))
        tiles.append(new_tile(1))
        sample_attn(s_, 2, tiles)
        for kvh in range(2):
            pm = sample_fin(kvh)
            dst = oas[0:8, 8 + kvh * 4:8 + (kvh + 1) * 4, :]
            tt('dve', dst, pm[:, :, 0:64], rr[0:8, :].unsqueeze(2).to_broadcast([8, 4, 64]), OP.mult, ['psM', 'rr'], ['oas'])
        dma('sp', oa[8 * s_:8 * s_ + 8, :, :], oas[:, :, :], ['oas'], ['proj'])

    def sample_phase(L):
        add('pool', lambda e: e.memset(fence_t[:], 0.0), r=P_RES, w=S_RES + ['fence_t'])
        sample_setup()
        for n in range(2):
            memset('pool', vp[n][:], 1.0, ['vp%d' % n])
        memset('pool', vcss[:], 1.0, ['vcs_s'])
        for c in range(3):
            ts('dve', idxv[:, c, :], ptf[:], float(L * NPOOL * 384 + c), None, OP.add, None, ['ptf'], ['idxv'])
        memset('pool', x[:], 0.0, ['x'])
        if L == 0:
            dma('sp', x[0:SS * 8, :], xs[:, :], [], ['x'])
        else:
            dma('sp', x[0:SS * 8, :], xsmid[:, :], ['xsmid'], ['x'])
        import os
        if os.environ.get('KDBG'): print('sample_phase', L, len(S.ops))
        proj_phase(L, c_rope_s[:, :], nsp[L, :, :], nsw[L, :, :], NR=SS * 8)
        if os.environ.get('KDBG'): print(' after sample proj', len(S.ops))
        for b in range(3):
            tp(psT[:, b * 128:(b + 1) * 128], kb16[:, b, :], id16[:, 0:128], ['kb16', 'id16'], ['psT'])
        cp('act', nkT[:], psT[:, 0:384].rearrange("p (b t) -> p b t", b=3), ['psT'], ['nkT'])
        cp('pool', vnew[:, 0, :, 0:64], okv[:, 384:512].rearrange("p (k d) -> p k d", k=2), ['okv'], ['vnew'])
        cp('pool', vnew[:, 1, :, 0:64], okv[:, 640:768].rearrange("p (k d) -> p k d", k=2), ['okv', 'vnew'], ['vnew'])
        cp('pool', vnew[:, 2, :, 0:64], ow[:, 128:256].rearrange("p (k d) -> p k d", k=2), ['ow', 'vnew'], ['vnew'])
        for wq in range(3):
            cp('dve', QTs[:, wq, :, :, :], QT[:, wq, :, 0:SS * 8].rearrange("p g (s q) -> p s g q", s=SS), ['QT'], ['QTs'])
        for s_ in range(SS):
            dma('sp', gs[:, s_, :], gate[8 * s_:8 * s_ + 8, :], ['gate'], ['gs'])
        for s_ in range(SS):
            sample_seq(L, s_)
        if L == 0:
            out_phase(L, SS * 8, ps_[L, :, :], None, xsmid[:, :], 'xsmid')
        else:
            out_phase(L, SS * 8, ps_[L, :, :], y_s[:, :], None)

    for L in range(2):
        load_weights(L)
        add('pool', lambda e: e.memset(fence_t[:], 0.0), r=S_RES, w=P_RES + ['fence_t'])
        memset('pool', VS[:], 1.0, [('VS', t) for t in range(NT)])
        memset('pool', VSw[:], 1.0, [('VSw', t) for t in range(8)])
        memset('pool', vcs[:], 1.0, ['vcs'])
        memset('pool', hidv[:], 0.0, ['hidv'])
        memset('pool', kcT[:], 0.0, ['kcT'])
        memset('pool', ckT[:], 0.0, ['ckT'])
        memset('pool', kmT[:], 0.0, ['kmT'])
        for i in range(NT):
            prompt_tile(L, i)
        if do_sample:
            sample_phase(L)
    S.finalize(es)
    es.close()
    return nc


def _consts(T, PAST):
    half = 8
    inv = (np.float32(500000.0) ** (-np.arange(half, dtype=np.float32) / np.float32(half))).astype(np.float32)

    def table(pos):
        ang = pos.astype(np.float32)[:, None] * inv[None, :]
        c, s = np.cos(ang).astype(np.float32), np.sin(ang).astype(np.float32)
        Ck = np.concatenate([c, c], 1); Sk = np.concatenate([-s, s], 1)
        return np.concatenate([Ck, Sk, Ck * np.float32(0.125), Sk * np.float32(0.125)], 1).astype(np.float32)

    rope_p = table(np.arange(T))
    rs = table(PAST + np.arange(8))
    rope_s = np.zeros((128, 64), np.float32)
    rope_s[:32] = np.tile(rs, (4, 1))
    bf = ml_dtypes.bfloat16
    id16 = np.tile(np.eye(128, dtype=np.float32), (1, 4)).astype(bf)
    idf = np.eye(128, dtype=np.float32)
    k = np.arange(128)[:, None]; q = np.arange(128)[None, :]
    t0 = np.where(k <= q, 0.0, NEG).astype(np.float32)
    t1 = np.where(k >= q, 0.0, NEG).astype(np.float32)
    tri = np.stack([np.tile(t0, (1, 4)), np.tile(t1, (1, 4))], 1).astype(bf)
    ncmp = T // 16 - 1
    nsel = T // 64
    c0 = np.arange(ncmp)[:, None] * 16; j0 = np.arange(nsel)[None, :] * 64
    ov = np.clip(np.minimum(c0 + 32, j0 + 64) - np.maximum(c0, j0), 0, None) / 16.0
    NCP = (ncmp + 127) // 128 * 128
    selmap = np.zeros((NCP, nsel), np.float32); selmap[:ncmp] = ov
    iota = np.tile(np.arange(256, dtype=np.float32)[None, :], (128, 1))
    ncs = PAST // 16 - 1
    ncts = (ncs + 127) // 128
    njs = max(PAST // 64, 8)
    c0 = np.arange(ncs)[:, None] * 16; j0 = np.arange(njs)[None, :] * 64
    ovs = np.clip(np.minimum(c0 + 32, j0 + 64) - np.maximum(c0, j0), 0, None) / 16.0
    selmap_s = np.zeros((ncts * 128, njs), np.float32); selmap_s[:ncs] = ovs
    kk = np.arange(32)[:, None, None]; ss = np.arange(4)[None, :, None]; qq = (np.arange(32) % 8)[None, None, :]
    smask = np.zeros((128, 4, 32), np.float32)
    smask[:32] = np.where((kk // 8 == ss) & (kk % 8 <= qq), 0.0, NEG)
    smask = smask.astype(bf)
    rr_ = np.arange(128)[:, None]; q2 = (np.arange(32) % 8)[None, :]
    wmask = np.where(rr_ < q2, NEG, 0.0).astype(np.float32).astype(bf)
    id8 = np.zeros((128, 32), np.float32)
    id8[:8] = np.tile(np.eye(8, dtype=np.float32), (1, 4))
    id8 = id8.astype(bf)
    return dict(c_rope_p=rope_p, c_rope_s=rope_s, c_id16=id16, c_idf=idf, c_tri=tri, c_selmap=selmap, c_iota=iota,
                c_selmap_s=selmap_s, c_smask=smask, c_wmask=wmask, c_id8=id8)


_CACHE = {}


def run(inputs, T, NPG, NPOOL, n_cores=8, do_sample=True):
    key = (T, NPG, NPOOL, do_sample)
    if key not in _CACHE:
        _CACHE[key] = build(T, NPG, NPOOL, do_sample=do_sample)
    nc = _CACHE[key]
    PAST = NPG * 128
    cst = _consts(T, PAST)
    f = lambda a: np.ascontiguousarray(np.asarray(a))
    B = inputs['x_prompt'].shape[0]
    SS = 4
    WT = min(4, T // 128)
    in_maps = []
    for c in range(n_cores):
        b = c % B
        s0 = c * SS
        m = dict(cst)
        m['xp'] = f(inputs['x_prompt'][b]); m['pp'] = f(inputs['p_prompt'][:, b])
        m['xs'] = f(inputs['x_sample'][s0:s0 + SS].reshape(SS * 8, D))
        m['ps'] = f(inputs['p_sample'][:, s0:s0 + SS].reshape(2, SS * 8, 256))
        m['cache'] = f(inputs['cache_paged_kv'].reshape(2, NPOOL, 128, 768))
        m['wk'] = f(inputs['cache_win_kv'][:, s0:s0 + SS].reshape(2, SS, 512, 256))
        m['pt'] = f(inputs['page_table'][s0:s0 + SS].reshape(1, SS * NPG).astype(np.int32))
        for k in ['g_mix', 'w_in', 'w_out', 'cmp_pos_k', 'cmp_pos_v', 'cmp_w1_k', 'cmp_w1_v', 'cmp_w2_k', 'cmp_w2_v',
                  'g_ple', 'w_ple_gate', 'w_ple_proj', 'g_final']:
            m[k] = f(inputs[k])
        in_maps.append(m)
    res = run_bass_kernel_spmd(nc, in_maps, core_ids=list(range(n_cores)))
    R = res.results
    y_p = np.stack([R[b]['y_p'] for b in range(B)])
    npp = np.stack([R[b]['npp'] for b in range(B)], 1).reshape(2, B, T, 6, 2, 64)
    npw = np.stack([R[b]['npw'] for b in range(B)], 1).reshape(2, B, WT * 128, 2, 2, 64)
    y_s = np.concatenate([R[c]['y_s'] for c in range(n_cores)]).reshape(n_cores * SS, 8, D)
    nsp = np.concatenate([R[c]['nsp'].reshape(2, SS, 8, 768) for c in range(n_cores)], 1).reshape(2, n_cores * SS, 8, 6, 2, 64)
    nsw = np.concatenate([R[c]['nsw'].reshape(2, SS, 8, 256) for c in range(n_cores)], 1).reshape(2, n_cores * SS, 8, 2, 2, 64)
    return (y_p.astype(np.float32), y_s.astype(np.float32), npp.astype(np.float32), npw.astype(np.float32),
            nsp.astype(np.float32), nsw.astype(np.float32))


def kernel(**inputs):
    inputs = {k: np.asarray(v) for k, v in inputs.items()}
    T = inputs['x_prompt'].shape[1]
    NPG = inputs['page_table'].shape[1]
    NPOOL = inputs['cache_paged_kv'].shape[1]
    return run(inputs, T, NPG, NPOOL)
```

```python
import math
from contextlib import ExitStack
import numpy as np
import ml_dtypes
import concourse.bass as bass
import concourse.mybir as mybir
from concourse.bass_utils import run_bass_kernel_spmd

F32, BF16, I32 = mybir.dt.float32, mybir.dt.bfloat16, mybir.dt.int32
AF = mybir.ActivationFunctionType
OP = mybir.AluOpType
AX = mybir.AxisListType

D = 1024
NEG = -30000.0
ENGS = ['pe', 'act', 'dve', 'pool', 'sp']
ETYPE = {'sp': mybir.EngineType.SP, 'act': mybir.EngineType.Activation, 'pool': mybir.EngineType.Pool}
IN_COLS = 3096
C_QA, C_KVA, C_GATE, C_ZA, C_QB, C_KVB, C_ZB = 0, 512, 1280, 1304, 1816, 2328, 2584


class Sched:
    def __init__(self, nc):
        self.nc = nc
        self.ops = []
        self.last_w = {}
        self.readers = {}
        self.by_eng = {e: [] for e in ENGS}
        self.nchan = {'sp': 8, 'pool': 4, 'act': 2}
        self.chan_next = {q: 0 for q in self.nchan}
        self.chan_last = {}
        self.chan_cnt = {}

    def add(self, eng, emit, r=(), w=(), dma=False):
        import os
        if len(self.ops) >= int(os.environ.get('KLIMIT', '100000000')):
            return -1
        idx = len(self.ops)
        deps = set()
        for x in r:
            if x in self.last_w:
                deps.add(self.last_w[x])
        for x in w:
            if x in self.last_w:
                deps.add(self.last_w[x])
            deps.update(self.readers.get(x, ()))
        op = {'idx': idx, 'eng': eng, 'emit': emit, 'deps': deps, 'dma': dma, 'signal': False}
        if dma:
            c = self.chan_next[eng]
            self.chan_next[eng] = (c + 1) % self.nchan[eng]
            key = ('dma', eng, c)
            if key in self.chan_last:
                deps.add(self.chan_last[key])
            self.chan_last[key] = idx
            self.chan_cnt[key] = self.chan_cnt.get(key, 0) + 1
            op['semkey'] = key
            op['val'] = 16 * self.chan_cnt[key]
        for x in w:
            self.last_w[x] = idx
            self.readers[x] = []
        for x in r:
            self.readers.setdefault(x, []).append(idx)
        deps.discard(idx)
        self.ops.append(op)
        self.by_eng[eng].append(op)
        return idx

    def finalize(self, es):
        nc = self.nc
        LIM = 30000
        for op in self.ops:
            for d in op['deps']:
                dop = self.ops[d]
                if dop['dma']:
                    continue
                if dop['eng'] == 'pe' and op['eng'] == 'pe' and not op['dma']:
                    continue
                dop['signal'] = True
        cnt = {e: 0 for e in ENGS}
        semkeys = set()
        for op in self.ops:
            if op['dma']:
                semkeys.add(op['semkey'])
                continue
            if op['signal']:
                e = op['eng']
                cnt[e] += 1
                op['semkey'] = ('eng', e, (cnt[e] - 1) // LIM)
                op['val'] = (cnt[e] - 1) % LIM + 1
                semkeys.add(op['semkey'])
        sems = {}
        for k in sorted(semkeys):
            sems[k] = es.enter_context(nc.semaphore("s_" + "_".join(str(x) for x in k)))
        final = {}
        for op in self.ops:
            if 'semkey' in op:
                final[op['semkey']] = max(final.get(op['semkey'], 0), op['val'])
        ops = self.ops
        by_eng = self.by_eng
        block = es.enter_context(nc.Block())

        def run(eng, e):
            seen = {}
            for op in by_eng[eng]:
                for d in sorted(op['deps']):
                    dop = ops[d]
                    if 'semkey' not in dop:
                        continue
                    if (not dop['dma']) and dop['eng'] == 'pe' and eng == 'pe' and not op['dma']:
                        continue
                    k, v = dop['semkey'], dop['val']
                    if seen.get(k, 0) >= v:
                        continue
                    seen[k] = v
                    e.wait_ge(sems[k], v)
                inst = op['emit'](e)
                if op['dma']:
                    inst.then_inc(sems[op['semkey']], 16)
                elif op['signal']:
                    inst.then_inc(sems[op['semkey']], 1)
            if eng == 'sp':
                for k in sorted(final):
                    if k[0] == 'dma':
                        e.wait_ge(sems[k], final[k])

        @block.tensor
        def _(e):
            run('pe', e)

        @block.scalar
        def _(e):
            run('act', e)

        @block.vector
        def _(e):
            run('dve', e)

        @block.gpsimd
        def _(e):
            run('pool', e)

        @block.sync
        def _(e):
            run('sp', e)


def build(T, NPG, NPOOL, SS=4, do_sample=True):
    NT = T // 128
    NJ = T // 64
    NB = max(T // 256, 1)
    NBP = max(NB, 8)
    NCT = (T // 16 - 1 + 127) // 128
    NCP = NCT * 128
    PAST = NPG * 128
    WT = min(4, NT)
    nc = bass.Bass("TRN2", target_bir_lowering=False)
    es = ExitStack()

    def din(name, shape, dt=F32):
        return nc.dram_tensor(name, list(shape), dt, kind="ExternalInput").ap()

    def dout(name, shape, dt=F32):
        return nc.dram_tensor(name, list(shape), dt, kind="ExternalOutput").ap()

    xp = din("xp", [T, D]); pp = din("pp", [2, T, 256])
    xs = din("xs", [SS * 8, D]); ps_ = din("ps", [2, SS * 8, 256])
    cache = din("cache", [2, NPOOL, 128, 768]); wk = din("wk", [2, SS, 512, 256])
    pt = din("pt", [1, SS * NPG], I32)
    g_mix = din("g_mix", [2, D]); w_in = din("w_in", [2, D, IN_COLS]); w_out = din("w_out", [2, D, D])
    cpos = [din("cmp_pos_k", [2, 32, 64]), din("cmp_pos_v", [2, 32, 64])]
    cw1 = [din("cmp_w1_k", [2, 32, 64, 256]), din("cmp_w1_v", [2, 32, 64, 256])]
    cw2 = [din("cmp_w2_k", [2, 256, 64]), din("cmp_w2_v", [2, 256, 64])]
    g_ple = din("g_ple", [2, D]); w_pg = din("w_ple_gate", [2, D, D]); w_pp = din("w_ple_proj", [2, 256, D])
    g_fin = din("g_final", [D])
    c_rope_p = din("c_rope_p", [T, 64]); c_rope_s = din("c_rope_s", [128, 64])
    c_id16 = din("c_id16", [128, 512], BF16); c_idf = din("c_idf", [128, 128])
    c_tri = din("c_tri", [128, 2, 512], BF16)
    c_selmap = din("c_selmap", [NCP, NJ]); c_iota = din("c_iota", [128, 256])
    NCS = PAST // 16 - 1
    NCTS = (NCS + 127) // 128
    NJS = max(PAST // 64, 8)
    JTS = PAST // 64
    BTS = PAST // 256
    NBS = max(BTS, 8)
    GP = min(16, NPG)
    NG = NPG // GP
    c_selmap_s = din("c_selmap_s", [NCTS * 128, NJS])
    c_smask = din("c_smask", [128, SS, 32], BF16); c_wmask = din("c_wmask", [128, 32], BF16)
    c_id8 = din("c_id8", [128, 32], BF16)
    xsmid = nc.dram_tensor("xsmid", [SS * 8, D], F32, kind="Internal").ap()
    cache3 = cache.rearrange("l n p (c s) -> (l n p c) s", c=3)
    y_p = dout("y_p", [T, D]); y_s = dout("y_s", [SS * 8, D])
    npp = dout("npp", [2, T, 768]); npw = dout("npw", [2, WT * 128, 256])
    nsp = dout("nsp", [2, SS * 8, 768]); nsw = dout("nsw", [2, SS * 8, 256])
    xmid = nc.dram_tensor("xmid", [T, D], F32, kind="Internal").ap()

    def sb(name, shape, dt=F32):
        return es.enter_context(nc.sbuf_tensor(name, list(shape), dt))

    def pst(name, shape, dt=F32):
        return es.enter_context(nc.psum_tensor(name, list(shape), dt))

    NBM = max(NBP, NBS) + 1
    Wi = sb("Wi", [128, 8, IN_COLS], BF16); Wo = sb("Wo", [128, 8, D], BF16)
    Wg = sb("Wg", [128, 8, D], BF16); Wp = sb("Wp", [128, 2, D], BF16)
    W1 = sb("W1", [128, 32, 256], BF16)
    W2k = sb("W2k", [128, 2, 2, 128], BF16)
    W2v = sb("W2v", [128, 2, 64], BF16)
    w2stage = sb("w2stage", [128, 2, 2, 64], BF16)
    posT = sb("posT", [128, 32], BF16)
    b1 = sb("b1", [128, 2, 2])
    gcol = sb("gcol", [128, 3, 8])
    gfin_bc = sb("gfin_bc", [128, D])
    id16 = sb("id16", [128, 512], BF16); idf = sb("idf", [128, 128])
    tri = sb("tri", [128, 2, 512], BF16)
    selmap = sb("selmap", [128, NCT, NJ], BF16)
    iota = sb("iota", [128, 256])
    zer16 = sb("zer16", [128, 512], BF16)
    ksum = sb("ksum", [128, NT + 1])
    kmT = sb("kmT", [128, NBP], BF16)
    x = sb("x", [128, D])
    st4 = sb("st4", [128, 8])
    h16 = sb("h16", [128, D], BF16); hT = sb("hT", [128, 8, 128], BF16)
    junk = h16; mix16 = h16; mixT = hT
    proj = sb("proj", [128, 2072])
    sg = proj[:, 0:1024]
    oa = proj[:, 1024:2048].rearrange("p (h d) -> p h d", d=64)
    sz = sb("sz", [128, D], BF16)
    gate = sb("gate", [128, 24])
    rp = sb("rp", [128, 64])
    rt = sb("rt", [128, 8, 16]); ru = sb("ru", [128, 8, 16])
    okv = sb("okv", [128, 768]); ow = sb("ow", [128, 256])
    oT = okv[0:65, 0:512]
    kb16 = sb("kb16", [128, 5, 128], BF16)
    q16 = sb("q16", [128, 3, 4, 2, 64], BF16)
    QT = sb("QT", [128, 3, 4, 128], BF16)
    pT = [sb("pT%d" % i, [128, 512], BF16) for i in range(4)]
    rr = sb("rr", [128, 4]); cf = sb("cf", [128, 4])
    otmp = sb("otmp", [128, 4, 64])
    imps = sb("imps", [128, 128]); elig = sb("elig", [128, 128]); t8 = sb("t8", [128, 8]); thr = sb("thr", [128, 1])
    Mb = sb("Mb", [128, 128], BF16)
    gsc = sb("gsc", [128, 4, NBM]); Mbm = sb("Mbm", [128, 4, NBM], BF16); t8m = sb("t8m", [128, 4, 8]); thrm = sb("thrm", [128, 4])
    ple = sb("ple", [128, 256]); ple16 = sb("ple16", [128, 256], BF16); pleT = sb("pleT", [128, 2, 128], BF16)
    hk16 = sb("hk16", [128, 2, 2, 128 if do_sample else 8], BF16)
    if do_sample:
        iop = sb("iop", [128, 1], I32); iof = sb("iof", [128, 1])
        ksums = sb("ksums", [128, NPG]); kmTs = sb("kmTs", [128, NBS], BF16)
        gs = sb("gs", [8, SS, 24])
    def _need(shape, dt):
        n = 1
        for v in shape[1:]:
            n *= v
        n *= 2 if dt in (F32, I32) else 1
        return (n + 15) // 16 * 16
    p_items = [('KT', [128, 2, T], BF16), ('KTw', [128, 8 * 128], BF16), ('VS', [128, NT, 2, 2, 65], BF16),
               ('VSw', [128, 8, 2, 65], BF16), ('ckT', [128, 2, 144], BF16), ('kcT', [128, NCP], BF16),
               ('hidv', [128, 2, 2, NCP], BF16), ('vcs', [128, NCT, 2, 65], BF16), ('pcT', [128, max(NCT, 1), 512], BF16),
               ('cb', [128, 2, 512], BF16)]
    s_items = []
    if do_sample:
        s_items = [('pg1_0', [128, 2, 256], F32), ('pg1_1', [128, 2, 256], F32), ('pg2_0', [128, 256], F32), ('pg2_1', [128, 256], F32),
                   ('c16_0', [128, 3, 128], BF16), ('c16_1', [128, 3, 128], BF16), ('ckTs', [128, 2, 16 + GP * 128], BF16),
                   ('kcTs', [128, NCTS * 128], BF16), ('hidvs', [128, 2, 2, NCTS * 128], BF16), ('vcss', [128, NCTS, 2, 65], BF16),
                   ('pcTs', [128, NCTS, 2, 32], BF16), ('ktp0', [128, 128], BF16), ('ktp1', [128, 128], BF16),
                   ('vp0', [128, 2, 65], BF16), ('vp1', [128, 2, 65], BF16), ('kp16_0', [128, 128], BF16), ('kp16_1', [128, 128], BF16),
                   ('wkb', [128, 4, 256], F32), ('ptb', [128, SS * NPG], I32), ('ptf', [128, SS * NPG], F32),
                   ('idxv', [128, 3, SS * NPG], I32), ('selmap_s', [128, NCTS, NJS], BF16), ('smask', [128, SS, 32], BF16),
                   ('wmask', [128, 32], BF16), ('id8', [128, 32], BF16), ('Mbs', [128, 2, 128], BF16), ('Mbms', [128, 2, 4, NBM], BF16),
                   ('nkT', [128, 3, 128], BF16), ('vnew', [128, 3, 2, 65], BF16), ('QTs', [128, 3, SS, 4, 8], BF16),
                   ('oas', [8, 16, 64], F32)]
    asz = max(sum(_need(sh, dt) for _, sh, dt in p_items), sum(_need(sh, dt) for _, sh, dt in s_items))
    arena = sb("arena", [128, asz], BF16)
    AV = {}
    for items in (p_items, s_items):
        off = 0
        for nm, sh, dt in items:
            nb_ = _need(sh, dt)
            n = 1
            for v in sh[1:]:
                n *= v
            v_ = arena[0:sh[0], off:off + (n * 2 if dt in (F32, I32) else n)]
            if dt in (F32, I32):
                v_ = v_.bitcast(dt)
            if len(sh) > 2:
                names = 'abcd'[:len(sh) - 1]
                v_ = v_.rearrange("p (%s) -> p %s" % (' '.join(names), ' '.join(names)), **{names[i]: sh[i + 1] for i in range(len(names))})
            AV[nm] = v_
            off += nb_
    KT, KTw, VS, VSw, ckT, kcT, hidv, vcs, pcT, cb = [AV[k] for k in ['KT', 'KTw', 'VS', 'VSw', 'ckT', 'kcT', 'hidv', 'vcs', 'pcT', 'cb']]
    if do_sample:
        pg1 = [AV['pg1_0'], AV['pg1_1']]; pg2 = [AV['pg2_0'], AV['pg2_1']]; c16 = [AV['c16_0'], AV['c16_1']]
        ckTs, kcTs, hidvs, vcss, pcTs, wkb = [AV[k] for k in ['ckTs', 'kcTs', 'hidvs', 'vcss', 'pcTs', 'wkb']]
        ktp = [AV['ktp0'], AV['ktp1']]; vp = [AV['vp0'], AV['vp1']]; kp16 = [AV['kp16_0'], AV['kp16_1']]
        ptb, ptf, idxv, selmap_s, smask, wmask, id8, Mbs, Mbms, nkT, vnew, QTs, oas = [AV[k] for k in [
            'ptb', 'ptf', 'idxv', 'selmap_s', 'smask', 'wmask', 'id8', 'Mbs', 'Mbms', 'nkT', 'vnew', 'QTs', 'oas']]
    P_RES = ([('KT', t) for t in range(NT)] + [('VS', t) for t in range(NT)] + [('KTw', t) for t in range(8)] + [('VSw', t) for t in range(8)]
             + ['ckT', 'kcT', 'hidv', 'vcs', ('cb', 0), ('cb', 1)] + [('pcT', n) for n in range(max(NCT, 1))])
    S_RES = []
    if do_sample:
        S_RES = (['pg1a_0', 'pg1a_1', 'pg1b_0', 'pg1b_1', 'pg2_0', 'pg2_1', 'c16_0', 'c16_1', 'ckTs', 'kcT_s', 'hidv_s', 'vcs_s',
                  'ktp0', 'ktp1', 'vp0', 'vp1', 'kp16_0', 'kp16_1', 'wkb', 'ptb', 'ptf', 'idxv', 'selmap_s', 'smask', 'wmask',
                  'id8', 'Mbs', 'Mbms', 'nkT', 'vnew', 'QTs', 'oas'] + [('pcTs', n, k) for n in range(NCTS) for k in range(2)])
    fence_t = sb("fence_t", [128, 8], BF16)
    psA = [pst("psA0", [128, 512]), pst("psA1", [128, 512])]
    psS = [pst("psS0", [128, 512]), pst("psS1", [128, 512])]
    psO = [pst("psO0", [128, 512]), pst("psO1", [128, 512])]
    psT = pst("psT", [128, 1024], BF16)
    psM = pst("psM", [128, 512])

    S = Sched(nc)
    add = S.add
    cnt = {'A': 0, 'S': 0, 'O': 0, 'P': 0, 'G': 0}
    regcache = {}

    def nxt(k, n):
        v = cnt[k] % n
        cnt[k] += 1
        return v

    def dma(q, out, in_, r, w, **kw):
        add(q, lambda e: e.dma_start(out=out, in_=in_, **kw), r=r, w=w, dma=True)

    def mm(out, lhsT, rhs, start, stop, r, w, skip=False):
        add('pe', lambda e: e.matmul(out, lhsT, rhs, start=start, stop=stop, skip_group_check=skip), r=r, w=w)

    def tp(out, in_, ident, r, w):
        add('pe', lambda e: e.transpose(out, in_, ident), r=r, w=w)

    def actf(out, in_, func, r, w, **kw):
        add('act', lambda e: e.activation(out=out, in_=in_, func=func, **kw), r=r, w=w)

    def cp(eng, out, in_, r, w):
        if eng == 'act':
            add('act', lambda e: e.copy(out=out, in_=in_), r=r, w=w)
        else:
            add(eng, lambda e: e.tensor_copy(out=out, in_=in_), r=r, w=w)

    def tt(eng, out, in0, in1, op, r, w):
        add(eng, lambda e: e.tensor_tensor(out=out, in0=in0, in1=in1, op=op), r=r, w=w)

    def ts(eng, out, in0, s1, s2, op0, op1, r, w):
        if op1 is None:
            add(eng, lambda e: e.tensor_scalar(out=out, in0=in0, scalar1=s1, scalar2=None, op0=op0), r=r, w=w)
        else:
            add(eng, lambda e: e.tensor_scalar(out=out, in0=in0, scalar1=s1, scalar2=s2, op0=op0, op1=op1), r=r, w=w)

    def stt(out, in0, scalar, in1, op0, op1, r, w):
        add('dve', lambda e: e.scalar_tensor_tensor(out=out, in0=in0, scalar=scalar, in1=in1, op0=op0, op1=op1), r=r, w=w)

    def memset(eng, ap, val, w):
        add(eng, lambda e: e.memset(ap, val), w=w)

    dma('sp', id16[:], c_id16[:, :], [], ['id16'])
    dma('sp', idf[:], c_idf[:, :], [], ['idf'])
    dma('sp', tri[:], c_tri[:, :, :], [], ['tri'])
    dma('sp', iota[:], c_iota[:, :], [], ['iota'])
    dma('sp', gfin_bc[:], g_fin.partition_broadcast(128), [], ['gfin_bc'])
    dma('pool', selmap[:], c_selmap.rearrange("(t p) j -> p t j", p=128), [], ['selmap'])
    memset('pool', zer16[:], 0.0, ['zer16'])
    memset('pool', x[:], 0.0, ['x'])

    def load_weights(L):
        for k in range(8):
            dma('pool', Wi[:, k, :], w_in[L, k * 128:(k + 1) * 128, :], [], ['Wi'], max_dma_last_dim=4096)
        dma('pool', Wo[:], w_out[L].rearrange("(k p) n -> p k n", p=128), [], ['Wo'])
        dma('pool', Wg[:], w_pg[L].rearrange("(k p) n -> p k n", p=128), [], ['Wg'])
        dma('pool', Wp[:], w_pp[L].rearrange("(k p) n -> p k n", p=128), [], ['Wp'])
        for kv in range(2):
            dma('pool', W1[kv * 64:(kv + 1) * 64, :, :], cw1[kv][L].rearrange("l d f -> d l f"), [], ['W1'])
            dma('pool', posT[kv * 64:(kv + 1) * 64, :], cpos[kv][L].rearrange("l d -> d l"), [], ['posT'],
                allow_slow_non_contiguous=True)
        dma('pool', w2stage[:, 0, :, :], cw2[0][L].rearrange("(c p) d -> p c d", p=128), [], ['w2stage'])
        dma('pool', W2v[:], cw2[1][L].rearrange("(c p) d -> p c d", p=128), [], ['W2v'])
        memset('pool', W2k[:], 0.0, ['W2k'])
        for fc in range(2):
            for kvh in range(2):
                cp('pool', W2k[:, fc, kvh, kvh * 64:(kvh + 1) * 64], w2stage[:, 0, fc, :], ['w2stage', 'W2k'], ['W2k'])
        dma('sp', gcol[:, 0, :], g_mix[L].rearrange("(k p) -> p k", p=128), [], ['gcol'], allow_slow_non_contiguous=True)
        dma('sp', gcol[:, 1, :], g_ple[L].rearrange("(k p) -> p k", p=128), [], ['gcol'], allow_slow_non_contiguous=True)
        cnt['A'] += 2
        for kv in range(2):
            a = psA[kv]; an = 'psA%d' % kv
            for fc in range(2):
                for l in range(32):
                    mm(a[:, fc:fc + 1], W1[kv * 64:(kv + 1) * 64, l, fc * 128:(fc + 1) * 128], posT[kv * 64:(kv + 1) * 64, l:l + 1],
                       l == 0, l == 31, ['W1', 'posT'], [an])
        for kv in range(2):
            cp('dve', b1[:, kv, :], psA[kv][:, 0:2], ['psA%d' % kv], ['b1'])

    def rmsnorm_T(src, srcn, gi, dstT, dstn):
        actf(junk[:], src, AF.Square, [srcn], ['h16'])
        add('dve', lambda e: e.reduce_sum(out=st4[:, 0:1], in_=junk[:], axis=AX.X), r=['h16'], w=['st4'])
        actf(st4[:, 1:2], st4[:, 0:1], AF.Sqrt, ['st4'], ['st4'], scale=1.0 / D, bias=1e-6)
        add('dve', lambda e: e.reciprocal(out=st4[:, 2:3], in_=st4[:, 1:2]), r=['st4'], w=['st4'])
        ts('dve', h16[:], src, st4[:, 2:3], None, OP.mult, None, [srcn, 'st4'], ['h16'])
        for k in range(8):
            tp(psT[:, k * 128:(k + 1) * 128], h16[:, k * 128:(k + 1) * 128], id16[:, 0:128], ['h16', 'id16'], ['psT'])
        tt('dve', dstT, psT[:, 0:1024].rearrange("p (k t) -> p k t", k=8),
           gcol[:, gi, :].unsqueeze(2).to_broadcast([128, 8, 128]), OP.mult, ['psT', 'gcol'], [dstn])

    def rope(src, dst, H, cs, r, w):
        C = rp[:, cs * 16:(cs + 1) * 16].unsqueeze(1).to_broadcast([128, H, 16])
        Sn = rp[:, (cs + 1) * 16:(cs + 1) * 16 + 8].unsqueeze(1).to_broadcast([128, H, 8])
        Sp = rp[:, (cs + 1) * 16 + 8:(cs + 2) * 16].unsqueeze(1).to_broadcast([128, H, 8])
        if cs == 2:
            add('act', lambda e: e.mul(out=dst[:, :, 16:64], in_=src[:, :, 16:64], mul=0.125), r=r, w=w)
        else:
            cp('act', dst[:, :, 16:64], src[:, :, 16:64], r, w)
        tt('dve', rt[:, 0:H, :], src[:, :, 0:16], C, OP.mult, r + ['rp'], ['rt'])
        tt('dve', ru[:, 0:H, 0:8], src[:, :, 8:16], Sn, OP.mult, r + ['rp'], ['ru'])
        tt('dve', ru[:, 0:H, 8:16], src[:, :, 0:8], Sp, OP.mult, r + ['rp', 'ru'], ['ru'])
        tt('dve', dst[:, :, 0:16], rt[:, 0:H, :], ru[:, 0:H, :], OP.add, ['rt', 'ru'], w)

    def proj_phase(L, rope_src, npaged_dst, nwin_dst, NR=128):
        dma('sp', rp[:], rope_src, [], ['rp'])
        rmsnorm_T(x[:], 'x', 0, hT[:], 'hT')
        groups = [(0, 512, 0), (512, 512, 512), (1024, 280, 1024), (1816, 512, 1304), (2328, 256, 1816)]
        for (c0, wd, d0) in groups:
            a = psA[nxt('A', 2)]; an = 'psA%d' % ((cnt['A'] - 1) & 1)
            for k in range(8):
                mm(a[:, 0:wd], hT[:, k, :], Wi[:, k, c0:c0 + wd], k == 0, k == 7, ['hT', 'Wi'], [an])
            if c0 == 1024:
                cp('dve', proj[:, 1024:1280], a[:, 0:256], [an], ['proj'])
                actf(gate[:], a[:, 256:280], AF.Sigmoid, [an], ['gate'])
            else:
                cp('dve', proj[:, d0:d0 + wd], a[:, 0:wd], [an], ['proj'])
        for (c0, d0) in [(C_ZA, 0), (C_ZB, 512)]:
            a = psA[nxt('A', 2)]; an = 'psA%d' % ((cnt['A'] - 1) & 1)
            for k in range(8):
                mm(a[:, 0:512], hT[:, k, :], Wi[:, k, c0:c0 + 512], k == 0, k == 7, ['hT', 'Wi'], [an])
            actf(sz[:, d0:d0 + 512], a[:, 0:512], AF.Silu, [an], ['sz'])
        kva = lambda s: proj[:, 512 + s * 128: 512 + (s + 1) * 128]
        kvb = lambda s: proj[:, 1816 + s * 128: 1816 + (s + 1) * 128]
        h2 = lambda ap: ap.rearrange("p (h d) -> p h d", d=64)
        cp('pool', okv[:, 0:256], proj[:, 512:768], ['proj'], ['okv'])
        cp('pool', okv[:, 384:512], kva(3), ['proj'], ['okv'])
        cp('pool', okv[:, 640:768], kvb(1), ['proj'], ['okv'])
        cp('pool', ow[:, 128:256], kva(5), ['proj'], ['ow'])
        rope(h2(kva(2)), h2(okv[:, 256:384]), 2, 0, ['proj'], ['okv'])
        rope(h2(kvb(0)), h2(okv[:, 512:640]), 2, 0, ['proj'], ['okv'])
        rope(h2(kva(4)), h2(ow[:, 0:128]), 2, 0, ['proj'], ['ow'])
        dma('sp', npaged_dst, okv[0:NR, :], ['okv'], [])
        if nwin_dst is not None:
            dma('sp', nwin_dst, ow[0:NR, :], ['ow'], [])
        qa = proj[:, 0:512].rearrange("p (k g d) -> p k g d", k=2, g=4)
        qb = proj[:, 1304:1816].rearrange("p (k g d) -> p k g d", k=2, g=4)
        for kvh in range(2):
            add('act', lambda e, kvh=kvh: e.mul(out=q16[:, 0, :, kvh, :], in_=qa[:, kvh, :, :], mul=0.125), r=['proj'], w=['q16'])
            rope(qa[:, kvh, :, :], q16[:, 1, :, kvh, :], 4, 2, ['proj'], ['q16'])
            rope(qb[:, kvh, :, :], q16[:, 2, :, kvh, :], 4, 2, ['proj'], ['q16'])
        cp('pool', kb16[:, 0, :], okv[:, 256:384], ['okv'], ['kb16'])
        cp('pool', kb16[:, 1, :], okv[:, 512:640], ['okv'], ['kb16'])
        cp('pool', kb16[:, 2, :], ow[:, 0:128], ['ow'], ['kb16'])
        cp('pool', kb16[:, 3:5, :].rearrange("p h (kv d) -> p h kv d", kv=2),
           okv[:, 0:256].rearrange("p (kv h d) -> p h kv d", kv=2, h=2), ['okv'], ['kb16'])
        q16f = q16[:].rearrange("p w g k d -> p (w g) (k d)")
        QTf = QT[:].rearrange("p w g t -> p (w g) t")
        for n in range(12):
            sl = n % 8
            tp(psT[:, sl * 128:(sl + 1) * 128], q16f[:, n, :], id16[:, 0:128], ['q16', 'id16'], ['psT'])
            if n == 7:
                cp('act', QTf[:, 0:8, :], psT[:, 0:1024].rearrange("p (n t) -> p n t", t=128), ['psT'], ['QT'])
            if n == 11:
                cp('act', QTf[:, 8:12, :], psT[:, 0:512].rearrange("p (n t) -> p n t", t=128), ['psT'], ['QT'])

    def kv_append(i, keyoff):
        for b in range(5):
            tp(psT[:, b * 128:(b + 1) * 128], kb16[:, b, :], id16[:, 0:128], ['kb16', 'id16'], ['psT'])
        cp('act', KT[:, :, i * 128:(i + 1) * 128], psT[:, 0:256].rearrange("p (b t) -> p b t", b=2), ['psT'], [('KT', i)])
        cp('act', KTw[:, (i % 8) * 128:(i % 8 + 1) * 128], psT[:, 256:384], ['psT'], [('KTw', i % 8)])
        cp('act', ckT[:, :, 0:16], ckT[:, :, 128:144], ['ckT'], ['ckT'])
        cp('act', ckT[:, :, 16:144], psT[:, 384:640].rearrange("p (b t) -> p b t", b=2), ['psT', 'ckT'], ['ckT'])
        add('dve', lambda e: e.reduce_sum(out=ksum[:, i:i + 1], in_=KT[:, 1, i * 128:(i + 1) * 128], axis=AX.X),
            r=[('KT', i)], w=['ksum'])
        cp('pool', VS[:, i, 0, :, 0:64], okv[:, 384:512].rearrange("p (k d) -> p k d", k=2), ['okv'], [('VS', i)])
        cp('pool', VS[:, i, 1, :, 0:64], okv[:, 640:768].rearrange("p (k d) -> p k d", k=2), ['okv', ('VS', i)], [('VS', i)])
        cp('pool', VSw[:, i % 8, :, 0:64], ow[:, 128:256].rearrange("p (k d) -> p k d", k=2), ['ow'], [('VSw', i % 8)])
        if i % 2 == 1:
            n = i // 2
            tt('dve', ksum[:, NT:NT + 1], ksum[:, i - 1:i], ksum[:, i:i + 1], OP.add, ['ksum'], ['ksum'])
            ts('dve', kmT[:, n:n + 1], ksum[:, NT:NT + 1], 1.0 / 256, None, OP.mult, None, ['ksum'], ['kmT'])

    def compress(c0, b0, nb, src, srcn, hidv_t, kcT_t, res_sfx=''):
        cnt['A'] += 2
        for kv in range(2):
            a = psA[kv]; an = 'psA%d' % kv
            for kvh in range(2):
                for fc in range(2):
                    o = (kvh * 2 + fc) * nb
                    for l in range(32):
                        mm(a[:, o:o + nb], W1[kv * 64:(kv + 1) * 64, l, fc * 128:(fc + 1) * 128],
                           src[kv * 64:(kv + 1) * 64, kvh, 16 * b0 + l:16 * b0 + l + 16 * (nb - 1) + 1:16],
                           l == 0, l == 31, ['W1', srcn], [an])
        for kv in range(2):
            av = psA[kv][:, 0:4 * nb].rearrange("p (kvh fc b) -> p kvh fc b", kvh=2, fc=2)
            an = 'psA%d' % kv
            for fc in range(2):
                if kv == 0:
                    actf(hk16[:, fc, :, 0:nb], av[:, :, fc, :], AF.Silu, [an, 'b1'], ['hk16'], bias=b1[:, 0, fc:fc + 1])
                else:
                    actf(hidv_t[:, fc, :, c0:c0 + nb], av[:, :, fc, :], AF.Silu, [an, 'b1'], ['hidv' + res_sfx], bias=b1[:, 1, fc:fc + 1])
        a2 = psA[nxt('A', 2)]; an2 = 'psA%d' % ((cnt['A'] - 1) & 1)
        n = 0
        for kvh in range(2):
            for fc in range(2):
                mm(a2[:, 0:nb], W2k[:, fc, kvh, :], hk16[:, fc, kvh, 0:nb], n == 0, n == 3, ['W2k', 'hk16'], [an2])
                n += 1
        cp('dve', kcT_t[:, c0:c0 + nb], a2[:, 0:nb], [an2], ['kcT' + res_sfx])

    def vc_refresh(ct, rows, hidv_t, vcs_t, res_sfx=''):
        a = psA[nxt('A', 2)]; an = 'psA%d' % ((cnt['A'] - 1) & 1)
        for kvh in range(2):
            for fc in range(2):
                mm(a[0:rows, kvh * 64:(kvh + 1) * 64], hidv_t[:, fc, kvh, ct * 128:ct * 128 + rows], W2v[:, fc, :],
                   fc == 0, fc == 1, ['hidv' + res_sfx, 'W2v'], [an])
        cp('dve', vcs_t[0:rows, ct, :, 0:64], a[0:rows, 0:128].rearrange("p (k d) -> p k d", k=2), [an], ['vcs' + res_sfx])

    def attn_core(NQ, kvh, qap, tiles, extra_r=()):
        W = 4 * NQ
        o = psO[nxt('O', 2)]; on = 'psO%d' % ((cnt['O'] - 1) & 1)
        nt = len(tiles)
        pend = None
        for n, (ktap, nk, vap, extras, rd) in enumerate(tiles):
            s = psS[nxt('S', 2)]; sn = 'psS%d' % ((cnt['S'] - 1) & 1)
            mm(s[0:nk, 0:W], ktap, qap, True, True, list(rd) + ['QT'], [sn])
            for j, (el, er, p0, p1, c0, c1, rd2) in enumerate(extras):
                mm(s[p0:p1, c0:c1], el, er, False, False, list(rd2), [sn], skip=True)
            pi = nxt('P', 3)
            p = pT[pi]
            actf(p[0:nk, 0:W], s[0:nk, 0:W], AF.Exp, [sn], ['pT%d' % pi])
            if pend is not None:
                (pn_, pvap, pp, pnk, prd, ppi) = pend
                mm(o[0:65, 0:W], pvap, pp[0:pnk, 0:W], pn_ == 0, False, list(prd) + ['pT%d' % ppi], [on])
            pend = (n, vap, p, nk, rd, pi)
        (pn_, pvap, pp, pnk, prd, ppi) = pend
        mm(o[0:65, 0:W], pvap, pp[0:pnk, 0:W], pn_ == 0, True, list(prd) + ['pT%d' % ppi], [on])
        cp('dve', oT[:, 0:W], o[0:65, 0:W], [on], ['okv'])
        for g in range(4):
            tp(psM[0:NQ, g * 65:(g + 1) * 65], oT[0:65, g * NQ:(g + 1) * NQ], idf[0:65, 0:65], ['okv', 'idf'], ['psM'])
        pm = psM[0:NQ, 0:260].rearrange("p (g e) -> p g e", g=4)
        ts('dve', rr[0:NQ, :], pm[:, :, 64], 1e-30, None, OP.max, None, ['psM'], ['rr'])
        add('dve', lambda e: e.reciprocal(out=rr[0:NQ, :], in_=rr[0:NQ, :]), r=['rr'], w=['rr'])
        return pm

    def accum_out(NQ, pm, kvh, gate_idx, first, gsrc=None, odst=None, gres='gate', ores='proj'):
        if gsrc is None:
            gsrc = gate[0:NQ, :]
        if odst is None:
            odst = oa
        if gate_idx is None:
            cfa = rr
            rd = ['rr']
        else:
            gv = gsrc.rearrange("p (h b) -> p h b", b=3)[:, kvh * 4:(kvh + 1) * 4, gate_idx]
            tt('dve', cf[0:NQ, :], rr[0:NQ, :], gv, OP.mult, ['rr', gres], ['cf'])
            cfa = cf
            rd = ['cf']
        dst = odst[0:NQ, kvh * 4:(kvh + 1) * 4, :]
        bc = cfa[0:NQ, :].unsqueeze(2).to_broadcast([NQ, 4, 64])
        if first:
            tt('dve', dst, pm[:, :, 0:64], bc, OP.mult, ['psM'] + rd, [ores])
        else:
            tt('dve', otmp[0:NQ], pm[:, :, 0:64], bc, OP.mult, ['psM'] + rd, ['otmp'])
            tt('dve', dst, dst, otmp[0:NQ], OP.add, [ores, 'otmp'], [ores])

    def sel_mask(NQ, pm_cmp_imp_ps, nj, halves, an):
        ip = pm_cmp_imp_ps
        ts('dve', imps[0:NQ, 0:nj], ip[:, 0, :], rr[0:NQ, 0:1], None, OP.mult, None, [an, 'rr'], ['imps'])
        for g in range(1, 4):
            stt(imps[0:NQ, 0:nj], ip[:, g, :], rr[0:NQ, g:g + 1], imps[0:NQ, 0:nj], OP.mult, OP.add, [an, 'rr', 'imps'], ['imps'])
        for (p0, p1, jt) in halves:
            ts('dve', elig[p0:p1, 0:nj], iota[p0:p1, 0:nj], float(jt), None, OP.is_lt, None, ['iota'], ['elig'])
        memset('dve', elig[0:NQ, 0:1], 0.0, ['elig'])
        stt(imps[0:NQ, 0:nj], imps[0:NQ, 0:nj], 1.0, elig[0:NQ, 0:nj], OP.add, OP.mult, ['imps', 'elig'], ['imps'])
        ts('dve', imps[0:NQ, 0:nj], imps[0:NQ, 0:nj], -1.0, None, OP.add, None, ['imps'], ['imps'])
        add('dve', lambda e: e.max(out=t8[0:NQ, :], in_=imps[0:NQ, 0:nj]), r=['imps'], w=['t8'])
        ts('dve', thr[0:NQ, :], t8[0:NQ, 5:6], 0.0, None, OP.max, None, ['t8'], ['thr'])
        ts('dve', elig[0:NQ, 0:nj], imps[0:NQ, 0:nj], thr[0:NQ, 0:1], None, OP.is_ge, None, ['imps', 'thr'], ['elig'])
        memset('dve', elig[0:NQ, 0:1], 1.0, ['elig'])
        for (p0, p1, jt) in halves:
            if jt < nj:
                memset('dve', elig[p0:p1, jt:jt + 1], 1.0, ['elig'])
        ts('dve', Mb[0:NQ, 0:nj], elig[0:NQ, 0:nj], -1.0, -NEG, OP.add, OP.mult, ['elig'], ['Mb'])

    def moba_mask(NQ, gps, nbk, bt, an):
        memset('dve', gsc[0:NQ, :, 0:nbk], -1e30, ['gsc'])
        if bt > 0:
            cp('dve', gsc[0:NQ, :, 0:bt], gps[:, :, 0:bt], [an], ['gsc'])
        for g in range(4):
            add('dve', lambda e, g=g: e.max(out=t8m[0:NQ, g, :], in_=gsc[0:NQ, g, 0:nbk]), r=['gsc'], w=['t8m'])
        ts('dve', thrm[0:NQ, :], t8m[0:NQ, :, 2], -1e29, None, OP.max, None, ['t8m'], ['thrm'])
        tt('dve', gsc[0:NQ, :, 0:nbk], gsc[0:NQ, :, 0:nbk], thrm[0:NQ, :].unsqueeze(2).to_broadcast([NQ, 4, nbk]), OP.is_ge, ['gsc', 'thrm'], ['gsc'])
        ts('dve', Mbm[0:NQ, :, 0:nbk], gsc[0:NQ, :, 0:nbk], -1.0, -NEG, OP.add, OP.mult, ['gsc'], ['Mbm'])
        memset('dve', Mbm[0:NQ, :, bt:bt + 1], 0.0, ['Mbm'])

    def out_phase(L, NQ, ple_src, final_dst, mid_dst, midres=None):
        tt('dve', mix16[:], oa[:].rearrange("p h d -> p (h d)"), sz[:], OP.mult, ['proj', 'sz'], ['h16'])
        dma('sp', ple[0:NQ, :], ple_src, [], ['ple'])
        for k in range(8):
            tp(psT[:, k * 128:(k + 1) * 128], mix16[:, k * 128:(k + 1) * 128], id16[:, 0:128], ['h16', 'id16'], ['psT'])
        cp('act', mixT[:], psT[:, 0:1024].rearrange("p (k t) -> p k t", k=8), ['psT'], ['hT'])
        for c in range(2):
            a = psA[nxt('A', 2)]; an = 'psA%d' % ((cnt['A'] - 1) & 1)
            for k in range(8):
                mm(a[:, 0:512], mixT[:, k, :], Wo[:, k, c * 512:(c + 1) * 512], k == 0, k == 7, ['hT', 'Wo'], [an])
            tt('dve', x[:, c * 512:(c + 1) * 512], x[:, c * 512:(c + 1) * 512], a[:, 0:512], OP.add, ['x', an], ['x'])
        rmsnorm_T(x[:], 'x', 1, hT[:], 'hT')
        cp('pool', ple16[:], ple[:], ['ple'], ['ple16'])
        for k in range(2):
            tp(psT[:, k * 128:(k + 1) * 128], ple16[:, k * 128:(k + 1) * 128], id16[:, 0:128], ['ple16', 'id16'], ['psT'])
        cp('act', pleT[:], psT[:, 0:256].rearrange("p (k t) -> p k t", k=2), ['psT'], ['pleT'])
        for c in range(2):
            a = psA[nxt('A', 2)]; an = 'psA%d' % ((cnt['A'] - 1) & 1)
            for k in range(8):
                mm(a[:, 0:512], hT[:, k, :], Wg[:, k, c * 512:(c + 1) * 512], k == 0, k == 7, ['hT', 'Wg'], [an])
            actf(sg[:, c * 512:(c + 1) * 512], a[:, 0:512], AF.Sigmoid, [an], ['proj'])
            a = psA[nxt('A', 2)]; an = 'psA%d' % ((cnt['A'] - 1) & 1)
            for k in range(2):
                mm(a[:, 0:512], pleT[:, k, :], Wp[:, k, c * 512:(c + 1) * 512], k == 0, k == 1, ['pleT', 'Wp'], [an])
            tt('dve', sg[:, c * 512:(c + 1) * 512], sg[:, c * 512:(c + 1) * 512], a[:, 0:512], OP.mult, ['proj', an], ['proj'])
        tt('pool', x[:], x[:], sg[:], OP.add, ['x', 'proj'], ['x'])
        if mid_dst is not None:
            dma('sp', mid_dst, x[0:NQ, :], ['x'], [midres])
        else:
            actf(junk[:], x[:], AF.Square, ['x'], ['h16'])
            add('dve', lambda e: e.reduce_sum(out=st4[:, 0:1], in_=junk[:], axis=AX.X), r=['h16'], w=['st4'])
            actf(st4[:, 1:2], st4[:, 0:1], AF.Sqrt, ['st4'], ['st4'], scale=1.0 / D, bias=1e-6)
            add('dve', lambda e: e.reciprocal(out=st4[:, 2:3], in_=st4[:, 1:2]), r=['st4'], w=['st4'])
            stt(sg[:], x[:], st4[:, 2:3], gfin_bc[:], OP.mult, OP.mult, ['x', 'st4', 'gfin_bc'], ['proj'])
            dma('sp', final_dst, sg[0:NQ, :], ['proj'], [])

    def prompt_tile(L, i):
        import os
        dbg = os.environ.get('KDBG')
        if dbg: print('tile', L, i, 'start', len(S.ops))
        src = xp if L == 0 else xmid
        dma('sp', x[:], src[i * 128:(i + 1) * 128, :], [('xmid', i)] if L == 1 else [], ['x'])
        wdst = None
        if i >= NT - WT:
            j = i - (NT - WT)
            wdst = npw[L, j * 128:(j + 1) * 128, :]
        proj_phase(L, c_rope_p[i * 128:(i + 1) * 128, :], npp[L, i * 128:(i + 1) * 128, :], wdst)
        if dbg: print(' after proj', len(S.ops))
        kv_append(i, 0)
        if dbg: print(' after kv_append', len(S.ops))
        b0 = 1 if i == 0 else 0
        compress(8 * i - 1 + b0, b0, 8 - b0, ckT, 'ckT', hidv, kcT)
        cts = sorted(set([max(8 * i - 1, 0) // 128, (8 * i + 6) // 128]))
        for ct in cts:
            vc_refresh(ct, 128, hidv, vcs)
        nct = (8 * i + 6) // 128 + 1
        if dbg: print(' after compress', len(S.ops))
        for ct in range(nct):
            base = 128 * i - 2048 * ct - 31

            def emit_sel(e, ct=ct, base=base):
                if 'fill' not in regcache:
                    regcache['fill'] = e.to_reg(NEG)
                return e.affine_select(out=cb[:, ct, :], in_=zer16[:], pattern=[[0, 4], [1, 128]], compare_op=OP.is_ge,
                                       fill=regcache['fill'], base=base, channel_multiplier=-16)
            add('pool', emit_sel, r=['zer16'], w=[('cb', ct)])
        for kvh in range(2):
            ks = slice(kvh * 64, (kvh + 1) * 64)
            tiles = []
            for ct in range(nct):
                tiles.append((kcT[ks, ct * 128:(ct + 1) * 128], 128, vcs[:, ct, kvh, :],
                              [(id16[:, 0:128], cb[:, ct, :], 0, 128, 0, 512, ['id16', ('cb', ct)])], ['kcT', 'vcs']))
            qap = QT[ks, 0, :, :].rearrange("p g t -> p (g t)")
            o = psO[nxt('O', 2)]; on = 'psO%d' % ((cnt['O'] - 1) & 1)
            for n, (ktap, nk, vap, extras, rd) in enumerate(tiles):
                s = psS[nxt('S', 2)]; sn = 'psS%d' % ((cnt['S'] - 1) & 1)
                mm(s[:, 0:512], ktap, qap, True, True, rd + ['QT'], [sn])
                (el, er, p0, p1, c0, c1, rd2) = extras[0]
                mm(s[:, 0:512], el, er, False, False, rd2, [sn], skip=True)
                actf(pcT[:, n, :], s[:, 0:512], AF.Exp, [sn], [('pcT', n)])
                mm(o[0:65, 0:512], vap, pcT[:, n, :], n == 0, n == nct - 1, rd + [('pcT', n)], [on])
            cp('dve', oT[:, :], o[0:65, 0:512], [on], ['okv'])
            for g in range(4):
                tp(psM[:, g * 65:(g + 1) * 65], oT[0:65, g * 128:(g + 1) * 128], idf[0:65, 0:65], ['okv', 'idf'], ['psM'])
            pm = psM[:, 0:260].rearrange("p (g e) -> p g e", g=4)
            ts('dve', rr[:, :], pm[:, :, 64], 1e-30, None, OP.max, None, ['psM'], ['rr'])
            add('dve', lambda e: e.reciprocal(out=rr[:, :], in_=rr[:, :]), r=['rr'], w=['rr'])
            accum_out(128, pm, kvh, 0, True)
            a = psA[nxt('A', 2)]; an = 'psA%d' % ((cnt['A'] - 1) & 1)
            for g in range(4):
                for n in range(nct):
                    mm(a[:, g * NJ:(g + 1) * NJ], pcT[:, n, g * 128:(g + 1) * 128], selmap[:, n, :], n == 0, n == nct - 1,
                       [('pcT', n), 'selmap'], [an])
            sel_mask(128, a[:, 0:4 * NJ].rearrange("p (g j) -> p g j", g=4), NJ, [(0, 64, 2 * i), (64, 128, 2 * i + 1)], an)
            if dbg: print('  after cmp+mask', kvh, len(S.ops))
            qrot = QT[ks, 1, :, :].rearrange("p g t -> p (g t)")
            tiles = []
            for kt in range(i + 1):
                ex = [(Mb[:, 2 * kt + hh:2 * kt + hh + 1].to_broadcast([128, 64]), id16[:, :], hh * 64, hh * 64 + 64, 0, 512, ['Mb', 'id16'])
                      for hh in range(2)]
                if kt == i:
                    ex.append((id16[:, 0:128], tri[:, 0, :], 0, 128, 0, 512, ['id16', 'tri']))
                tiles.append((KT[ks, 0, kt * 128:(kt + 1) * 128], 128, VS[:, kt, 0, kvh, :], ex, [('KT', kt), ('VS', kt)]))
            pm = attn_core(128, kvh, qrot, tiles)
            accum_out(128, pm, kvh, 1, False)
            if dbg: print('  after sel', kvh, len(S.ops))
            tiles = []
            for kt in range(max(0, i - 4), i + 1):
                ex = []
                if kt == i:
                    ex.append((id16[:, 0:128], tri[:, 0, :], 0, 128, 0, 512, ['id16', 'tri']))
                if kt == i - 4:
                    ex.append((id16[:, 0:128], tri[:, 1, :], 0, 128, 0, 512, ['id16', 'tri']))
                tiles.append((KTw[ks, (kt % 8) * 128:(kt % 8 + 1) * 128], 128, VSw[:, kt % 8, kvh, :], ex,
                              [('KTw', kt % 8), ('VSw', kt % 8)]))
            pm = attn_core(128, kvh, qrot, tiles)
            accum_out(128, pm, kvh, 2, False)
            if dbg: print('  after win', kvh, len(S.ops))
            bt = i // 2
            a = psA[nxt('A', 2)]; an = 'psA%d' % ((cnt['A'] - 1) & 1)
            for g in range(4):
                mm(a[:, g * NBP:(g + 1) * NBP], QT[ks, 2, g, :], kmT[ks, 0:NBP], True, True, ['QT', 'kmT'], [an])
            moba_mask(128, a[:, 0:4 * NBP].rearrange("p (g n) -> p g n", g=4), NBP, bt, an)
            qm = QT[ks, 2, :, :].rearrange("p g t -> p (g t)")
            tiles = []
            for kt in range(i + 1):
                ex = []
                for g in range(4):
                    ex.append((Mbm[:, g, kt // 2:kt // 2 + 1].to_broadcast([128, 128]), id16[:, 0:128], 0, 128, g * 128, (g + 1) * 128, ['Mbm', 'id16']))
                if kt == i:
                    ex.append((id16[:, 0:128], tri[:, 0, :], 0, 128, 0, 512, ['id16', 'tri']))
                tiles.append((KT[ks, 1, kt * 128:(kt + 1) * 128], 128, VS[:, kt, 1, kvh, :], ex, [('KT', kt), ('VS', kt)]))
            pm = attn_core(128, kvh, qm, tiles)
            dst = oa[:, 8 + kvh * 4:8 + (kvh + 1) * 4, :]
            tt('dve', dst, pm[:, :, 0:64], rr[:, :].unsqueeze(2).to_broadcast([128, 4, 64]), OP.mult, ['psM', 'rr'], ['proj'])
        if dbg: print(' after moba', len(S.ops))
        if L == 0:
            out_phase(L, 128, pp[L, i * 128:(i + 1) * 128, :], None, xmid[i * 128:(i + 1) * 128, :], ('xmid', i))
        else:
            out_phase(L, 128, pp[L, i * 128:(i + 1) * 128, :], y_p[i * 128:(i + 1) * 128, :], None)


    def gather(out_ap, chunk, pgidx, wname):
        add('pool', lambda e: e.indirect_dma_start(out=out_ap, out_offset=None, in_=cache3[:, :],
                                                   in_offset=bass.IndirectOffsetOnAxis(ap=idxv[:, chunk, pgidx:pgidx + 1], axis=0)),
            r=['idxv'], w=[wname], dma=True)

    def sample_setup():
        dma('pool', ptb[:], pt.partition_broadcast(128), [], ['ptb'])
        add('pool', lambda e: e.iota(iop[:], pattern=[[0, 1]], base=0, channel_multiplier=3), w=['iop'])
        cp('dve', ptf[:], ptb[:], ['ptb'], ['ptf'])
        cp('dve', iof[:], iop[:], ['iop'], ['iof'])
        ts('dve', ptf[:], ptf[:], 384.0, iof[:, 0:1], OP.mult, OP.add, ['ptf', 'iof'], ['ptf'])
        dma('pool', selmap_s[:], c_selmap_s.rearrange("(t p) j -> p t j", p=128), [], ['selmap_s'])
        dma('sp', smask[:], c_smask[:, :, :], [], ['smask'])
        dma('sp', wmask[:], c_wmask[:, :], [], ['wmask'])
        dma('sp', id8[:], c_id8[:, :], [], ['id8'])
        memset('pool', vnew[:], 1.0, ['vnew'])
        memset('pool', Mbs[:], 0.0, ['Mbs'])
        memset('pool', Mbms[:], 0.0, ['Mbms'])

    def sample_attn(s_, wq, tiles, keep=None):
        nt = len(tiles)
        pend = None
        for n, prep in enumerate(tiles):
            ktf, nk, vf, exf, rd = prep()
            cur = []
            for kvh in range(2):
                ks = slice(kvh * 64, (kvh + 1) * 64)
                sps = psS[kvh]; sn = 'psS%d' % kvh
                qap = QTs[ks, wq, s_, :, :].rearrange("p g q -> p (g q)")
                mm(sps[0:nk, 0:32], ktf(kvh), qap, True, True, list(rd) + ['QTs'], [sn])
                for (el, er, p0, p1, c0, c1, rd2) in exf(kvh):
                    mm(sps[p0:p1, c0:c1], el, er, False, False, list(rd2), [sn], skip=True)
                if keep is not None:
                    p = keep[:, n, kvh, :]; pn = ('pcTs', n, kvh)
                else:
                    pi = nxt('P', 4); p = pT[pi]; pn = 'pT%d' % pi
                actf(p[0:nk, 0:32], sps[0:nk, 0:32], AF.Exp, [sn], [pn])
                cur.append((kvh, vf(kvh), p, nk, list(rd), pn))
            if pend is not None:
                for (kvh, vap, p, pnk, prd, pn) in pend[1]:
                    mm(psO[kvh][0:65, 0:32], vap, p[0:pnk, 0:32], pend[0] == 0, False, prd + [pn], ['psO%d' % kvh])
            pend = (n, cur)
        for (kvh, vap, p, pnk, prd, pn) in pend[1]:
            mm(psO[kvh][0:65, 0:32], vap, p[0:pnk, 0:32], pend[0] == 0, True, prd + [pn], ['psO%d' % kvh])

    def sample_fin(kvh):
        on = 'psO%d' % kvh
        cp('dve', oT[:, 0:32], psO[kvh][0:65, 0:32], [on], ['okv'])
        for g in range(4):
            tp(psM[0:8, g * 65:(g + 1) * 65], oT[0:65, g * 8:(g + 1) * 8], idf[0:65, 0:65], ['okv', 'idf'], ['psM'])
        pm = psM[0:8, 0:260].rearrange("p (g e) -> p g e", g=4)
        ts('dve', rr[0:8, :], pm[:, :, 64], 1e-30, None, OP.max, None, ['psM'], ['rr'])
        add('dve', lambda e: e.reciprocal(out=rr[0:8, :], in_=rr[0:8, :]), r=['rr'], w=['rr'])
        return pm

    def sample_seq(L, s_):
        import os
        dbg = os.environ.get('KDBG')
        if dbg: print('sample_seq', L, s_, len(S.ops))
        pbase = s_ * NPG
        memset('pool', kmTs[:], 0.0, ['kmTs'])
        memset('pool', hidvs[:], 0.0, ['hidv_s'])
        memset('pool', kcTs[:], 0.0, ['kcT_s'])
        for gi in range(NG):
            if gi > 0:
                cp('act', ckTs[:, :, 0:16], ckTs[:, :, GP * 128:GP * 128 + 16], ['ckTs'], ['ckTs'])
            else:
                memset('pool', ckTs[:, :, 0:16], 0.0, ['ckTs'])
            for j in range(GP):
                pgi = gi * GP + j
                n = nxt('G', 2)
                gather(pg1[n][:, 0, :], 0, pbase + pgi, 'pg1a_%d' % n)
                gather(pg1[n][:, 1, :], 2, pbase + pgi, 'pg1b_%d' % n)
                cp('pool', c16[n][:, 0:2, :].rearrange("p h (kv d) -> p h kv d", kv=2),
                   pg1[n][:, 0, :].rearrange("p (kv h d) -> p h kv d", kv=2, h=2), ['pg1a_%d' % n], ['c16_%d' % n])
                cp('pool', c16[n][:, 2, :], pg1[n][:, 1, 0:128], ['pg1b_%d' % n, 'c16_%d' % n], ['c16_%d' % n])
                for b in range(3):
                    tp(psT[:, b * 128:(b + 1) * 128], c16[n][:, b, :], id16[:, 0:128], ['c16_%d' % n, 'id16'], ['psT'])
                cp('act', ckTs[:, :, 16 + j * 128:16 + (j + 1) * 128], psT[:, 0:256].rearrange("p (b t) -> p b t", b=2),
                   ['psT', 'ckTs'], ['ckTs'])
                cp('act', kp16[n][:], psT[:, 256:384], ['psT'], ['kp16_%d' % n])
                add('dve', lambda e, pgi=pgi, n=n: e.reduce_sum(out=ksums[:, pgi:pgi + 1], in_=kp16[n][:], axis=AX.X),
                    r=['kp16_%d' % n], w=['ksums'])
            b0 = 1 if gi == 0 else 0
            compress(GP * 8 * gi - 1 + b0, b0, GP * 8 - b0, ckTs, 'ckTs', hidvs, kcTs, '_s')
        if dbg: print(' after pass1', len(S.ops))
        rows_of = lambda ct: min(128, NCS - ct * 128)
        for ct in range(NCTS):
            vc_refresh(ct, rows_of(ct), hidvs, vcss, '_s')
        if BTS > 0:
            kv2 = ksums[:, 0:2 * BTS].rearrange("p (n two) -> p n two", two=2)
            tt('dve', imps[:, 0:BTS], kv2[:, :, 0], kv2[:, :, 1], OP.add, ['ksums'], ['imps'])
            ts('dve', kmTs[:, 0:BTS], imps[:, 0:BTS], 1.0 / 256, None, OP.mult, None, ['imps'], ['kmTs'])
        if dbg: print(' after vc/kmean', len(S.ops))
        gsr = gs[:, s_, :]
        tiles = []
        for ct in range(NCTS):
            def prep(ct=ct):
                rws = rows_of(ct)
                return (lambda kvh: kcTs[kvh * 64:(kvh + 1) * 64, ct * 128:ct * 128 + rws], rws,
                        lambda kvh: vcss[0:rws, ct, kvh, :], lambda kvh: [], ['kcT_s', 'vcs_s'])
            tiles.append(prep)
        sample_attn(s_, 0, tiles, keep=pcTs)
        for kvh in range(2):
            pm = sample_fin(kvh)
            accum_out(8, pm, kvh, 0, True, gsrc=gsr, odst=oas, gres='gs', ores='oas')
            a = psA[nxt('A', 2)]; an = 'psA%d' % ((cnt['A'] - 1) & 1)
            for g in range(4):
                for n in range(NCTS):
                    rws = rows_of(n)
                    mm(a[0:8, g * NJS:(g + 1) * NJS], pcTs[0:rws, n, kvh, g * 8:(g + 1) * 8], selmap_s[0:rws, n, :],
                       n == 0, n == NCTS - 1, [('pcTs', n, kvh), 'selmap_s'], [an])
            sel_mask(8, a[0:8, 0:4 * NJS].rearrange("p (g j) -> p g j", g=4), NJS, [(0, 8, JTS)], an)
            cp('dve', Mbs[0:8, kvh, 0:NJS], Mb[0:8, 0:NJS], ['Mb'], ['Mbs'])
            a = psA[nxt('A', 2)]; an = 'psA%d' % ((cnt['A'] - 1) & 1)
            for g in range(4):
                mm(a[0:8, g * NBS:(g + 1) * NBS], QTs[kvh * 64:(kvh + 1) * 64, 2, s_, g, :], kmTs[kvh * 64:(kvh + 1) * 64, 0:NBS],
                   True, True, ['QTs', 'kmTs'], [an])
            moba_mask(8, a[0:8, 0:4 * NBS].rearrange("p (g n) -> p g n", g=4), NBS, BTS, an)
            cp('dve', Mbms[0:8, kvh, :, 0:BTS + 1], Mbm[0:8, :, 0:BTS + 1], ['Mbm'], ['Mbms'])

        if dbg: print(' after cmp+masks', len(S.ops))
        def new_tile(bi):
            def prep():
                return (lambda kvh: nkT[kvh * 64:(kvh + 1) * 64, bi, 0:32], 32, lambda kvh: vnew[0:32, bi, kvh, :],
                        lambda kvh: [(id16[:, 0:32], smask[:, s_, :], 0, 32, 0, 32, ['id16', 'smask'])], ['nkT', 'vnew'])
            return prep

        def kv_tile(load, bi, exf):
            def prep():
                n = nxt('G', 2)
                kin, vin, rd0 = load(n)
                cp('pool', kp16[n][:], kin, rd0, ['kp16_%d' % n])
                cp('pool', vp[n][:, :, 0:64], vin.rearrange("p (k d) -> p k d", k=2), rd0, ['vp%d' % n])
                tp(psT[:, 0:128], kp16[n][:], id16[:, 0:128], ['kp16_%d' % n, 'id16'], ['psT'])
                cp('act', ktp[n][:], psT[:, 0:128], ['psT'], ['ktp%d' % n])
                return (lambda kvh: ktp[n][kvh * 64:(kvh + 1) * 64, :], 128, lambda kvh: vp[n][:, kvh, :], exf,
                        ['ktp%d' % n, 'vp%d' % n])
            return prep

        def page_load(pgi, chunk):
            def load(n):
                gather(pg2[n][:, :], chunk, pbase + pgi, 'pg2_%d' % n)
                return pg2[n][:, 0:128], pg2[n][:, 128:256], ['pg2_%d' % n]
            return load

        tiles = []
        for pgi in range(NPG):
            exf = lambda kvh, pgi=pgi: [(Mbs[:, kvh, 2 * pgi + hh:2 * pgi + hh + 1].to_broadcast([128, 64]), id8[:, :],
                                         hh * 64, hh * 64 + 64, 0, 32, ['Mbs', 'id8']) for hh in range(2)]
            tiles.append(kv_tile(page_load(pgi, 1), 0, exf))
        tiles.append(new_tile(0))
        sample_attn(s_, 1, tiles)
        for kvh in range(2):
            pm = sample_fin(kvh)
            accum_out(8, pm, kvh, 1, False, gsrc=gsr, odst=oas, gres='gs', ores='oas')
        if dbg: print(' after sel', len(S.ops))
        dma('sp', wkb[:], wk[L, s_].rearrange("(t p) c -> p t c", p=128), [], ['wkb'])
        tiles = []
        for wt in range(4):
            def wload(n, wt=wt):
                return wkb[:, wt, 0:128], wkb[:, wt, 128:256], ['wkb']
            exf = (lambda kvh: [(id16[:, 0:128], wmask[:, :], 0, 128, 0, 32, ['id16', 'wmask'])]) if wt == 0 else (lambda kvh: [])
            tiles.append(kv_tile(wload, 2, exf))
        tiles.append(new_tile(2))
        sample_attn(s_, 1, tiles)
        for kvh in range(2):
            pm = sample_fin(kvh)
            accum_out(8, pm, kvh, 2, False, gsrc=gsr, odst=oas, gres='gs', ores='oas')
        if dbg: print(' after win', len(S.ops))
        tiles = []
        for pgi in range(NPG):
            exf = lambda kvh, pgi=pgi: [(Mbms[:, kvh, g, pgi // 2:pgi // 2 + 1].to_broadcast([128, 128]), id8[:, 0:8],
                                         0, 128, g * 8, g * 8 + 8, ['Mbms', 'id8']) for g in range(4)]
            tiles.append(kv_tile(page_load(pgi, 2), 1, exf))
        tiles.append(new_tile(1))
        sample_attn(s_, 2, tiles)
        for kvh in range(2):
            pm = sample_fin(kvh)
            dst = oas[0:8, 8 + kvh * 4:8 + (kvh + 1) * 4, :]
            tt('dve', dst, pm[:, :, 0:64], rr[0:8, :].unsqueeze(2).to_broadcast([8, 4, 64]), OP.mult, ['psM', 'rr'], ['oas'])
        dma('sp', oa[8 * s_:8 * s_ + 8, :, :], oas[:, :, :], ['oas'], ['proj'])

    def sample_phase(L):
        add('pool', lambda e: e.memset(fence_t[:], 0.0), r=P_RES, w=S_RES + ['fence_t'])
        sample_setup()
        for n in range(2):
            memset('pool', vp[n][:], 1.0, ['vp%d' % n])
        memset('pool', vcss[:], 1.0, ['vcs_s'])
        for c in range(3):
            ts('dve', idxv[:, c, :], ptf[:], float(L * NPOOL * 384 + c), None, OP.add, None, ['ptf'], ['idxv'])
        memset('pool', x[:], 0.0, ['x'])
        if L == 0:
            dma('sp', x[0:SS * 8, :], xs[:, :], [], ['x'])
        else:
            dma('sp', x[0:SS * 8, :], xsmid[:, :], ['xsmid'], ['x'])
        import os
        if os.environ.get('KDBG'): print('sample_phase', L, len(S.ops))
        proj_phase(L, c_rope_s[:, :], nsp[L, :, :], nsw[L, :, :], NR=SS * 8)
        if os.environ.get('KDBG'): print(' after sample proj', len(S.ops))
        for b in range(3):
            tp(psT[:, b * 128:(b + 1) * 128], kb16[:, b, :], id16[:, 0:128], ['kb16', 'id16'], ['psT'])
        cp('act', nkT[:], psT[:, 0:384].rearrange("p (b t) -> p b t", b=3), ['psT'], ['nkT'])
        cp('pool', vnew[:, 0, :, 0:64], okv[:, 384:512].rearrange("p (k d) -> p k d", k=2), ['okv'], ['vnew'])
        cp('pool', vnew[:, 1, :, 0:64], okv[:, 640:768].rearrange("p (k d) -> p k d", k=2), ['okv', 'vnew'], ['vnew'])
        cp('pool', vnew[:, 2, :, 0:64], ow[:, 128:256].rearrange("p (k d) -> p k d", k=2), ['ow', 'vnew'], ['vnew'])
        for wq in range(3):
            cp('dve', QTs[:, wq, :, :, :], QT[:, wq, :, 0:SS * 8].rearrange("p g (s q) -> p s g q", s=SS), ['QT'], ['QTs'])
        for s_ in range(SS):
            dma('sp', gs[:, s_, :], gate[8 * s_:8 * s_ + 8, :], ['gate'], ['gs'])
        for s_ in range(SS):
            sample_seq(L, s_)
        if L == 0:
            out_phase(L, SS * 8, ps_[L, :, :], None, xsmid[:, :], 'xsmid')
        else:
            out_phase(L, SS * 8, ps_[L, :, :], y_s[:, :], None)

    for L in range(2):
        load_weights(L)
        add('pool', lambda e: e.memset(fence_t[:], 0.0), r=S_RES, w=P_RES + ['fence_t'])
        memset('pool', VS[:], 1.0, [('VS', t) for t in range(NT)])
        memset('pool', VSw[:], 1.0, [('VSw', t) for t in range(8)])
        memset('pool', vcs[:], 1.0, ['vcs'])
        memset('pool', hidv[:], 0.0, ['hidv'])
        memset('pool', kcT[:], 0.0, ['kcT'])
        memset('pool', ckT[:], 0.0, ['ckT'])
        memset('pool', kmT[:], 0.0, ['kmT'])
        for i in range(NT):
            prompt_tile(L, i)
        if do_sample:
            sample_phase(L)
    S.finalize(es)
    es.close()
    return nc


def _consts(T, PAST):
    half = 8
    inv = (np.float32(500000.0) ** (-np.arange(half, dtype=np.float32) / np.float32(half))).astype(np.float32)

    def table(pos):
        ang = pos.astype(np.float32)[:, None] * inv[None, :]
        c, s = np.cos(ang).astype(np.float32), np.sin(ang).astype(np.float32)
        Ck = np.concatenate([c, c], 1); Sk = np.concatenate([-s, s], 1)
        return np.concatenate([Ck, Sk, Ck * np.float32(0.125), Sk * np.float32(0.125)], 1).astype(np.float32)

    rope_p = table(np.arange(T))
    rs = table(PAST + np.arange(8))
    rope_s = np.zeros((128, 64), np.float32)
    rope_s[:32] = np.tile(rs, (4, 1))
    bf = ml_dtypes.bfloat16
    id16 = np.tile(np.eye(128, dtype=np.float32), (1, 4)).astype(bf)
    idf = np.eye(128, dtype=np.float32)
    k = np.arange(128)[:, None]; q = np.arange(128)[None, :]
    t0 = np.where(k <= q, 0.0, NEG).astype(np.float32)
    t1 = np.where(k >= q, 0.0, NEG).astype(np.float32)
    tri = np.stack([np.tile(t0, (1, 4)), np.tile(t1, (1, 4))], 1).astype(bf)
    ncmp = T // 16 - 1
    nsel = T // 64
    c0 = np.arange(ncmp)[:, None] * 16; j0 = np.arange(nsel)[None, :] * 64
    ov = np.clip(np.minimum(c0 + 32, j0 + 64) - np.maximum(c0, j0), 0, None) / 16.0
    NCP = (ncmp + 127) // 128 * 128
    selmap = np.zeros((NCP, nsel), np.float32); selmap[:ncmp] = ov
    iota = np.tile(np.arange(256, dtype=np.float32)[None, :], (128, 1))
    ncs = PAST // 16 - 1
    ncts = (ncs + 127) // 128
    njs = max(PAST // 64, 8)
    c0 = np.arange(ncs)[:, None] * 16; j0 = np.arange(njs)[None, :] * 64
    ovs = np.clip(np.minimum(c0 + 32, j0 + 64) - np.maximum(c0, j0), 0, None) / 16.0
    selmap_s = np.zeros((ncts * 128, njs), np.float32); selmap_s[:ncs] = ovs
    kk = np.arange(32)[:, None, None]; ss = np.arange(4)[None, :, None]; qq = (np.arange(32) % 8)[None, None, :]
    smask = np.zeros((128, 4, 32), np.float32)
    smask[:32] = np.where((kk // 8 == ss) & (kk % 8 <= qq), 0.0, NEG)
    smask = smask.astype(bf)
    rr_ = np.arange(128)[:, None]; q2 = (np.arange(32) % 8)[None, :]
    wmask = np.where(rr_ < q2, NEG, 0.0).astype(np.float32).astype(bf)
    id8 = np.zeros((128, 32), np.float32)
    id8[:8] = np.tile(np.eye(8, dtype=np.float32), (1, 4))
    id8 = id8.astype(bf)
    return dict(c_rope_p=rope_p, c_rope_s=rope_s, c_id16=id16, c_idf=idf, c_tri=tri, c_selmap=selmap, c_iota=iota,
                c_selmap_s=selmap_s, c_smask=smask, c_wmask=wmask, c_id8=id8)


_CACHE = {}


def run(inputs, T, NPG, NPOOL, n_cores=8, do_sample=True):
    key = (T, NPG, NPOOL, do_sample)
    if key not in _CACHE:
        _CACHE[key] = build(T, NPG, NPOOL, do_sample=do_sample)
    nc = _CACHE[key]
    PAST = NPG * 128
    cst = _consts(T, PAST)
    f = lambda a: np.ascontiguousarray(np.asarray(a))
    B = inputs['x_prompt'].shape[0]
    SS = 4
    WT = min(4, T // 128)
    in_maps = []
    for c in range(n_cores):
        b = c % B
        s0 = c * SS
        m = dict(cst)
        m['xp'] = f(inputs['x_prompt'][b]); m['pp'] = f(inputs['p_prompt'][:, b])
        m['xs'] = f(inputs['x_sample'][s0:s0 + SS].reshape(SS * 8, D))
        m['ps'] = f(inputs['p_sample'][:, s0:s0 + SS].reshape(2, SS * 8, 256))
        m['cache'] = f(inputs['cache_paged_kv'].reshape(2, NPOOL, 128, 768))
        m['wk'] = f(inputs['cache_win_kv'][:, s0:s0 + SS].reshape(2, SS, 512, 256))
        m['pt'] = f(inputs['page_table'][s0:s0 + SS].reshape(1, SS * NPG).astype(np.int32))
        for k in ['g_mix', 'w_in', 'w_out', 'cmp_pos_k', 'cmp_pos_v', 'cmp_w1_k', 'cmp_w1_v', 'cmp_w2_k', 'cmp_w2_v',
                  'g_ple', 'w_ple_gate', 'w_ple_proj', 'g_final']:
            m[k] = f(inputs[k])
        in_maps.append(m)
    res = run_bass_kernel_spmd(nc, in_maps, core_ids=list(range(n_cores)))
    R = res.results
    y_p = np.stack([R[b]['y_p'] for b in range(B)])
    npp = np.stack([R[b]['npp'] for b in range(B)], 1).reshape(2, B, T, 6, 2, 64)
    npw = np.stack([R[b]['npw'] for b in range(B)], 1).reshape(2, B, WT * 128, 2, 2, 64)
    y_s = np.concatenate([R[c]['y_s'] for c in range(n_cores)]).reshape(n_cores * SS, 8, D)
    nsp = np.concatenate([R[c]['nsp'].reshape(2, SS, 8, 768) for c in range(n_cores)], 1).reshape(2, n_cores * SS, 8, 6, 2, 64)
    nsw = np.concatenate([R[c]['nsw'].reshape(2, SS, 8, 256) for c in range(n_cores)], 1).reshape(2, n_cores * SS, 8, 2, 2, 64)
    return (y_p.astype(np.float32), y_s.astype(np.float32), npp.astype(np.float32), npw.astype(np.float32),
            nsp.astype(np.float32), nsw.astype(np.float32))


def kernel(**inputs):
    inputs = {k: np.asarray(v) for k, v in inputs.items()}
    T = inputs['x_prompt'].shape[1]
    NPG = inputs['page_table'].shape[1]
    NPOOL = inputs['cache_paged_kv'].shape[1]
    return run(inputs, T, NPG, NPOOL)
```
